# Optimizing a Trainium2 kernel written in Bass

```python
import math
import jax
import jax.numpy as jnp
from jax import lax
import numpy as np

D_MODEL = 1024
BATCH = 8
SEQ = 2048
DEPTH = 2

CTX_LEN = 256
GRID_W = 64
N_MOD = 9
D_FF = 2816
RMS_EPS = 1e-6
ROPE_BASE = 10000.0

GLA_HEADS = 4
GLA_DK = 32
GLA_DV = 64
GLA_GATE_RANK = 16
GLA_TAU = 16.0
GLA_CHUNK = 64
SWA_HEADS = 8
SWA_KV_HEADS = 2
SWA_HD = 64
SWA_WINDOW = 128
SWA_BLOCK = 128
DIFF_HEADS = 4
DIFF_QK = 32
DIFF_V = 64
DIFF_QBLOCK = 128

GLA_QK_W = GLA_HEADS * GLA_DK
GLA_WIDTH = GLA_HEADS * GLA_DV
SWA_WIDTH = SWA_HEADS * SWA_HD
SWA_KV_W = SWA_KV_HEADS * SWA_HD
DIFF_QK_W = DIFF_HEADS * 2 * DIFF_QK
DIFF_WIDTH = DIFF_HEADS * DIFF_V
MIX_WIDTH = GLA_WIDTH + SWA_WIDTH + DIFF_WIDTH
IN_SIZES = (GLA_QK_W, GLA_QK_W, GLA_WIDTH, GLA_GATE_RANK, GLA_GATE_RANK, GLA_WIDTH,
            SWA_WIDTH, SWA_KV_W, SWA_KV_W,
            DIFF_QK_W, DIFF_QK_W, DIFF_WIDTH)
IN_COLS = 2 * GLA_QK_W + 2 * GLA_WIDTH + 2 * GLA_GATE_RANK + SWA_WIDTH + 2 * SWA_KV_W + 2 * DIFF_QK_W + DIFF_WIDTH

kernel_name = 'hybrid_gla_swa_diffattn_macaron_dit'


def rms_norm(x, g):
    xf = x.astype(jnp.float32)
    y = xf * lax.rsqrt(jnp.mean(xf * xf, axis=-1, keepdims=True) + RMS_EPS)
    return (y * g.astype(jnp.float32)).astype(x.dtype)


def modulate(h, shift, scale):
    return h * (1 + scale) + shift


def swiglu(h, w_in, w_out):
    gate, up = jnp.split(h @ w_in, 2, axis=-1)
    return (jax.nn.silu(gate) * up) @ w_out


def to_heads(t, n_heads):
    b, s, _ = t.shape
    return t.reshape(b, s, n_heads, -1).transpose(0, 2, 1, 3)


def from_heads(t):
    b, h, s, d = t.shape
    return t.transpose(0, 2, 1, 3).reshape(b, s, h * d)


def head_rms(o, g, dtype):
    of = o.astype(jnp.float32)
    of = of * lax.rsqrt(jnp.mean(of * of, axis=-1, keepdims=True) + RMS_EPS)
    return (from_heads(of) * g.astype(jnp.float32)).astype(dtype)


def split_cols(p, sizes):
    parts, off = [], 0
    for sz in sizes:
        parts.append(p[..., off:off + sz])
        off += sz
    return parts


def axial_rope_tables(rows, head_dim):
    half = head_dim // 2
    row = jnp.repeat(jnp.arange(rows, dtype=jnp.float32), GRID_W)
    col = jnp.tile(jnp.arange(GRID_W, dtype=jnp.float32), rows)
    inv_freq = 1.0 / (ROPE_BASE ** (jnp.arange(0, half, 2, dtype=jnp.float32) / half))
    def axis_angles(pos):
        a = pos[:, None] * inv_freq[None, :]
        return jnp.concatenate([a, a], axis=-1)
    ang = jnp.concatenate([axis_angles(row), axis_angles(col)], axis=-1)
    return jnp.cos(ang), jnp.sin(ang)


def apply_axial_rope(x, cos, sin):
    half = x.shape[-1] // 2
    quarter = half // 2
    def rot(t):
        return jnp.concatenate([-t[..., quarter:], t[..., :quarter]], axis=-1)
    xr = jnp.concatenate([rot(x[..., :half]), rot(x[..., half:])], axis=-1)
    return (x * cos + xr * sin).astype(x.dtype)


def gla_log_gate(z_low, w_up, b_up):
    z = z_low.astype(jnp.float32) @ w_up.astype(jnp.float32) + b_up.astype(jnp.float32)
    return to_heads(jax.nn.log_sigmoid(z) / GLA_TAU, GLA_HEADS)


def gla_scan(q, k, v, log_g, s0):
    b, h, t, _ = k.shape
    n = t // GLA_CHUNK
    with_out = q is not None
    def chunks(a):
        return a.reshape(b, h, n, GLA_CHUNK, a.shape[-1]).transpose(2, 0, 1, 3, 4)
    causal = jnp.tril(jnp.ones((GLA_CHUNK, GLA_CHUNK), dtype=bool))[:, :, None]
    def step(state, inp):
        kc, vc, gc = inp[0], inp[1], inp[2]
        cum = jnp.cumsum(gc, axis=2)
        last = cum[:, :, -1:, :]
        new_state = (jnp.exp(last[:, :, 0, :, None]) * state
                     + jnp.einsum('bhjd,bhjv->bhdv', kc * jnp.exp(last - cum), vc))
        if not with_out:
            return new_state, None
        qc = inp[3]
        o_inter = jnp.einsum('bhid,bhdv->bhiv', qc * jnp.exp(cum), state)
        decay = jnp.exp(jnp.where(causal, cum[:, :, :, None, :] - cum[:, :, None, :, :], -jnp.inf))
        att = jnp.einsum('bhid,bhjd,bhijd->bhij', qc, kc, decay)
        return new_state, o_inter + jnp.einsum('bhij,bhjv->bhiv', att, vc)
    xs = (chunks(k), chunks(v), chunks(log_g)) + ((chunks(q),) if with_out else ())
    state, o = lax.scan(step, s0, xs)
    if not with_out:
        return None, state
    return o.transpose(1, 2, 0, 3, 4).reshape(b, h, t, GLA_DV), state


def gla_mixer(xp, cp, w_gate, b_gate, g_norm, ctx_out):
    dtype = xp[2].dtype
    def prep(parts):
        q, k, v, gf, gb, og = parts
        q = to_heads(q, GLA_HEADS).astype(jnp.float32) * (GLA_DK ** -0.5)
        k = to_heads(k, GLA_HEADS).astype(jnp.float32)
        v = to_heads(v, GLA_HEADS).astype(jnp.float32)
        return q, k, v, gla_log_gate(gf, w_gate[0], b_gate[0]), gla_log_gate(gb, w_gate[1], b_gate[1]), og
    def flip(a):
        return jnp.flip(a, axis=2)
    def finish(o, og):
        return (head_rms(o, g_norm, jnp.float32) * jax.nn.silu(og.astype(jnp.float32))).astype(dtype)
    qc, kc, vc, lfc, lbc, ogc = prep(cp)
    s0 = jnp.zeros((kc.shape[0], GLA_HEADS, GLA_DK, GLA_DV), jnp.float32)
    oc_f, st_f = gla_scan(qc if ctx_out else None, kc, vc, lfc, s0)
    oc_b, st_b = gla_scan(flip(qc) if ctx_out else None, flip(kc), flip(vc), flip(lbc), s0)
    qx, kx, vx, lfx, lbx, ogx = prep(xp)
    ox_f, _ = gla_scan(qx, kx, vx, lfx, st_f)
    ox_b, _ = gla_scan(flip(qx), flip(kx), flip(vx), flip(lbx), st_b)
    y_x = finish(ox_f + flip(ox_b), ogx)
    y_c = finish(oc_f + flip(oc_b), ogc) if ctx_out else None
    return y_x, y_c


def swa_mixer(xp, cp, sink, rope, ctx_out):
    cos, sin = rope
    xq, xk, xv = xp
    cq, ck, cv = cp
    dtype = xv.dtype
    b, s, _ = xq.shape
    n_ctx = ck.shape[1]
    grp = SWA_HEADS // SWA_KV_HEADS
    nb = s // SWA_BLOCK
    band_w = 3 * SWA_BLOCK
    scale = SWA_HD ** -0.5
    sink_f = sink.astype(jnp.float32)
    q = apply_axial_rope(to_heads(xq, SWA_HEADS), cos, sin).reshape(b, SWA_KV_HEADS, grp, nb, SWA_BLOCK, SWA_HD)
    k = apply_axial_rope(to_heads(xk, SWA_KV_HEADS), cos, sin)
    v = to_heads(xv, SWA_KV_HEADS)
    kc = to_heads(ck, SWA_KV_HEADS)
    vc = to_heads(cv, SWA_KV_HEADS)
    def band(a):
        ap = jnp.pad(a, ((0, 0), (0, 0), (SWA_BLOCK, SWA_BLOCK), (0, 0))).reshape(b, SWA_KV_HEADS, nb + 2, SWA_BLOCK, SWA_HD)
        return jnp.concatenate([ap[:, :, :nb], ap[:, :, 1:nb + 1], ap[:, :, 2:]], axis=3)
    kb, vb = band(k), band(v)
    blk = jnp.arange(nb)[:, None] * SWA_BLOCK
    qpos = blk + jnp.arange(SWA_BLOCK)[None, :]
    kpos = blk - SWA_BLOCK + jnp.arange(band_w)[None, :]
    mask = ((jnp.abs(qpos[:, :, None] - kpos[:, None, :]) <= SWA_WINDOW)
            & (kpos[:, None, :] >= 0) & (kpos[:, None, :] < s))
    s_band = jnp.where(mask, jnp.einsum('bkgnqd,bknjd->bkgnqj', q, kb).astype(jnp.float32) * scale, -jnp.inf)
    s_ctx = jnp.einsum('bkgnqd,bkld->bkgnql', q, kc).astype(jnp.float32) * scale
    s_sink = jnp.broadcast_to(sink_f.reshape(1, SWA_KV_HEADS, grp, 1, 1, 1), s_ctx.shape[:-1] + (1,))
    p = jax.nn.softmax(jnp.concatenate([s_band, s_ctx, s_sink], axis=-1), axis=-1)
    o = (jnp.einsum('bkgnqj,bknjd->bkgnqd', p[..., :band_w].astype(dtype), vb)
         + jnp.einsum('bkgnql,bkld->bkgnqd', p[..., band_w:band_w + n_ctx].astype(dtype), vc))
    y_x = from_heads(o.reshape(b, SWA_HEADS, s, SWA_HD))
    y_c = None
    if ctx_out:
        qc = to_heads(cq, SWA_HEADS).reshape(b, SWA_KV_HEADS, grp, n_ctx, SWA_HD)
        sc = jnp.einsum('bkgqd,bkld->bkgql', qc, kc).astype(jnp.float32) * scale
        sc_sink = jnp.broadcast_to(sink_f.reshape(1, SWA_KV_HEADS, grp, 1, 1), sc.shape[:-1] + (1,))
        pc = jax.nn.softmax(jnp.concatenate([sc, sc_sink], axis=-1), axis=-1)
        oc = jnp.einsum('bkgql,bkld->bkgqd', pc[..., :n_ctx].astype(dtype), vc)
        y_c = from_heads(oc.reshape(b, SWA_HEADS, n_ctx, SWA_HD))
    return y_x, y_c


def diff_mixer(xp, cp, lam_vecs, g_norm, lambda_init, rope, ctx_out):
    cos, sin = rope
    xq, xk, xv = xp
    cq, ck, cv = cp
    dtype = xv.dtype
    b, s, _ = xq.shape
    nb = s // DIFF_QBLOCK
    scale = DIFF_QK ** -0.5
    lv = lam_vecs.astype(jnp.float32)
    lam = jnp.exp(jnp.sum(lv[0] * lv[1])) - jnp.exp(jnp.sum(lv[2] * lv[3])) + lambda_init
    def pair(t, rotary):
        hh = to_heads(t, DIFF_HEADS)
        h1, h2 = hh[..., :DIFF_QK], hh[..., DIFF_QK:]
        if rotary:
            return apply_axial_rope(h1, cos, sin), apply_axial_rope(h2, cos, sin)
        return h1, h2
    def attend(q1, q2, k1, k2, v):
        p1 = jax.nn.softmax(jnp.einsum('bhqd,bhkd->bhqk', q1, k1).astype(jnp.float32) * scale, axis=-1)
        p2 = jax.nn.softmax(jnp.einsum('bhqd,bhkd->bhqk', q2, k2).astype(jnp.float32) * scale, axis=-1)
        return jnp.einsum('bhqk,bhkd->bhqd', (p1 - lam * p2).astype(dtype), v)
    q1, q2 = pair(xq, True)
    k1, k2 = pair(xk, True)
    v = to_heads(xv, DIFF_HEADS)
    k1c, k2c = pair(ck, False)
    vc = to_heads(cv, DIFF_HEADS)
    k1a = jnp.concatenate([k1, k1c], axis=2)
    k2a = jnp.concatenate([k2, k2c], axis=2)
    va = jnp.concatenate([v, vc], axis=2)
    def blocks(a):
        return a.reshape(b, DIFF_HEADS, nb, DIFF_QBLOCK, DIFF_QK).transpose(2, 0, 1, 3, 4)
    ob = lax.map(lambda qb: attend(qb[0], qb[1], k1a, k2a, va), (blocks(q1), blocks(q2)))
    o = ob.transpose(1, 2, 0, 3, 4).reshape(b, DIFF_HEADS, s, DIFF_V)
    out_scale = 1.0 - lambda_init
    y_x = head_rms(o, g_norm, dtype) * out_scale
    y_c = None
    if ctx_out:
        q1c, q2c = pair(cq, False)
        y_c = head_rms(attend(q1c, q2c, k1c, k2c, vc), g_norm, dtype) * out_scale
    return y_x, y_c


def hybrid_layer(x, ctx, mod_x, mod_c, g_ffn1, w_ffn1_in, w_ffn1_out, g_mix, w_in, w_out,
                 w_gla_gate, b_gla_gate, g_gla_norm, swa_sink, diff_lambda, g_diff_norm,
                 g_ffn2, w_ffn2_in, w_ffn2_out, lambda_init, rope_swa, rope_diff, ctx_out):
    def m(mod, i):
        return mod[:, :, i]
    x = x + 0.5 * m(mod_x, 2) * swiglu(modulate(rms_norm(x, g_ffn1), m(mod_x, 0), m(mod_x, 1)), w_ffn1_in, w_ffn1_out)
    ctx = ctx + 0.5 * m(mod_c, 2) * swiglu(modulate(rms_norm(ctx, g_ffn1), m(mod_c, 0), m(mod_c, 1)), w_ffn1_in, w_ffn1_out)
    px = split_cols(modulate(rms_norm(x, g_mix), m(mod_x, 3), m(mod_x, 4)) @ w_in, IN_SIZES)
    pc = split_cols(modulate(rms_norm(ctx, g_mix), m(mod_c, 3), m(mod_c, 4)) @ w_in, IN_SIZES)
    gla_x, gla_c = gla_mixer(px[0:6], pc[0:6], w_gla_gate, b_gla_gate, g_gla_norm, ctx_out)
    swa_x, swa_c = swa_mixer(px[6:9], pc[6:9], swa_sink, rope_swa, ctx_out)
    diff_x, diff_c = diff_mixer(px[9:12], pc[9:12], diff_lambda, g_diff_norm, lambda_init, rope_diff, ctx_out)
    x = x + m(mod_x, 5) * (jnp.concatenate([gla_x, swa_x, diff_x], axis=-1) @ w_out)
    x = x + 0.5 * m(mod_x, 8) * swiglu(modulate(rms_norm(x, g_ffn2), m(mod_x, 6), m(mod_x, 7)), w_ffn2_in, w_ffn2_out)
    if ctx_out:
        ctx = ctx + m(mod_c, 5) * (jnp.concatenate([gla_c, swa_c, diff_c], axis=-1) @ w_out)
        ctx = ctx + 0.5 * m(mod_c, 8) * swiglu(modulate(rms_norm(ctx, g_ffn2), m(mod_c, 6), m(mod_c, 7)), w_ffn2_in, w_ffn2_out)
    return x, ctx


def setup_inputs(seed: int = 0) -> dict:
    key = jax.random.key(seed)
    ks = jax.random.split(key, 24)
    L = DEPTH
    D = D_MODEL
    def nrm(k, shape, sd):
        return jax.random.normal(k, shape, jnp.float32) * sd
    def gain(k, shape):
        return 1.0 + 0.01 * jax.random.normal(k, shape, jnp.float32)
    return {
        'x': nrm(ks[0], (BATCH, SEQ, D), 1.0),
        'c': nrm(ks[1], (BATCH, D), 1.0),
        'ctx': nrm(ks[2], (BATCH, CTX_LEN, D), 1.0),
        'c_ctx': nrm(ks[3], (D,), 1.0),
        'w_mod': nrm(ks[4], (L, D, N_MOD * D), 0.3 * D ** -0.5),
        'b_mod': nrm(ks[5], (L, N_MOD * D), 0.01),
        'g_ffn1': gain(ks[6], (L, D)),
        'w_ffn1_in': nrm(ks[7], (L, D, 2 * D_FF), D ** -0.5),
        'w_ffn1_out': nrm(ks[8], (L, D_FF, D), D_FF ** -0.5),
        'g_mix': gain(ks[9], (L, D)),
        'w_in': nrm(ks[10], (L, D, IN_COLS), D ** -0.5),
        'w_out': nrm(ks[11], (L, MIX_WIDTH, D), MIX_WIDTH ** -0.5),
        'w_gla_gate': nrm(ks[12], (L, 2, GLA_GATE_RANK, GLA_QK_W), GLA_GATE_RANK ** -0.5),
        'b_gla_gate': nrm(ks[13], (L, 2, GLA_QK_W), 0.1),
        'g_gla_norm': gain(ks[14], (L, GLA_WIDTH)),
        'swa_sink': nrm(ks[15], (L, SWA_HEADS), 0.5),
        'diff_lambda': nrm(ks[16], (L, 4, DIFF_QK), 0.1),
        'g_diff_norm': gain(ks[17], (L, DIFF_WIDTH)),
        'g_ffn2': gain(ks[18], (L, D)),
        'w_ffn2_in': nrm(ks[19], (L, D, 2 * D_FF), D ** -0.5),
        'w_ffn2_out': nrm(ks[20], (L, D_FF, D), D_FF ** -0.5),
        'g_final': gain(ks[21], (D,)),
    }


def reference(x, c, ctx, c_ctx, w_mod, b_mod, g_ffn1, w_ffn1_in, w_ffn1_out, g_mix, w_in, w_out,
              w_gla_gate, b_gla_gate, g_gla_norm, swa_sink, diff_lambda, g_diff_norm,
              g_ffn2, w_ffn2_in, w_ffn2_out, g_final):
    b, s, d = x.shape
    rows = s // GRID_W
    rope_swa = axial_rope_tables(rows, SWA_HD)
    rope_diff = axial_rope_tables(rows, DIFF_QK)
    for l in range(DEPTH):
        mod_x = (jax.nn.silu(c) @ w_mod[l] + b_mod[l]).reshape(b, 1, N_MOD, d)
        mod_c = (jax.nn.silu(c_ctx) @ w_mod[l] + b_mod[l]).reshape(1, 1, N_MOD, d)
        lambda_init = 0.8 - 0.6 * math.exp(-0.3 * l)
        x, ctx = hybrid_layer(x, ctx, mod_x, mod_c, g_ffn1[l], w_ffn1_in[l], w_ffn1_out[l], g_mix[l],
                              w_in[l], w_out[l], w_gla_gate[l], b_gla_gate[l], g_gla_norm[l], swa_sink[l],
                              diff_lambda[l], g_diff_norm[l], g_ffn2[l], w_ffn2_in[l], w_ffn2_out[l],
                              lambda_init, rope_swa, rope_diff, l < DEPTH - 1)
    return rms_norm(x, g_final)
```

```python
import math
import os
import numpy as np
import concourse.bass as bass
import concourse.mybir as mybir
from concourse.bass_utils import run_bass_kernel_spmd

F32 = mybir.dt.float32
BF16 = mybir.dt.bfloat16
ALU = mybir.AluOpType
AF = mybir.ActivationFunctionType
AX = mybir.AxisListType

D = 1024
S = 2048
C = 256
T = S + C
NT = T // 128
DFF = 2816
NJ = DFF // 128
L = 2
EPS = 1e-6

ENGS = ("pe", "act", "dve", "pool", "sp")
NSLOT = 8
NFM = 27
NMIX = NFM * 128 + 1024


def _esz(dt):
    return mybir.dt.size(dt)


class _Op:
    __slots__ = ("eng", "fn", "waits", "signal", "ev", "dma")


class Sched:
    def __init__(self):
        self.ops = {e: [] for e in ENGS}
        self.clock = {e: {} for e in ENGS}
        self.snap = {}
        self.opof = {}
        self.recs = {}
        self.slot_cnt = {e: [0] * NSLOT for e in ENGS}
        self.slot_rr = {e: 0 for e in ENGS}
        self.untracked = set()
        self.GR = 2048

    def boxes(self, ap):
        name = ap.tensor.name
        if name in self.untracked:
            return []
        es = _esz(ap.dtype)
        aps = list(ap.ap)
        off = ap.offset
        if str(ap.space) not in ("SB", "PSUM"):
            ext = sum((c - 1) * abs(s) for s, c in aps)
            return [(name, 0, 1, off * es, (off + ext + 1) * es)]
        pstep, pcnt = aps[0]
        p0 = off // pstep
        f0 = off % pstep
        free = aps[1:]
        out = []

        def rec(base, dims):
            if not dims:
                out.append((name, p0, p0 + pcnt, base * es, (base + 1) * es))
                return
            inner_ext = sum((c - 1) * abs(s) for s, c in dims[1:]) + 1
            s0, c0 = dims[0]
            if len(dims) > 1 and abs(s0) >= inner_ext and 1 < c0 <= 64 and s0 > 0:
                for i in range(c0):
                    rec(base + i * s0, dims[1:])
            else:
                ext = sum((c - 1) * abs(s) for s, c in dims) + 1
                out.append((name, p0, p0 + pcnt, base * es, (base + ext) * es))

        rec(f0, free)
        return out

    def _conf(self, b, kind, deps, eng):
        name, p0, p1, f0, f1 = b
        for g in range(f0 // self.GR, (f1 - 1) // self.GR + 1):
            for r in self.recs.get((name, g), ()):
                rb = r[0]
                if rb[1] < p1 and p0 < rb[2] and rb[3] < f1 and f0 < rb[4]:
                    rk = r[1]
                    if kind == "R" and rk == "R":
                        continue
                    if kind == "X" and rk == "X" and r[3] == eng:
                        continue
                    deps.add(r[2])

    def _reg(self, b, kind, ev, eng, dma):
        name, p0, p1, f0, f1 = b
        for g in range(f0 // self.GR, (f1 - 1) // self.GR + 1):
            lst = self.recs.setdefault((name, g), [])
            g0 = max(f0, g * self.GR)
            g1 = min(f1, (g + 1) * self.GR)
            keep = []
            for r in lst:
                rb = r[0]
                r0 = max(rb[3], g * self.GR)
                r1 = min(rb[4], (g + 1) * self.GR)
                contained = rb[1] >= p0 and rb[2] <= p1 and r0 >= g0 and r1 <= g1
                if contained:
                    if kind == "W":
                        continue
                    if kind == r[1] and r[3] == eng and not dma and not r[4]:
                        continue
                keep.append(r)
            keep.append((b, kind, ev, eng, dma))
            self.recs[(name, g)] = keep

    def add(self, eng, fn, outs=(), ins=(), dma=False):
        op = _Op()
        op.eng, op.fn, op.dma, op.signal = eng, fn, dma, dma
        idx = len(self.ops[eng])
        deps = set()
        acc = []
        for ap in ins:
            for b in self.boxes(ap):
                if b[0] == "ps":
                    acc.append(((b[0], 0, 128, (b[3] // 2048) * 2048, ((b[4] - 1) // 2048 + 1) * 2048), "X"))
                else:
                    acc.append((b, "R"))
        for ap in outs:
            for b in self.boxes(ap):
                if b[0] == "ps":
                    acc.append(((b[0], 0, 128, (b[3] // 2048) * 2048, ((b[4] - 1) // 2048 + 1) * 2048), "W"))
                else:
                    acc.append((b, "W"))
        for b, kind in acc:
            self._conf(b, kind, deps, eng)
        if dma:
            s = self.slot_rr[eng]
            self.slot_rr[eng] = (s + 1) % NSLOT
            k = self.slot_cnt[eng][s]
            if k > 0:
                deps.add(((eng, s), k))
            self.slot_cnt[eng][s] = k + 1
            ev = ((eng, s), k + 1)
        else:
            ev = (eng, idx + 1)
        op.ev = ev
        clk = self.clock[eng]
        waits = []
        for key, val in sorted(deps, key=lambda d: -d[1]):
            if eng == "pe" and key == "pe":
                continue
            if clk.get(key, 0) >= val:
                continue
            waits.append((key, val))
            self.opof[(key, val)].signal = True
            for k2, v2 in self.snap[(key, val)].items():
                if clk.get(k2, 0) < v2:
                    clk[k2] = v2
            clk[key] = max(clk.get(key, 0), val)
        op.waits = waits
        self.snap[ev] = dict(clk)
        self.opof[ev] = op
        for b, kind in acc:
            self._reg(b, kind, ev, eng, dma)
        self.ops[eng].append(op)
        return op

    def emit(self, nc, block, sems, dsems):
        rank = {}
        for e in ENGS:
            n = 0
            for i, op in enumerate(self.ops[e]):
                if op.signal and not op.dma:
                    n += 1
                    rank[(e, i + 1)] = n

        def run(eng_name, eng):
            for op in self.ops[eng_name]:
                for key, val in op.waits:
                    if isinstance(key, tuple):
                        eng.wait_ge(dsems[key[0]][key[1]], 16 * val)
                    else:
                        eng.wait_ge(sems[key], rank[(key, val)])
                if op.fn is None:
                    continue
                inst = op.fn(eng)
                if op.dma:
                    inst.then_inc(dsems[op.ev[0][0]][op.ev[0][1]], 16)
                elif op.signal:
                    inst.then_inc(sems[eng_name], 1)

        @block.tensor
        def _(e):
            run("pe", e)

        @block.scalar
        def _(e):
            run("act", e)

        @block.vector
        def _(e):
            run("dve", e)

        @block.gpsimd
        def _(e):
            run("pool", e)

        @block.sync
        def _(e):
            run("sp", e)


class Arena:
    def __init__(self, t, nbytes, base=0):
        self.t = t
        self.nbytes = base + nbytes
        self.base = base
        self.top = base

    def alloc(self, shape, dt, name=None):
        es = _esz(dt)
        n = 1
        for s in shape[1:]:
            n *= s
        nb = (n * es + 63) // 64 * 64
        off = self.top
        self.top += nb
        assert self.top <= self.nbytes, ("arena overflow", name, self.top)
        v = self.t[0:shape[0], off // 4:(off + nb) // 4]
        if dt != F32:
            v = v.bitcast(dt)
        v = v[:, 0:n]
        if len(shape) == 3:
            v = v.rearrange("p (a b) -> p a b", a=shape[1])
        elif len(shape) == 4:
            v = v.rearrange("p (a b c) -> p a b c", a=shape[1], b=shape[2])
        return v


def build(dbg_stage="full"):
    nc = bass.Bass("TRN2", target_bir_lowering=False)
    dram = {}

    def din(name, shape, dt=F32):
        dram[name] = nc.dram_tensor(name, list(shape), dt, kind="ExternalInput").ap()
        return dram[name]

    x_d = din("x", [S, D])
    ctx_d = din("ctx", [C, D])
    cT_d = din("cT", [128, 8, 2])
    wmod_d = din("w_mod", [L, 72, 128, 8, 128])
    bmodT_d = din("b_modT", [L, 128, 72])
    gT_d = din("gT", [128, 7, 8])
    w1i_d = din("w_ffn1_in", [L, D, 2 * DFF])
    w1o_d = din("w_ffn1_out", [L, DFF, D])
    w2i_d = din("w_ffn2_in", [L, D, 2 * DFF])
    w2o_d = din("w_ffn2_out", [L, DFF, D])
    wmix_d = din("w_mix", [L, D, NMIX])
    wout_d = din("w_out", [L, D, D])
    rope_d = din("rope", [128, 4, S])
    small_d = din("small_bc", [L, 128, 648])
    wup_d = din("w_up", [L, 32, 2, 128])
    bg_d = din("b_gate", [L, 1, 256])
    perm_d = din("permT", [128, 2, 128])
    rmask_d = din("rmask", [128, 4])
    xsp_d = nc.dram_tensor("xspill", [128, 8, T], F32, kind="Internal").ap()
    out_d = nc.dram_tensor("out", [S, D], F32, kind="ExternalOutput").ap()
    DUMP = os.environ.get("MK_DUMP", "")
    if DUMP:
        dbg_d = nc.dram_tensor("dbg", [128, 16384], F32, kind="ExternalOutput").ap()

    def dump(name, ap):
        if not DUMP or name != DUMP:
            return
        shp = list(ap.shape)
        n = 1
        for v_ in shp[1:]:
            n *= v_
        dv = dbg_d[0:shp[0], 0:n]
        if len(shp) == 3:
            dv = dv.rearrange("p (a b) -> p a b", a=shp[1])
        elif len(shp) == 4:
            dv = dv.rearrange("p (a b c) -> p a b c", a=shp[1], b=shp[2])
        sch.add("pool", lambda e, o=dv, i=ap: e.dma_start(out=o, in_=i), outs=[dv], ins=[ap], dma=True)

    sch = Sched()
    for n in ("x", "ctx", "cT", "w_mod", "b_modT", "gT", "w_ffn1_in", "w_ffn1_out", "w_ffn2_in", "w_ffn2_out",
              "w_mix", "w_out", "rope", "small_bc", "w_up", "b_gate", "permT", "rmask"):
        sch.untracked.add(n)

    ARENA_BYTES = 206 * 1024
    from contextlib import ExitStack
    es = ExitStack()
    arena_t = es.enter_context(nc.sbuf_tensor("arena", [128, ARENA_BYTES // 4], F32))
    ps_t = es.enter_context(nc.psum_tensor("ps", [128, 8, 512], F32))
    A = Arena(arena_t, ARENA_BYTES)

    def psb(b, n=512, dt=F32):
        v = ps_t[:, b, :]
        if dt != F32:
            v = v.bitcast(dt)
        return v[:, 0:n]

    xT = A.alloc([128, 8, T], F32, "xT")
    hT = A.alloc([128, 8, T], BF16, "hT")
    ident = A.alloc([128, 128], F32, "ident")
    ones_bf = A.alloc([128, 128], BF16, "ones_bf")
    ones_f = A.alloc([128, 128], F32, "ones_f")
    gT = A.alloc([128, 7, 8], F32, "gT")
    cT = A.alloc([128, 8, 2], F32, "cT")
    scT = A.alloc([128, 8, 2], BF16, "scT")
    modTs = [A.alloc([128, 72, 2], F32, "modT%d" % i) for i in range(2)]
    gsTs = [A.alloc([128, 3, 8, 2], F32, "gsT%d" % i) for i in range(2)]
    ghTs = [A.alloc([128, 3, 8, 2], F32, "ghT%d" % i) for i in range(2)]
    bmTs = [A.alloc([128, 72], F32, "bmT%d" % i) for i in range(2)]
    LP = {"p": 0}
    ident_bf = A.alloc([128, 128], BF16, "ident_bf")
    triA = [A.alloc([128, 128], F32, "triA%d" % i) for i in range(2)]
    triB = [A.alloc([128, 128], F32, "triB%d" % i) for i in range(2)]
    msk = [A.alloc([128, 1, 128], BF16, "msk%d" % i) for i in range(2)]
    bmask = A.alloc([128, 4, 64], F32, "bmask")
    bmask4 = A.alloc([128, 4, 128], BF16, "bmask4")
    PERSIST_TOP = A.top
    m16 = A.alloc([128, 128], F32, "m16")
    ones4 = A.alloc([128, 4, 128], F32, "ones4")
    XT_BYTES = 8 * T * 4
    HT_OFF = XT_BYTES
    HT_BYTES = 8 * T * 2

    P = sch.add

    P("pool", lambda e: e.memset(ident, 0.0), outs=[ident])
    P("pool", lambda e: e.memset(ones_f, 1.0), outs=[ones_f])
    P("pool", lambda e: e.affine_select(ident, ones_f, [[-1, 128]], ALU.is_equal, 0.0, base=0, channel_multiplier=1),
      outs=[ident], ins=[ones_f])
    P("pool", lambda e: e.memset(ones_bf, 1.0), outs=[ones_bf])
    P("dve", lambda e: e.tensor_copy(out=ident_bf, in_=ident), outs=[ident_bf], ins=[ident])
    P("pool", lambda e: e.memset(m16, -1.0 / 16.0), outs=[m16])
    P("pool", lambda e: e.memset(ones4, 1.0), outs=[ones4])
    P("pool", lambda e: e.affine_select(triA[0], m16, [[1, 128]], ALU.is_ge, 0.0, base=0, channel_multiplier=-1),
      outs=[triA[0]], ins=[m16])
    P("pool", lambda e: e.affine_select(triA[1], m16, [[-1, 128]], ALU.is_ge, 0.0, base=0, channel_multiplier=1),
      outs=[triA[1]], ins=[m16])
    P("pool", lambda e: e.affine_select(triB[0], m16, [[-1, 128]], ALU.is_gt, 0.0, base=0, channel_multiplier=1),
      outs=[triB[0]], ins=[m16])
    P("pool", lambda e: e.affine_select(triB[1], m16, [[1, 128]], ALU.is_gt, 0.0, base=0, channel_multiplier=-1),
      outs=[triB[1]], ins=[m16])
    P("pool", lambda e: e.affine_select(msk[0][:, 0, :], ones_f, [[1, 128]], ALU.is_ge, 0.0, base=0, channel_multiplier=-1),
      outs=[msk[0]], ins=[ones_f])
    P("pool", lambda e: e.affine_select(msk[1][:, 0, :], ones_f, [[-1, 128]], ALU.is_ge, 0.0, base=0, channel_multiplier=1),
      outs=[msk[1]], ins=[ones_f])
    tmpm = A.alloc([128, 4, 128], F32, "tmpm")
    P("pool", lambda e: e.affine_select(tmpm, ones4, [[-32, 4], [0, 128]], ALU.is_ge, 0.0, base=0, channel_multiplier=1),
      outs=[tmpm], ins=[ones4])
    P("pool", lambda e: e.affine_select(bmask4, tmpm, [[32, 4], [0, 128]], ALU.is_ge, 0.0, base=31, channel_multiplier=-1),
      outs=[bmask4], ins=[tmpm])
    P("pool", lambda e: e.affine_select(bmask, tmpm[:, :, 0:64], [[32, 4], [0, 64]], ALU.is_ge, 0.0, base=31, channel_multiplier=-1),
      outs=[bmask], ins=[tmpm])
    P("sp", lambda e: e.dma_start(out=gT, in_=gT_d), outs=[gT], dma=True)
    P("sp", lambda e: e.dma_start(out=cT, in_=cT_d), outs=[cT], dma=True)
    sig = A.alloc([128, 8, 2], F32, "sig")
    P("act", lambda e: e.activation(out=sig, in_=cT, func=AF.Silu), outs=[sig], ins=[cT])
    P("dve", lambda e: e.tensor_copy(out=scT, in_=sig), outs=[scT], ins=[sig])

    ldb = [A.alloc([128, D], F32, "ld%d" % i) for i in range(2)]
    for t in range(NT):
        src = x_d[t * 128:(t + 1) * 128, :] if t < 16 else ctx_d[(t - 16) * 128:(t - 15) * 128, :]
        lb = ldb[t % 2]
        P("sp", lambda e, lb=lb, src=src: e.dma_start(out=lb, in_=src), outs=[lb], dma=True)
        for h in range(2):
            bank = (t * 2 + h) % 8
            pv = psb(bank).rearrange("p (a b) -> p a b", a=4)
            for c4 in range(4):
                c = h * 4 + c4
                P("pe", lambda e, o=pv[:, c4, :], i=lb[:, c * 128:(c + 1) * 128]: e.transpose(out=o, in_=i, identity=ident),
                  outs=[pv[:, c4, :]], ins=[lb[:, c * 128:(c + 1) * 128], ident])
            dst = xT[:, h * 4:(h + 1) * 4, t * 128:(t + 1) * 128]
            if (t + h) % 2 == 0:
                P("dve", lambda e, o=dst, i=pv: e.tensor_copy(out=o, in_=i), outs=[dst], ins=[pv])
            else:
                P("act", lambda e, o=dst, i=pv: e.copy(out=o, in_=i), outs=[dst], ins=[pv])

    A.top = PERSIST_TOP
    def norm_bufs():
        return dict(sq=[A.alloc([128, 512], BF16) for _ in range(2)], rs=A.alloc([128, 512], F32),
                    t1=[A.alloc([128, 512], F32) for _ in range(2)])

    def norm_groups(tok0, ntok, gs_idx, which, nb, final=False, dst=None):
        sq, rs, t1b = nb["sq"], nb["rs"], nb["t1"]
        lp = LP["p"]

        def emit_group(g0, n, gi):
            bank = 7 - (gi % 2)
            acc = psb(bank, n)
            for c in range(8):
                s_ = sq[c % 2][:, 0:n]
                src = xT[:, c, g0:g0 + n]
                if c % 2 == 0:
                    P("act", lambda e, o=s_, i=src: e.activation(out=o, in_=i, func=AF.Square), outs=[s_], ins=[src])
                else:
                    P("pool", lambda e, o=s_, i=src: e.tensor_tensor(out=o, in0=i, in1=i, op=ALU.mult), outs=[s_], ins=[src])
                P("pe", lambda e, o=acc, r=s_, c=c: e.matmul(o, lhsT=ones_bf, rhs=r, start=(c == 0), stop=(c == 7)),
                  outs=[acc], ins=[ones_bf, s_])
            r = rs[:, 0:n]
            P("act", lambda e, o=r, i=acc: e.activation(out=o, in_=i, func=AF.Sqrt, bias=EPS, scale=1.0 / D), outs=[r], ins=[acc])
            P("dve", lambda e, o=r: e.reciprocal(out=o, in_=o), outs=[r], ins=[r])
            for c in range(8):
                src = xT[:, c, g0:g0 + n]
                if final:
                    o = dst[:, c, g0 - tok0:g0 - tok0 + n]
                    P("dve", lambda e, o=o, i=src, r=r, c=c: e.scalar_tensor_tensor(
                        out=o, in0=i, scalar=gT[:, 6, c:c + 1], in1=r, op0=ALU.mult, op1=ALU.mult), outs=[o], ins=[src, r, gT])
                else:
                    o = hT[:, c, g0:g0 + n]
                    t1 = t1b[c % 2][:, 0:n]
                    gs_ = gsTs[lp][:, gs_idx, c, which:which + 1]
                    P("dve", lambda e, o=t1, i=src, r=r, gs_=gs_: e.scalar_tensor_tensor(
                        out=o, in0=i, scalar=gs_, in1=r, op0=ALU.mult, op1=ALU.mult),
                      outs=[t1], ins=[src, r, gs_])
                    sh = modTs[lp][:, (3 * gs_idx) * 8 + c, which:which + 1]
                    P("act", lambda e, o=o, i=t1, sh=sh: e.activation(out=o, in_=i, func=AF.Identity, bias=sh, scale=1.0),
                      outs=[o], ins=[t1, sh])

        out = []
        g0 = tok0
        gi = 0
        while g0 < tok0 + ntok:
            n = min(512, tok0 + ntok - g0)
            out.append(lambda g0=g0, n=n, gi=gi: emit_group(g0, n, gi))
            g0 += n
            gi += 1
        return out

    def rms_norm_to_hT(tok0, ntok, gs_idx, which, final=False, dst=None):
        tmp_top = A.top
        nb = norm_bufs()
        for g in norm_groups(tok0, ntok, gs_idx, which, nb, final, dst):
            g()
        A.top = tmp_top

    def mod_emitters(l, wbufs, bank):
        mT, gS, gH, bm = modTs[l % 2], gsTs[l % 2], ghTs[l % 2], bmTs[l % 2]
        acc = psb(bank, 144).rearrange("p (a b) -> p a b", a=72)

        def chunk(fc):
            w = wbufs[fc % len(wbufs)]
            src = wmod_d[l, fc]
            P("pool", lambda e, o=w, i=src: e.dma_start(out=o, in_=i), outs=[w], dma=True)
            for k in range(8):
                P("pe", lambda e, o=acc[:, fc, :], w_=w[:, k, :], r=scT[:, k, :], k=k:
                  e.matmul(o, lhsT=w_, rhs=r, start=(k == 0), stop=(k == 7)),
                  outs=[acc[:, fc, :]], ins=[w[:, k, :], scT[:, k, :]])

        def fin():
            P("sp", lambda e: e.dma_start(out=bm, in_=bmodT_d[l]), outs=[bm], dma=True)
            for w_ in range(2):
                P("dve", lambda e, w_=w_: e.tensor_tensor(out=mT[:, :, w_], in0=acc[:, :, w_], in1=bm, op=ALU.add),
                  outs=[mT[:, :, w_]], ins=[acc[:, :, w_], bm])
            for n in range(3):
                for w_ in range(2):
                    sc = mT[:, (3 * n + 1) * 8:(3 * n + 2) * 8, w_]
                    o = gS[:, n, :, w_]
                    P("dve", lambda e, o=o, sc=sc, n=n: e.scalar_tensor_tensor(
                        out=o, in0=sc, scalar=1.0, in1=gT[:, 3 * l + n, :], op0=ALU.add, op1=ALU.mult),
                      outs=[o], ins=[sc, gT])
                    ga = mT[:, (3 * n + 2) * 8:(3 * n + 3) * 8, w_]
                    o2 = gH[:, n, :, w_]
                    P("dve", lambda e, o=o2, ga=ga, n=n: e.tensor_scalar(
                        out=o, in0=ga, scalar1=(1.0 if n == 1 else 0.5), scalar2=None, op0=ALU.mult),
                      outs=[o2], ins=[ga])

        return [(lambda fc=fc: chunk(fc)) for fc in range(72)] + [fin]

    def compute_mod(l):
        tmp_top = A.top
        wb = [A.alloc([128, 8, 128], BF16) for _ in range(4)]
        for em in mod_emitters(l, wb, 6):
            em()
        A.top = tmp_top

    def ffn(l, n_idx, w_in_d, w_out_d, groups):
        tmp_top = A.top
        aT = A.alloc([128, NJ, 1152], BF16, "aT")
        wi = [A.alloc([128, 8, 256], BF16, "wi%d" % i) for i in range(3)]
        wo = [A.alloc([128, NJ, 128], BF16, "wo%d" % i) for i in range(3)]
        sg = [A.alloc([128, 384], F32, "sg%d" % i) for i in range(2)]
        nbuf = norm_bufs()
        for pi, (p0, pn, segs) in enumerate(groups):
            if pi == 0:
                for (t0, tn, which) in segs:
                    for g in norm_groups(t0, tn, n_idx, which, nbuf):
                        g()
            nxt_norm = []
            if pi + 1 < len(groups):
                for (t0, tn, which) in groups[pi + 1][2]:
                    nxt_norm += norm_groups(t0, tn, n_idx, which, nbuf)
            regs = []
            r0 = p0
            while r0 < p0 + pn:
                rn = min(384, p0 + pn - r0)
                regs.append((r0, rn))
                r0 += rn
            assert len(regs) <= 3
            for j in range(NJ):
                w = wi[j % 3]
                srcg = w_in_d[l, :, j * 128:(j + 1) * 128].rearrange("(kc p) f -> p kc f", p=128)
                srcu = w_in_d[l, :, DFF + j * 128:DFF + (j + 1) * 128].rearrange("(kc p) f -> p kc f", p=128)
                P("pool", lambda e, o=w[:, :, 0:128], i=srcg: e.dma_start(out=o, in_=i), outs=[w[:, :, 0:128]], dma=True)
                P("pool", lambda e, o=w[:, :, 128:256], i=srcu: e.dma_start(out=o, in_=i), outs=[w[:, :, 128:256]], dma=True)
                for ri, (r0, rn) in enumerate(regs):
                    pg = psb(ri, rn)
                    pu = psb(3 + ri, rn)
                    for k in range(8):
                        P("pe", lambda e, o=pg, w_=w[:, k, 0:128], r=hT[:, k, r0:r0 + rn], k=k:
                          e.matmul(o, lhsT=w_, rhs=r, start=(k == 0), stop=(k == 7)),
                          outs=[pg], ins=[w[:, k, 0:128], hT[:, k, r0:r0 + rn]])
                    for k in range(8):
                        P("pe", lambda e, o=pu, w_=w[:, k, 128:256], r=hT[:, k, r0:r0 + rn], k=k:
                          e.matmul(o, lhsT=w_, rhs=r, start=(k == 0), stop=(k == 7)),
                          outs=[pu], ins=[w[:, k, 128:256], hT[:, k, r0:r0 + rn]])
                    s = sg[(j * 3 + ri) % 2][:, 0:rn]
                    P("act", lambda e, o=s, i=pg: e.activation(out=o, in_=i, func=AF.Silu), outs=[s], ins=[pg])
                    o = aT[:, j, r0 - p0:r0 - p0 + rn]
                    P("dve", lambda e, o=o, a=pu, b=s: e.tensor_tensor(out=o, in0=a, in1=b, op=ALU.mult), outs=[o], ins=[pu, s])
            bi = 0
            for c in range(8):
                w = wo[c % 3]
                src = w_out_d[l, :, c * 128:(c + 1) * 128].rearrange("(j p) f -> p j f", p=128)
                P("pool", lambda e, o=w, i=src: e.dma_start(out=o, in_=i), outs=[w], dma=True)
                for (r0, rn) in regs:
                    py = psb(bi % 6, rn)
                    bi += 1
                    for j in range(NJ):
                        P("pe", lambda e, o=py, w_=w[:, j, :], r=aT[:, j, r0 - p0:r0 - p0 + rn], j=j:
                          e.matmul(o, lhsT=w_, rhs=r, start=(j == 0), stop=(j == NJ - 1)),
                          outs=[py], ins=[w[:, j, :], aT[:, j, r0 - p0:r0 - p0 + rn]])
                    for (t0, tn, which) in segs:
                        a0 = max(t0, r0)
                        a1 = min(t0 + tn, r0 + rn)
                        if a1 <= a0:
                            continue
                        xs = xT[:, c, a0:a1]
                        ys = py[:, a0 - r0:a1 - r0]
                        gh_ = ghTs[LP["p"]][:, n_idx, c, which:which + 1]
                        P("dve", lambda e, xs=xs, ys=ys, gh_=gh_: e.scalar_tensor_tensor(
                            out=xs, in0=ys, scalar=gh_, in1=xs, op0=ALU.mult, op1=ALU.add),
                          outs=[xs], ins=[ys, xs, gh_])
                if nxt_norm and c % 2 == 1:
                    nxt_norm.pop(0)()
            while nxt_norm:
                nxt_norm.pop(0)()
        A.top = tmp_top

    def mixer(l):
        ctx_out = (l < L - 1)
        lam_init = 0.8 - 0.6 * math.exp(-0.3 * l)
        top0 = A.top
        out_tiles = list(range(NT)) if ctx_out else list(range(16))
        rms_norm_to_hT(0, S, 1, 0)
        rms_norm_to_hT(S, C, 1, 1)
        P("sp", lambda e: e.dma_start(out=xsp_d, in_=xT), outs=[xsp_d], ins=[xT], dma=True)
        AXa = Arena(arena_t, XT_BYTES, 0)
        AH = Arena(arena_t, HT_BYTES, HT_OFF)
        glaQT = AXa.alloc([128, 1, T], BF16)
        glaKT = AXa.alloc([128, 1, T], BF16)
        gfbT = AXa.alloc([128, 1, T], BF16)
        swaQT = AXa.alloc([128, 4, T], BF16)
        swaKT = AXa.alloc([128, 2, T], BF16)
        diffQT = AXa.alloc([128, 3, T], BF16)
        diffKT = AXa.alloc([128, 3, T], BF16)
        glaTM = A.alloc([128, NT, 640], BF16)
        swaV = A.alloc([128, NT, 2, 65], BF16)
        diffV = A.alloc([128, NT, 4, 65], BF16)
        small = A.alloc([128, 648], F32)
        wup = A.alloc([32, 2, 128], BF16)
        bg = A.alloc([1, 256], BF16)
        esink = A.alloc([128, 8], F32)
        nlam = A.alloc([128, 1], F32)
        gdiff = A.alloc([128, 4, 64], F32)
        lamt = A.alloc([128, 2, 32], F32)
        lams = A.alloc([128, 2], F32)
        mix_top = A.top
        ropeT = A.alloc([128, 4, S], BF16)
        P("pool", lambda e: e.dma_start(out=ropeT, in_=rope_d), outs=[ropeT], dma=True)
        P("sp", lambda e: e.dma_start(out=small, in_=small_d[l]), outs=[small], dma=True)
        P("pool", lambda e: e.dma_start(out=wup, in_=wup_d[l]), outs=[wup], dma=True)
        P("pool", lambda e: e.dma_start(out=bg, in_=bg_d[l]), outs=[bg], dma=True)
        P("pool", lambda e: e.memset(swaV[:, :, :, 64:65], 1.0), outs=[swaV[:, :, :, 64:65]])
        P("pool", lambda e: e.memset(diffV[:, :, :, 64:65], 1.0), outs=[diffV[:, :, :, 64:65]])
        P("act", lambda e: e.activation(out=esink, in_=small[:, 512:520], func=AF.Exp), outs=[esink], ins=[small])
        dl = small[:, 520:648].rearrange("p (a b) -> p a b", a=4)
        P("dve", lambda e: e.tensor_tensor(out=lamt[:, 0, :], in0=dl[:, 0, :], in1=dl[:, 1, :], op=ALU.mult),
          outs=[lamt[:, 0, :]], ins=[small])
        P("dve", lambda e: e.tensor_tensor(out=lamt[:, 1, :], in0=dl[:, 2, :], in1=dl[:, 3, :], op=ALU.mult),
          outs=[lamt[:, 1, :]], ins=[small])
        P("dve", lambda e: e.reduce_sum(out=lams, in_=lamt, axis=AX.X), outs=[lams], ins=[lamt])
        P("act", lambda e: e.activation(out=lams, in_=lams, func=AF.Exp), outs=[lams], ins=[lams])
        P("dve", lambda e: e.scalar_tensor_tensor(out=nlam, in0=lams[:, 1:2], scalar=-lam_init, in1=lams[:, 0:1],
                                                  op0=ALU.add, op1=ALU.subtract), outs=[nlam], ins=[lams])
        P("dve", lambda e: e.tensor_scalar(out=gdiff, in0=small[:, 256:512].rearrange("p (a b) -> p a b", a=4),
                                           scalar1=1.0 - lam_init, scalar2=None, op0=ALU.mult), outs=[gdiff], ins=[small])

        wfm = [A.alloc([128, 2, 8, 128], BF16) for _ in range(2)]
        permT = A.alloc([128, 2, 128], BF16)
        xb = [A.alloc([128, 512], BF16) for _ in range(2)]
        P("pool", lambda e: e.dma_start(out=permT, in_=perm_d), outs=[permT], dma=True)
        wtm = [A.alloc([128, 8, 512], BF16) for _ in range(2)]
        rt = [A.alloc([128, 512], F32) for _ in range(2)]
        fm = [(0, None, glaQT[:, 0, :], None), (1, None, glaKT[:, 0, :], None), (2, None, gfbT[:, 0, :], None)]
        for c in range(4):
            fm.append((3 + c, 7 + c, swaQT[:, c, :], 0))
        for c in range(2):
            fm.append((11 + c, 13 + c, swaKT[:, c, :], 0))
        for c in range(3):
            fm.append((15 + c, 18 + c, diffQT[:, c, :], 2))
        for c in range(3):
            fm.append((21 + c, 24 + c, diffKT[:, c, :], 2))
        groups = [(0, 512), (512, 512), (1024, 512), (1536, 512), (2048, 256)]
        gi = 0
        def load_fm(fi):
            cm, cp, dest, tb = fm[fi]
            w = wfm[fi % 2]
            src = wmix_d[l, :, cm * 128:(cm + 1) * 128].rearrange("(kc p) f -> p kc f", p=128)
            P("pool", lambda e, o=w[:, 0, :, :], i=src: e.dma_start(out=o, in_=i), outs=[w[:, 0, :, :]], dma=True)

        load_fm(0)
        for g in range(2):
            src = wmix_d[l, :, NFM * 128 + g * 512:NFM * 128 + (g + 1) * 512].rearrange("(kc p) f -> p kc f", p=128)
            P("pool", lambda e, o=wtm[g], i=src: e.dma_start(out=o, in_=i), outs=[wtm[g]], dma=True)
        for fi, (cm, cp, dest, tb) in enumerate(fm):
            w = wfm[fi % 2]
            if fi + 1 < len(fm):
                load_fm(fi + 1)
            for (g0, n) in groups:
                b0 = (2 * gi) % 4
                gi += 1
                pm = psb(b0, n)
                pp = psb(b0 + 1, n)
                rope = (cp is not None) and g0 < S
                for k in range(8):
                    P("pe", lambda e, o=pm, w_=w[:, 0, k, :], r=hT[:, k, g0:g0 + n], k=k:
                      e.matmul(o, lhsT=w_, rhs=r, start=(k == 0), stop=(k == 7)),
                      outs=[pm], ins=[w[:, 0, k, :], hT[:, k, g0:g0 + n]])
                if rope:
                    xb_ = xb[gi % 2][:, 0:n]
                    P("act", lambda e, o=xb_, i=pm: e.copy(out=o, in_=i), outs=[xb_], ins=[pm])
                    pmx = permT[:, (0 if tb == 0 else 1), :]
                    P("pe", lambda e, o=pp, w_=pmx, r=xb_: e.matmul(o, lhsT=w_, rhs=r, start=True, stop=True),
                      outs=[pp], ins=[pmx, xb_])
                    t1 = rt[0][:, 0:n]
                    t2 = rt[1][:, 0:n]
                    cs = ropeT[:, tb, g0:g0 + n]
                    sn = ropeT[:, tb + 1, g0:g0 + n]
                    P("dve", lambda e, o=t1, a=pm, b=cs: e.tensor_tensor(out=o, in0=a, in1=b, op=ALU.mult), outs=[t1], ins=[pm, cs])
                    P("dve", lambda e, o=t2, a=pp, b=sn: e.tensor_tensor(out=o, in0=a, in1=b, op=ALU.mult), outs=[t2], ins=[pp, sn])
                    d_ = dest[:, g0:g0 + n]
                    P("pool", lambda e, o=d_, a=t1, b=t2: e.tensor_tensor(out=o, in0=a, in1=b, op=ALU.add), outs=[d_], ins=[t1, t2])
                else:
                    d_ = dest[:, g0:g0 + n]
                    P("act", lambda e, o=d_, i=pm: e.copy(out=o, in_=i), outs=[d_], ins=[pm])
        for t in range(NT):
            for g in range(2):
                pb = psb(4 + (t * 2 + g) % 4)
                for k in range(8):
                    P("pe", lambda e, o=pb, a=hT[:, k, t * 128:(t + 1) * 128], w_=wtm[g][:, k, :], k=k:
                      e.matmul(o, lhsT=a, rhs=w_, start=(k == 0), stop=(k == 7)),
                      outs=[pb], ins=[hT[:, k, t * 128:(t + 1) * 128], wtm[g][:, k, :]])
                if g == 0:
                    d_ = glaTM[:, t, 0:512]
                    P("act", lambda e, o=d_, i=pb: e.copy(out=o, in_=i), outs=[d_], ins=[pb])
                else:
                    d_ = glaTM[:, t, 512:640]
                    P("dve", lambda e, o=d_, i=pb[:, 0:128]: e.tensor_copy(out=o, in_=i), outs=[d_], ins=[pb[:, 0:128]])
                    d2 = swaV[:, t, :, 0:64]
                    s2 = pb[:, 128:256].rearrange("p (a b) -> p a b", a=2)
                    P("act", lambda e, o=d2, i=s2: e.copy(out=o, in_=i), outs=[d2], ins=[s2])
                    d3 = diffV[:, t, :, 0:64]
                    s3 = pb[:, 256:512].rearrange("p (a b) -> p a b", a=4)
                    P("dve", lambda e, o=d3, i=s3: e.tensor_copy(out=o, in_=i), outs=[d3], ins=[s3])
        A.top = mix_top
        for nm_, ap_ in (("glaQT", glaQT), ("glaKT", glaKT), ("gfbT", gfbT), ("swaQT", swaQT), ("swaKT", swaKT),
                         ("diffQT", diffQT), ("diffKT", diffKT), ("glaTM", glaTM), ("swaV", swaV), ("diffV", diffV), ("hT", hT)):
            dump(nm_, ap_)

        _cut = os.environ.get("MK_MIXCUT", "")
        if _cut == "inproj":
            P("sp", lambda e: e.dma_start(out=xT, in_=xsp_d), outs=[xT], ins=[xsp_d], dma=True)
            A.top = top0
            return
        esp = AH.alloc([128, 2, NT, 128], F32)
        ost = AH.alloc([128, NT, 256], F32)
        gla_out = A.alloc([128, NT, 256], BF16)
        gla_top = A.top
        bi = 0
        for d_ in range(2):
            for t0 in range(0, NT, 4):
                nb = min(4, NT - t0)
                pb = psb(bi % 4).rearrange("p (a b) -> p a b", a=4)
                bi += 1
                for s_ in range(nb):
                    t = t0 + s_
                    P("pe", lambda e, o=pb[:, s_, :], a=gfbT[0:32, 0, t * 128:(t + 1) * 128], w_=wup[:, d_, :]:
                      e.matmul(o, lhsT=a, rhs=w_, start=True, stop=False),
                      outs=[pb[:, s_, :]], ins=[gfbT[0:32, 0, t * 128:(t + 1) * 128], wup[:, d_, :]])
                    P("pe", lambda e, o=pb[:, s_, :], a=ones_bf[0:1, :], w_=bg[0:1, d_ * 128:(d_ + 1) * 128]:
                      e.matmul(o, lhsT=a, rhs=w_, start=False, stop=True),
                      outs=[pb[:, s_, :]], ins=[ones_bf[0:1, :], bg[0:1, d_ * 128:(d_ + 1) * 128]])
                o_ = esp[:, d_, t0:t0 + nb, :]
                P("act", lambda e, o=o_, i=pb[:, 0:nb, :]: e.activation(out=o, in_=i, func=AF.Exp, scale=-1.0),
                  outs=[o_], ins=[pb[:, 0:nb, :]])
        for d_ in range(2):
            P("act", lambda e, o=esp[:, d_, :, :]: e.activation(out=o, in_=o, func=AF.Ln, bias=1.0, scale=1.0),
              outs=[esp[:, d_, :, :]], ins=[esp[:, d_, :, :]])
        E1 = [A.alloc([128, 128], F32) for _ in range(2)]
        E2 = [A.alloc([128, 128], F32) for _ in range(2)]
        E3 = [A.alloc([128, 128], F32) for _ in range(2)]
        KtT = [A.alloc([128, 128], BF16) for _ in range(2)]
        Kh = [A.alloc([128, 128], BF16) for _ in range(2)]
        Qexp = [A.alloc([128, 4, 128], BF16) for _ in range(2)]
        QtT = [[A.alloc([128, 1, 128], BF16) for _ in range(2)] for _ in range(2)]
        attm = [[A.alloc([128, 4, 128], BF16) for _ in range(2)] for _ in range(2)]
        Um = [[A.alloc([128, 4, 64], F32) for _ in range(2)] for _ in range(2)]
        decs = [[A.alloc([128, 1], F32) for _ in range(2)] for _ in range(2)]
        Sf = [A.alloc([128, 4, 64], F32) for _ in range(2)]
        Sb = [[A.alloc([128, 4, 64], BF16) for _ in range(2)] for _ in range(2)]
        orders = [[16, 17] + list(range(16)), [17, 16] + list(range(15, -1, -1))]
        visited = set()
        for d_ in range(2):
            P("pool", lambda e, d_=d_: e.memset(Sf[d_], 0.0), outs=[Sf[d_]])
            P("pool", lambda e, d_=d_: e.memset(Sb[d_][0], 0.0), outs=[Sb[d_][0]])

        def gla_prep(st, d_):
            t = orders[d_][st]
            par = st % 2
            sp_t = esp[:, d_, t, :]
            bA = psb(3 * d_)
            pc, pr, pu = bA[:, 0:128], bA[:, 128:256], bA[:, 256:512]
            P("pe", lambda e, o=pc, a=sp_t, b=triA[d_]: e.matmul(o, lhsT=a, rhs=b, start=True, stop=True),
              outs=[pc], ins=[sp_t, triA[d_]])
            P("pe", lambda e, o=pr, a=triB[d_], b=sp_t: e.matmul(o, lhsT=a, rhs=b, start=True, stop=True),
              outs=[pr], ins=[triB[d_], sp_t])
            e1, e2, e3 = E1[d_], E2[d_], E3[d_]
            P("act", lambda e, o=e3, i=pr: e.activation(out=o, in_=i, func=AF.Exp), outs=[e3], ins=[pr])
            P("act", lambda e, o=e1, i=pc: e.activation(out=o, in_=i, func=AF.Exp), outs=[e1], ins=[pc])
            if t in out_tiles:
                P("act", lambda e, o=e2, i=pc: e.activation(out=o, in_=i, func=AF.Exp, scale=-1.0), outs=[e2], ins=[pc])
            kh = Kh[d_]
            P("pool", lambda e, o=kh, a=glaTM[:, t, 512:640], b=e3: e.tensor_tensor(out=o, in0=a, in1=b, op=ALU.mult),
              outs=[kh], ins=[glaTM[:, t, 512:640], e3])
            P("pe", lambda e, o=pu, a=kh, b=glaTM[:, t, 0:256]: e.matmul(o, lhsT=a, rhs=b, start=True, stop=True),
              outs=[pu], ins=[kh, glaTM[:, t, 0:256]])
            dsrc = e1[:, 127:128] if d_ == 0 else e1[:, 0:1]
            dc = decs[d_][par]
            P("pool", lambda e, o=dc, i=dsrc: e.tensor_copy(out=o, in_=i), outs=[dc], ins=[dsrc])
            um = Um[d_][par]
            P("dve", lambda e, o=um, a=pu.rearrange("p (a b) -> p a b", a=4): e.tensor_tensor(out=o, in0=a, in1=bmask, op=ALU.mult),
              outs=[um], ins=[pu, bmask])
            if t in out_tiles:
                qt_, kt_ = QtT[d_][par], KtT[d_]
                tq = glaQT[:, :, t * 128:(t + 1) * 128]
                P("dve", lambda e, o=qt_, a=tq, b=e1: e.scalar_tensor_tensor(
                    out=o[:, 0, :], in0=a[:, 0, :], scalar=float(32 ** -0.5), in1=b, op0=ALU.mult, op1=ALU.mult),
                  outs=[qt_], ins=[tq, e1])
                tk = glaKT[:, 0, t * 128:(t + 1) * 128]
                P("pool", lambda e, o=kt_, a=tk, b=e2: e.tensor_tensor(out=o, in0=a, in1=b, op=ALU.mult), outs=[kt_], ins=[tk, e2])
                qe = Qexp[d_]
                P("pool", lambda e, o=qe, a=qt_: e.tensor_tensor(out=o, in0=bmask4, in1=a.broadcast_to([128, 4, 128]), op=ALU.mult),
                  outs=[qe], ins=[bmask4, qt_])
                pa = psb(3 * d_ + 1)
                P("pe", lambda e, o=pa, a=kt_, b=qe: e.matmul(o, lhsT=a, rhs=b.rearrange("p a b -> p (a b)"), start=True, stop=True),
                  outs=[pa], ins=[kt_, qe])
                am = attm[d_][par]
                P("dve", lambda e, o=am, a=pa.rearrange("p (a b) -> p a b", a=4), m_=msk[d_]:
                  e.tensor_tensor(out=o, in0=a, in1=m_.broadcast_to([128, 4, 128]), op=ALU.mult), outs=[am], ins=[pa, msk[d_]])

        def gla_chain(st, d_):
            t = orders[d_][st]
            par = st % 2
            cur, nxt = st % 2, (st + 1) % 2
            sb_cur, sb_nxt = Sb[d_][cur], Sb[d_][nxt]
            if t in out_tiles:
                qt_, am = QtT[d_][par], attm[d_][par]
                po = psb(3 * d_ + 2, 256)
                for h in range(4):
                    oh = po[:, h * 64:(h + 1) * 64]
                    P("pe", lambda e, o=oh, a=qt_[:, 0, :], b=sb_cur[:, h, :]: e.matmul(o, lhsT=a, rhs=b, start=True, stop=False),
                      outs=[oh], ins=[qt_, sb_cur[:, h, :]])
                    P("pe", lambda e, o=oh, a=am[:, h, :], b=glaTM[:, t, h * 64:(h + 1) * 64]: e.matmul(o, lhsT=a, rhs=b, start=False, stop=True),
                      outs=[oh], ins=[am[:, h, :], glaTM[:, t, h * 64:(h + 1) * 64]])
                if t not in visited:
                    visited.add(t)
                    P("act", lambda e, o=ost[:, t, :], i=po: e.copy(out=o, in_=i), outs=[ost[:, t, :]], ins=[po])
                else:
                    P("dve", lambda e, o=ost[:, t, :], i=po: e.tensor_tensor(out=o, in0=i, in1=o, op=ALU.add),
                      outs=[ost[:, t, :]], ins=[po, ost[:, t, :]])
            P("dve", lambda e, o=Sf[d_], sc=decs[d_][par], b=Um[d_][par]: e.scalar_tensor_tensor(
                out=o, in0=o, scalar=sc, in1=b, op0=ALU.mult, op1=ALU.add), outs=[Sf[d_]], ins=[Sf[d_], decs[d_][par], Um[d_][par]])
            P("act", lambda e, o=sb_nxt, i=Sf[d_]: e.copy(out=o, in_=i), outs=[sb_nxt], ins=[Sf[d_]])

        mod_ems = []
        if l + 1 < L and dbg_stage == "full":
            wb2 = [A.alloc([128, 8, 128], BF16) for _ in range(2)]
            mod_ems = mod_emitters(l + 1, wb2, 6)
        for st in range(NT + 1):
            for d_ in range(2):
                if st < NT:
                    gla_prep(st, d_)
            for d_ in range(2):
                if st >= 1:
                    gla_chain(st - 1, d_)
            for _ in range(4):
                if len(mod_ems) > 1:
                    mod_ems.pop(0)()
        while mod_ems:
            mod_ems.pop(0)()
        A.top = gla_top
        no = len(out_tiles)
        ssg = A.alloc([128, NT, 4, 1], F32)
        sqg = [A.alloc([128, 4, 64], F32) for _ in range(2)]
        sgl = [A.alloc([128, 256], F32) for _ in range(2)]
        for t in out_tiles:
            o4 = ost[:, t, :].rearrange("p (a b) -> p a b", a=4)
            q_ = sqg[t % 2]
            P("pool", lambda e, o=q_, a=o4: e.tensor_tensor(out=o, in0=a, in1=a, op=ALU.mult), outs=[q_], ins=[o4])
            P("dve", lambda e, o=ssg[:, t, :, 0], i=q_: e.reduce_sum(out=o, in_=i, axis=AX.X), outs=[ssg[:, t, :, :]], ins=[q_])
        sv = ssg[:, 0:no, :, :]
        P("act", lambda e, o=sv: e.activation(out=o, in_=o, func=AF.Sqrt, bias=EPS, scale=1.0 / 64.0), outs=[sv], ins=[sv])
        P("dve", lambda e, o=sv: e.reciprocal(out=o, in_=o), outs=[sv], ins=[sv])
        ggla = small[:, 0:256].rearrange("p (a b) -> p a b", a=4)
        for t in out_tiles:
            o4 = ost[:, t, :].rearrange("p (a b) -> p a b", a=4)
            q_ = sqg[t % 2]
            sg_ = sgl[t % 2]
            P("act", lambda e, o=sg_, i=glaTM[:, t, 256:512]: e.activation(out=o, in_=i, func=AF.Silu), outs=[sg_], ins=[glaTM[:, t, 256:512]])
            P("dve", lambda e, o=q_, a=o4, r=ssg[:, t, :, :]: e.tensor_tensor(out=o, in0=a, in1=r.broadcast_to([128, 4, 64]), op=ALU.mult),
              outs=[q_], ins=[o4, ssg[:, t, :, :]])
            P("pool", lambda e, o=q_: e.tensor_tensor(out=o, in0=o, in1=ggla, op=ALU.mult), outs=[q_], ins=[q_, small])
            go = gla_out[:, t, :]
            P("dve", lambda e, o=go, a=q_, b=sg_: e.tensor_tensor(out=o, in0=a.rearrange("p a b -> p (a b)"), in1=b, op=ALU.mult),
              outs=[go], ins=[q_, sg_])
        A.top = gla_top

        dump("gla_out", gla_out)
        dump("ost", ost)
        if _cut == "gla":
            P("sp", lambda e: e.dma_start(out=xT, in_=xsp_d), outs=[xT], ins=[xsp_d, gla_out], dma=True)
            A.top = top0
            return
        AH2 = Arena(arena_t, HT_BYTES, HT_OFF)
        wo_sb = AH2.alloc([128, 8, D], BF16)
        P("pool", lambda e: e.dma_start(out=wo_sb, in_=wout_d[l].rearrange("(kc p) f -> p kc f", p=128)), outs=[wo_sb], dma=True)
        pT = [AH2.alloc([128, 8, 128], BF16) for _ in range(3)]
        otok = [A.alloc([128, 768], BF16) for _ in range(2)]
        dstore = AH2.alloc([128, 4, 8, 64], F32)
        rmask = A.alloc([128, 4], F32)
        qmb = [A.alloc([128, 512], BF16) for _ in range(2)]
        P("sp", lambda e: e.dma_start(out=rmask, in_=rmask_d), outs=[rmask], dma=True)
        oTblk = A.alloc([128, 8, 512], BF16)
        xcs = [A.alloc([128, 512], F32) for _ in range(2)]
        den = [AH2.alloc([128, 4, 1], F32) for _ in range(2)]
        dctx = AH2.alloc([128, 8, 64], F32)
        od = AH2.alloc([128, 4, 64], F32)
        od2 = AH2.alloc([128, 4, 64], F32)
        ssd = AH2.alloc([128, 4, 1], F32)
        state = {"bank": 0, "pt": 0, "xc": 0, "yb": 0}
        dump("wo_sb", wo_sb[:, :, 0:2048] if False else wo_sb)

        def run_blocks(blocks, scale):
            batches = []
            for blk in blocks:
                rg = blk[0].base_partition()
                w = blk[1].shape[-1]
                if (batches and batches[-1][0][0].base_partition() == rg and batches[-1][0][1].shape[-1] == w
                        and (len(batches[-1]) + 1) * w <= 1024):
                    batches[-1].append(blk)
                else:
                    batches.append([blk])
            pend = None
            for bl in batches + [None]:
                cur = None
                if bl is not None:
                    w = bl[0][1].shape[-1]
                    ncol = len(bl) * w
                    b2 = (state["bank"] % 2) * 2
                    state["bank"] += 1
                    bank = ps_t[:, b2:b2 + 2, :].rearrange("p a c -> p (a c)")
                    p_ = pT[state["pt"] % 3].rearrange("p a c -> p (a c)")
                    state["pt"] += 1
                    for i, (kT, qT, m_, v, accs, first, last) in enumerate(bl):
                        P("pe", lambda e, o=bank[:, i * w:(i + 1) * w], a=kT, b=qT: e.matmul(o, lhsT=a, rhs=b, start=True, stop=True),
                          outs=[bank[:, i * w:(i + 1) * w]], ins=[kT, qT])
                    P("act", lambda e, o=p_[:, 0:ncol], i=bank[:, 0:ncol]: e.activation(out=o, in_=i, func=AF.Exp, scale=scale),
                      outs=[p_[:, 0:ncol]], ins=[bank[:, 0:ncol]])
                    for i, (kT, qT, m_, v, accs, first, last) in enumerate(bl):
                        if m_ is not None:
                            P("pool", lambda e, o=p_[:, i * w:(i + 1) * w], m_=m_: e.tensor_tensor(out=o, in0=o, in1=m_[:, 0, :], op=ALU.mult),
                              outs=[p_[:, i * w:(i + 1) * w]], ins=[p_[:, i * w:(i + 1) * w], m_])
                    cur = (bl, p_, w)
                if pend is not None:
                    pbl, pp_, pw = pend
                    for i, (kT, qT, m_, v, accs, first, last) in enumerate(pbl):
                        for su, acc in enumerate(accs):
                            lh = pp_[:, i * pw + su * 128:i * pw + (su + 1) * 128]
                            st_ = first and su == 0
                            P("pe", lambda e, o=acc, a=lh, b=v, st_=st_, last=last: e.matmul(
                                o, lhsT=a, rhs=b, start=st_, stop=last, skip_group_check=(len(accs) > 1)),
                              outs=[acc], ins=[lh, v])
                pend = cur

        def diff_finish(src4, ot):
            t4 = src4.rearrange("p (h w) d -> p h w d", w=2)
            P("dve", lambda e, a=t4[:, :, 1, :], b=t4[:, :, 0, :]: e.scalar_tensor_tensor(
                out=od, in0=a, scalar=nlam[:, 0:1], in1=b, op0=ALU.mult, op1=ALU.add), outs=[od], ins=[src4, nlam])
            P("pool", lambda e: e.tensor_tensor(out=od2, in0=od, in1=od, op=ALU.mult), outs=[od2], ins=[od])
            P("dve", lambda e: e.reduce_sum(out=ssd[:, :, 0], in_=od2, axis=AX.X), outs=[ssd], ins=[od2])
            P("act", lambda e: e.activation(out=ssd, in_=ssd, func=AF.Sqrt, bias=EPS, scale=1.0 / 64.0), outs=[ssd], ins=[ssd])
            P("dve", lambda e: e.reciprocal(out=ssd, in_=ssd), outs=[ssd], ins=[ssd])
            P("dve", lambda e: e.tensor_tensor(out=od2, in0=od, in1=ssd.broadcast_to([128, 4, 64]), op=ALU.mult), outs=[od2], ins=[od, ssd])
            o_ = ot[:, 512:768].rearrange("p (a b) -> p a b", a=4)
            P("pool", lambda e, o=o_: e.tensor_tensor(out=o, in0=od2, in1=gdiff, op=ALU.mult), outs=[o_], ins=[od2, gdiff])

        def diff_qblock(qb):
            q0 = qb * 512
            for g in range(8):
                h = g // 2
                ch, base = g // 3, (g % 3) * 32
                accb = psb(6 + g % 2)[:, 0:260].rearrange("p (a b) -> p a b", a=4)
                qm = qmb[g % 2]
                P("dve", lambda e, o=qm, a=diffQT[:, ch, q0:q0 + 512], m_=rmask[:, (g % 3):(g % 3) + 1]: e.tensor_scalar(
                    out=o, in0=a, scalar1=m_, scalar2=None, op0=ALU.mult), outs=[qm], ins=[diffQT[:, ch, q0:q0 + 512], rmask])
                blocks = []
                for kt in range(NT):
                    blocks.append((diffKT[:, ch, kt * 128:(kt + 1) * 128], qm, None,
                                   diffV[:, kt, h, :], [accb[:, su, :] for su in range(4)], kt == 0, kt == NT - 1))
                run_blocks(blocks, float(32 ** -0.5))
                dn = den[g % 2]
                P("dve", lambda e, o=dn, a=accb[:, :, 64:65]: e.reciprocal(out=o, in_=a), outs=[dn], ins=[accb[:, :, 64:65]])
                P("dve", lambda e, o=dstore[:, :, g, :], a=accb[:, :, 0:64], r=dn: e.tensor_tensor(
                    out=o, in0=a, in1=r.broadcast_to([128, 4, 64]), op=ALU.mult), outs=[dstore[:, :, g, :]], ins=[accb[:, :, 0:64], dn])

        for qi, qt in enumerate(out_tiles):
            which = 0 if qt < 16 else 1
            ot = otok[qi % 2]
            qs = slice(qt * 128, (qt + 1) * 128)
            if qt < 16 and qt % 4 == 0:
                diff_qblock(qt // 4)
            blocks = []
            for h in range(8):
                kg, kq, base = h // 4, h // 2, (h % 2) * 64
                if qt < 16:
                    kts = [(kt, (None if kt == qt else (msk[1] if kt < qt else msk[0])))
                           for kt in (qt - 1, qt, qt + 1) if 0 <= kt < 16] + [(16, None), (17, None)]
                else:
                    kts = [(16, None), (17, None)]
                acc = psb(4 + h // 4)[:, (h % 4) * 65:(h % 4) * 65 + 65]
                for j, (kt, m_) in enumerate(kts):
                    blocks.append((swaKT[base:base + 64, kg, kt * 128:(kt + 1) * 128], swaQT[base:base + 64, kq, qs], m_,
                                   swaV[:, kt, kg, :], [acc], j == 0, j == len(kts) - 1))
            run_blocks(blocks, 0.125)
            for b_ in range(2):
                av = psb(4 + b_)[:, 0:260].rearrange("p (a b) -> p a b", a=4)
                dn = den[b_]
                es_ = esink[:, 4 * b_:4 * b_ + 4].rearrange("p (a b) -> p a b", b=1)
                P("dve", lambda e, o=dn, a=av[:, :, 64:65], b=es_: e.tensor_tensor(out=o, in0=a, in1=b, op=ALU.add),
                  outs=[dn], ins=[av[:, :, 64:65], esink])
                P("dve", lambda e, o=dn: e.reciprocal(out=o, in_=o), outs=[dn], ins=[dn])
                o_ = ot[:, b_ * 256:(b_ + 1) * 256].rearrange("p (a b) -> p a b", a=4)
                P("dve", lambda e, o=o_, a=av[:, :, 0:64], r=dn: e.tensor_tensor(out=o, in0=a, in1=r.broadcast_to([128, 4, 64]), op=ALU.mult),
                  outs=[o_], ins=[av[:, :, 0:64], dn])
            if qt < 16:
                diff_finish(dstore[:, qt % 4, :, :], ot)
            else:
                blocks = []
                for g in range(8):
                    h = g // 2
                    ch, base = g // 3, (g % 3) * 32
                    kts = [16, 17]
                    acc = psb(6 + g // 4)[:, (g % 4) * 65:(g % 4) * 65 + 65]
                    for j, kt in enumerate(kts):
                        blocks.append((diffKT[base:base + 32, ch, kt * 128:(kt + 1) * 128], diffQT[base:base + 32, ch, qs], None,
                                       diffV[:, kt, h, :], [acc], j == 0, j == len(kts) - 1))
                run_blocks(blocks, float(32 ** -0.5))
                for b_ in range(2):
                    av = psb(6 + b_)[:, 0:260].rearrange("p (a b) -> p a b", a=4)
                    dn = den[b_]
                    P("dve", lambda e, o=dn, a=av[:, :, 64:65]: e.reciprocal(out=o, in_=a), outs=[dn], ins=[av[:, :, 64:65]])
                    tm_ = dctx[:, 4 * b_:4 * b_ + 4, :]
                    P("dve", lambda e, o=tm_, a=av[:, :, 0:64], r=dn: e.tensor_tensor(out=o, in0=a, in1=r.broadcast_to([128, 4, 64]), op=ALU.mult),
                      outs=[tm_], ins=[av[:, :, 0:64], dn])
                diff_finish(dctx, ot)
            _sel = os.environ.get("MK_SEL", "gsd")
            if "g" not in _sel:
                P("pool", lambda e, o=gla_out[:, qt, :]: e.memset(o, 0.0), outs=[gla_out[:, qt, :]])
            if "s" not in _sel:
                P("pool", lambda e, o=ot[:, 0:512]: e.memset(o, 0.0), outs=[ot[:, 0:512]])
            if "d" not in _sel:
                P("pool", lambda e, o=ot[:, 512:768]: e.memset(o, 0.0), outs=[ot[:, 512:768]])
            if qt == int(os.environ.get("MK_DUMP_QT", "3")):
                dump("otok", ot)
            tb_ = psb((state["bank"] % 2) * 2, 1024, BF16).rearrange("p (a b) -> p a b", a=8)
            for c in range(8):
                src = gla_out[:, qt, c * 128:(c + 1) * 128] if c < 2 else ot[:, (c - 2) * 128:(c - 1) * 128]
                P("pe", lambda e, o=tb_[:, c, :], i=src: e.transpose(out=o, in_=i, identity=ident_bf),
                  outs=[tb_[:, c, :]], ins=[src, ident_bf])
            sblk = qt % 4 if qt < 16 else qt - 16
            ob_ = oTblk[:, :, sblk * 128:(sblk + 1) * 128]
            P("act", lambda e, o=ob_, i=tb_: e.copy(out=o, in_=i), outs=[ob_], ins=[tb_])
            last_in_blk = (qt % 4 == 3) if qt < 16 else (qt == 17)
            if not last_in_blk:
                continue
            q0 = (qt // 4) * 512 if qt < 16 else S
            ntok = 512 if qt < 16 else C
            for c in range(8):
                xc = xcs[state["xc"] % 2][:, 0:ntok]
                state["xc"] += 1
                src = xsp_d[:, c, q0:q0 + ntok]
                P("sp", lambda e, o=xc, i=src: e.dma_start(out=o, in_=i), outs=[xc], ins=[src], dma=True)
                yb = psb(state["yb"] % 4, ntok)
                state["yb"] += 1
                for k in range(8):
                    P("pe", lambda e, o=yb, w_=wo_sb[:, k, c * 128:(c + 1) * 128], r=oTblk[:, k, 0:ntok], k=k:
                      e.matmul(o, lhsT=w_, rhs=r, start=(k == 0), stop=(k == 7)),
                      outs=[yb], ins=[wo_sb[:, k, c * 128:(c + 1) * 128], oTblk[:, k, 0:ntok]])
                gh_ = ghTs[l % 2][:, 1, c, which:which + 1]
                P("dve", lambda e, o=xc, y=yb, gh_=gh_: e.scalar_tensor_tensor(
                    out=o, in0=y, scalar=gh_, in1=o, op0=ALU.mult, op1=ALU.add),
                  outs=[xc], ins=[yb, xc, gh_])
                P("sp", lambda e, o=src, i=xc: e.dma_start(out=o, in_=i), outs=[src], ins=[xc], dma=True)
        P("sp", lambda e: e.dma_start(out=xT, in_=xsp_d), outs=[xT], ins=[xsp_d], dma=True)
        A.top = top0

    n_layers = L
    if dbg_stage == "ident":
        n_layers = 0
    for l in range(n_layers):
        LP["p"] = l % 2
        if l == 0:
            compute_mod(l)
        ffn(l, 0, w1i_d, w1o_d, [(0, 1152, [(0, 1152, 0)]), (1152, 1152, [(1152, 896, 0), (2048, 256, 1)])])
        if dbg_stage == "ffn1":
            break
        mixer(l)
        if dbg_stage == "mix":
            break
        last = (l == L - 1)
        if last:
            ffn(l, 2, w2i_d, w2o_d, [(0, 1024, [(0, 1024, 0)]), (1024, 1024, [(1024, 1024, 0)])])
        else:
            ffn(l, 2, w2i_d, w2o_d, [(0, 1152, [(0, 1152, 0)]), (1152, 1152, [(1152, 896, 0), (2048, 256, 1)])])

    tmp_top = A.top
    yT = A.alloc([128, 8, 512], F32, "yT")
    ob = [A.alloc([128, D], F32, "ob%d" % i) for i in range(2)]
    bi = 0
    for g in range(S // 512):
        rms_norm_to_hT(g * 512, 512, 0, 0, final=True, dst=yT)
        for tt in range(4):
            t = g * 4 + tt
            o_sb = ob[t % 2]
            for h in range(2):
                bank = bi % 6
                bi += 1
                pv = psb(bank).rearrange("p (a b) -> p a b", a=4)
                for c4 in range(4):
                    c = h * 4 + c4
                    src = yT[:, c, tt * 128:(tt + 1) * 128]
                    P("pe", lambda e, o=pv[:, c4, :], i=src: e.transpose(out=o, in_=i, identity=ident),
                      outs=[pv[:, c4, :]], ins=[src, ident])
                dst = o_sb[:, h * 512:(h + 1) * 512]
                pvf = psb(bank)
                if h == 0:
                    P("dve", lambda e, o=dst, i=pvf: e.tensor_copy(out=o, in_=i), outs=[dst], ins=[pvf])
                else:
                    P("act", lambda e, o=dst, i=pvf: e.copy(out=o, in_=i), outs=[dst], ins=[pvf])
            dd = out_d[t * 128:(t + 1) * 128, :]
            P("sp", lambda e, o=dd, i=o_sb: e.dma_start(out=o, in_=i), outs=[dd], ins=[o_sb], dma=True)
    A.top = tmp_top
    for e_ in ENGS:
        P(e_, None, ins=[out_d])

    with ExitStack() as st:
        sems = {e: st.enter_context(nc.semaphore("s_" + e)) for e in ENGS}
        dsems = {e: [st.enter_context(nc.semaphore("d_%s%d" % (e, i))) for i in range(NSLOT)] for e in ("sp", "pool", "act")}
        block = st.enter_context(nc.Block())
        sch.emit(nc, block, sems, dsems)
    es.close()
    return nc, sch


def _mix_cols():
    GQ, GK, GV, GF, GB, OG = 0, 128, 256, 512, 528, 544
    SQ, SK, SV = 800, 1312, 1440
    DQ, DK_, DV = 1568, 1824, 2080

    def rng(a, n):
        return list(range(a, a + n))
    perm64 = rng(16, 16) + rng(0, 16) + rng(48, 16) + rng(32, 16)
    perm32 = rng(8, 8) + rng(0, 8) + rng(24, 8) + rng(16, 8)

    def partner64(cl):
        return [cl[h * 64 + perm64[i]] for h in range(2) for i in range(64)]
    fm = [rng(GQ, 128), rng(GK, 128), rng(GF, 16) + rng(GB, 16) + [-1] * 96]
    swaq = [rng(SQ + c * 128, 128) for c in range(4)]
    fm += swaq
    fm += [partner64(c) for c in swaq]
    swak = [rng(SK, 64) * 2, rng(SK + 64, 64) * 2]
    fm += swak
    fm += [partner64(c) for c in swak]

    def dgroups(base):
        return [rng(base + h * 64 + w * 32, 32) for h in range(4) for w in range(2)]

    def dchunks(gr):
        out = []
        for gs in ((0, 1, 2), (3, 4, 5), (6, 7)):
            cc = []
            for g in gs:
                cc += gr[g]
            cc += [-1] * (128 - len(cc))
            out.append(cc)
        return out

    def partner32(gr):
        return [[g[perm32[i]] for i in range(32)] for g in gr]
    dq, dk = dgroups(DQ), dgroups(DK_)
    fm += dchunks(dq) + dchunks(partner32(dq)) + dchunks(dk) + dchunks(partner32(dk))
    assert len(fm) == NFM
    cols = []
    for c in fm:
        assert len(c) == 128
        cols += c
    cols += rng(GV, 256) + rng(OG, 256) + rng(GK, 128) + rng(SV, 128) + rng(DV, 256)
    return np.asarray(cols, np.int64)


def _rope_tables():
    rows = S // 64

    def tables(hd):
        half = hd // 2
        row = np.repeat(np.arange(rows, dtype=np.float32), 64)
        col = np.tile(np.arange(64, dtype=np.float32), rows)
        inv = (1.0 / (np.float32(10000.0) ** (np.arange(0, half, 2, dtype=np.float32) / np.float32(half)))).astype(np.float32)

        def ang(pos):
            a = (pos[:, None] * inv[None, :]).astype(np.float32)
            return np.concatenate([a, a], -1)
        an = np.concatenate([ang(row), ang(col)], -1)
        q = hd // 4
        sign = np.concatenate([-np.ones(q), np.ones(q), -np.ones(q), np.ones(q)]).astype(np.float32)
        return np.cos(an).astype(np.float32).T, (np.sin(an).astype(np.float32) * sign[None, :]).T
    c64, s64 = tables(64)
    c32, s32 = tables(32)
    out = np.zeros((128, 4, S), np.float32)
    out[:, 0] = np.tile(c64, (2, 1))
    out[:, 1] = np.tile(s64, (2, 1))
    out[:, 2] = np.tile(c32, (4, 1))
    out[:, 3] = np.tile(s32, (4, 1))
    return out


def _shared_inputs(inputs):
    f = np.float32
    d = {}
    wm = np.asarray(inputs["w_mod"], dtype=f).reshape(L, 8, 128, 72, 128)
    d["w_mod"] = np.ascontiguousarray(wm.transpose(0, 3, 2, 1, 4))
    bm = np.asarray(inputs["b_mod"], f)
    d["b_modT"] = np.ascontiguousarray(bm.reshape(L, 72, 128).transpose(0, 2, 1))
    gs = [inputs["g_ffn1"][0], inputs["g_mix"][0], inputs["g_ffn2"][0],
          inputs["g_ffn1"][1], inputs["g_mix"][1], inputs["g_ffn2"][1], inputs["g_final"]]
    g = np.stack([np.asarray(v, f) for v in gs], 0)
    d["gT"] = np.ascontiguousarray(g.reshape(7, 8, 128).transpose(2, 0, 1))
    for k in ("w_ffn1_in", "w_ffn1_out", "w_ffn2_in", "w_ffn2_out", "w_out"):
        d[k] = np.ascontiguousarray(inputs[k], dtype=f)
    cols = _mix_cols()
    w_in = np.asarray(inputs["w_in"], f)
    wz = np.concatenate([w_in, np.zeros((L, D, 1), f)], axis=-1)
    d["w_mix"] = np.ascontiguousarray(wz[:, :, cols])
    d["rope"] = _rope_tables()
    pm = np.zeros((128, 2, 128), f)
    perm64 = list(range(16, 32)) + list(range(0, 16)) + list(range(48, 64)) + list(range(32, 48))
    perm32 = list(range(8, 16)) + list(range(0, 8)) + list(range(24, 32)) + list(range(16, 24))
    for m in range(128):
        pm[(m // 64) * 64 + perm64[m % 64], 0, m] = 1.0
        pm[(m // 32) * 32 + perm32[m % 32], 1, m] = 1.0
    d["permT"] = pm
    rm = np.zeros((128, 4), f)
    for p_ in range(128):
        rm[p_, p_ // 32] = 1.0
    d["rmask"] = rm
    sm = np.concatenate([np.asarray(inputs["g_gla_norm"], f), np.asarray(inputs["g_diff_norm"], f),
                         np.asarray(inputs["swa_sink"], f), np.asarray(inputs["diff_lambda"], f).reshape(L, 128)], axis=-1)
    d["small_bc"] = np.ascontiguousarray(np.broadcast_to(sm[:, None, :], (L, 128, 648)))
    wg = np.asarray(inputs["w_gla_gate"], f)
    wup = np.zeros((L, 32, 2, 128), f)
    wup[:, 0:16, 0, :] = wg[:, 0]
    wup[:, 16:32, 1, :] = wg[:, 1]
    d["w_up"] = wup
    d["b_gate"] = np.ascontiguousarray(np.asarray(inputs["b_gla_gate"], f).reshape(L, 1, 256))
    return d


def _prep_inputs(inputs, core, shared):
    f = np.float32
    d = dict(shared)
    d["x"] = np.ascontiguousarray(inputs["x"][core], dtype=f)
    d["ctx"] = np.ascontiguousarray(inputs["ctx"][core], dtype=f)
    cc = np.stack([np.asarray(inputs["c"][core], f), np.asarray(inputs["c_ctx"], f)], axis=-1)
    d["cT"] = np.ascontiguousarray(cc.reshape(8, 128, 2).transpose(1, 0, 2))
    return d


_CACHE = {}


def kernel(**inputs):
    stage = os.environ.get("MK_STAGE", "full")
    if stage not in _CACHE:
        _CACHE[stage] = build(stage)[0]
    nc = _CACHE[stage]
    shared = _shared_inputs(inputs)
    in_maps = [_prep_inputs(inputs, core, shared) for core in range(8)]
    res = run_bass_kernel_spmd(nc, in_maps, core_ids=list(range(8)))
    out = np.stack([np.asarray(r["out"], dtype=np.float32) for r in res.results], axis=0)
    return out
```

```python
import math
import os
import numpy as np
import concourse.bass as bass
import concourse.mybir as mybir
from concourse.bass_utils import run_bass_kernel_spmd

F32 = mybir.dt.float32
BF16 = mybir.dt.bfloat16
ALU = mybir.AluOpType
AF = mybir.ActivationFunctionType
AX = mybir.AxisListType

D = 1024
S = 2048
C = 256
T = S + C
NT = T // 128
DFF = 2816
NJ = DFF // 128
L = 2
EPS = 1e-6

ENGS = ("pe", "act", "dve", "pool", "sp")
NSLOT = 8
NFM = 27
NMIX = NFM * 128 + 1024


def _esz(dt):
    return mybir.dt.size(dt)


class _Op:
    __slots__ = ("eng", "fn", "waits", "signal", "ev", "dma")


class Sched:
    def __init__(self):
        self.ops = {e: [] for e in ENGS}
        self.clock = {e: {} for e in ENGS}
        self.snap = {}
        self.opof = {}
        self.recs = {}
        self.slot_cnt = {e: [0] * NSLOT for e in ENGS}
        self.slot_rr = {e: 0 for e in ENGS}
        self.untracked = set()
        self.GR = 2048

    def boxes(self, ap):
        name = ap.tensor.name
        if name in self.untracked:
            return []
        es = _esz(ap.dtype)
        aps = list(ap.ap)
        off = ap.offset
        if str(ap.space) not in ("SB", "PSUM"):
            ext = sum((c - 1) * abs(s) for s, c in aps)
            return [(name, 0, 1, off * es, (off + ext + 1) * es)]
        pstep, pcnt = aps[0]
        p0 = off // pstep
        f0 = off % pstep
        free = aps[1:]
        out = []

        def rec(base, dims):
            if not dims:
                out.append((name, p0, p0 + pcnt, base * es, (base + 1) * es))
                return
            inner_ext = sum((c - 1) * abs(s) for s, c in dims[1:]) + 1
            s0, c0 = dims[0]
            if len(dims) > 1 and abs(s0) >= inner_ext and 1 < c0 <= 64 and s0 > 0:
                for i in range(c0):
                    rec(base + i * s0, dims[1:])
            else:
                ext = sum((c - 1) * abs(s) for s, c in dims) + 1
                out.append((name, p0, p0 + pcnt, base * es, (base + ext) * es))

        rec(f0, free)
        return out

    def _conf(self, b, kind, deps, eng):
        name, p0, p1, f0, f1 = b
        for g in range(f0 // self.GR, (f1 - 1) // self.GR + 1):
            for r in self.recs.get((name, g), ()):
                rb = r[0]
                if rb[1] < p1 and p0 < rb[2] and rb[3] < f1 and f0 < rb[4]:
                    rk = r[1]
                    if kind == "R" and rk == "R":
                        continue
                    if kind == "X" and rk == "X" and r[3] == eng:
                        continue
                    deps.add(r[2])

    def _reg(self, b, kind, ev, eng, dma):
        name, p0, p1, f0, f1 = b
        for g in range(f0 // self.GR, (f1 - 1) // self.GR + 1):
            lst = self.recs.setdefault((name, g), [])
            g0 = max(f0, g * self.GR)
            g1 = min(f1, (g + 1) * self.GR)
            keep = []
            for r in lst:
                rb = r[0]
                r0 = max(rb[3], g * self.GR)
                r1 = min(rb[4], (g + 1) * self.GR)
                contained = rb[1] >= p0 and rb[2] <= p1 and r0 >= g0 and r1 <= g1
                if contained:
                    if kind == "W":
                        continue
                    if kind == r[1] and r[3] == eng and not dma and not r[4]:
                        continue
                keep.append(r)
            keep.append((b, kind, ev, eng, dma))
            self.recs[(name, g)] = keep

    def add(self, eng, fn, outs=(), ins=(), dma=False):
        op = _Op()
        op.eng, op.fn, op.dma, op.signal = eng, fn, dma, dma
        idx = len(self.ops[eng])
        deps = set()
        acc = []
        for ap in ins:
            for b in self.boxes(ap):
                if b[0] == "ps":
                    acc.append(((b[0], 0, 128, (b[3] // 2048) * 2048, ((b[4] - 1) // 2048 + 1) * 2048), "X"))
                else:
                    acc.append((b, "R"))
        for ap in outs:
            for b in self.boxes(ap):
                if b[0] == "ps":
                    acc.append(((b[0], 0, 128, (b[3] // 2048) * 2048, ((b[4] - 1) // 2048 + 1) * 2048), "W"))
                else:
                    acc.append((b, "W"))
        for b, kind in acc:
            self._conf(b, kind, deps, eng)
        if dma:
            s = self.slot_rr[eng]
            self.slot_rr[eng] = (s + 1) % NSLOT
            k = self.slot_cnt[eng][s]
            if k > 0:
                deps.add(((eng, s), k))
            self.slot_cnt[eng][s] = k + 1
            ev = ((eng, s), k + 1)
        else:
            ev = (eng, idx + 1)
        op.ev = ev
        clk = self.clock[eng]
        waits = []
        for key, val in sorted(deps, key=lambda d: -d[1]):
            if eng == "pe" and key == "pe":
                continue
            if clk.get(key, 0) >= val:
                continue
            waits.append((key, val))
            self.opof[(key, val)].signal = True
            for k2, v2 in self.snap[(key, val)].items():
                if clk.get(k2, 0) < v2:
                    clk[k2] = v2
            clk[key] = max(clk.get(key, 0), val)
        op.waits = waits
        self.snap[ev] = dict(clk)
        self.opof[ev] = op
        for b, kind in acc:
            self._reg(b, kind, ev, eng, dma)
        self.ops[eng].append(op)
        return op

    def emit(self, nc, block, sems, dsems):
        rank = {}
        for e in ENGS:
            n = 0
            for i, op in enumerate(self.ops[e]):
                if op.signal and not op.dma:
                    n += 1
                    rank[(e, i + 1)] = n

        def run(eng_name, eng):
            for op in self.ops[eng_name]:
                for key, val in op.waits:
                    if isinstance(key, tuple):
                        eng.wait_ge(dsems[key[0]][key[1]], 16 * val)
                    else:
                        eng.wait_ge(sems[key], rank[(key, val)])
                if op.fn is None:
                    continue
                inst = op.fn(eng)
                if op.dma:
                    inst.then_inc(dsems[op.ev[0][0]][op.ev[0][1]], 16)
                elif op.signal:
                    inst.then_inc(sems[eng_name], 1)

        @block.tensor
        def _(e):
            run("pe", e)

        @block.scalar
        def _(e):
            run("act", e)

        @block.vector
        def _(e):
            run("dve", e)

        @block.gpsimd
        def _(e):
            run("pool", e)

        @block.sync
        def _(e):
            run("sp", e)


class Arena:
    def __init__(self, t, nbytes, base=0):
        self.t = t
        self.nbytes = base + nbytes
        self.base = base
        self.top = base

    def alloc(self, shape, dt, name=None):
        es = _esz(dt)
        n = 1
        for s in shape[1:]:
            n *= s
        nb = (n * es + 63) // 64 * 64
        off = self.top
        self.top += nb
        assert self.top <= self.nbytes, ("arena overflow", name, self.top)
        v = self.t[0:shape[0], off // 4:(off + nb) // 4]
        if dt != F32:
            v = v.bitcast(dt)
        v = v[:, 0:n]
        if len(shape) == 3:
            v = v.rearrange("p (a b) -> p a b", a=shape[1])
        elif len(shape) == 4:
            v = v.rearrange("p (a b c) -> p a b c", a=shape[1], b=shape[2])
        return v


def build(dbg_stage="full"):
    nc = bass.Bass("TRN2", target_bir_lowering=False)
    dram = {}

    def din(name, shape, dt=F32):
        dram[name] = nc.dram_tensor(name, list(shape), dt, kind="ExternalInput").ap()
        return dram[name]

    x_d = din("x", [S, D])
    ctx_d = din("ctx", [C, D])
    cT_d = din("cT", [128, 8, 2])
    wmod_d = din("w_mod", [L, 72, 128, 8, 128])
    bmodT_d = din("b_modT", [L, 128, 72])
    gT_d = din("gT", [128, 7, 8])
    w1i_d = din("w_ffn1_in", [L, D, 2 * DFF])
    w1o_d = din("w_ffn1_out", [L, DFF, D])
    w2i_d = din("w_ffn2_in", [L, D, 2 * DFF])
    w2o_d = din("w_ffn2_out", [L, DFF, D])
    wmix_d = din("w_mix", [L, D, NMIX])
    wout_d = din("w_out", [L, D, D])
    rope_d = din("rope", [128, 4, S])
    small_d = din("small_bc", [L, 128, 648])
    wup_d = din("w_up", [L, 32, 2, 128])
    bg_d = din("b_gate", [L, 1, 256])
    perm_d = din("permT", [128, 2, 128])
    rmask_d = din("rmask", [128, 4])
    xsp_d = nc.dram_tensor("xspill", [128, 8, T], F32, kind="Internal").ap()
    out_d = nc.dram_tensor("out", [S, D], F32, kind="ExternalOutput").ap()
    DUMP = os.environ.get("MK_DUMP", "")
    if DUMP:
        dbg_d = nc.dram_tensor("dbg", [128, 16384], F32, kind="ExternalOutput").ap()

    def dump(name, ap):
        if not DUMP or name != DUMP:
            return
        shp = list(ap.shape)
        n = 1
        for v_ in shp[1:]:
            n *= v_
        dv = dbg_d[0:shp[0], 0:n]
        if len(shp) == 3:
            dv = dv.rearrange("p (a b) -> p a b", a=shp[1])
        elif len(shp) == 4:
            dv = dv.rearrange("p (a b c) -> p a b c", a=shp[1], b=shp[2])
        sch.add("pool", lambda e, o=dv, i=ap: e.dma_start(out=o, in_=i), outs=[dv], ins=[ap], dma=True)

    sch = Sched()
    for n in ("x", "ctx", "cT", "w_mod", "b_modT", "gT", "w_ffn1_in", "w_ffn1_out", "w_ffn2_in", "w_ffn2_out",
              "w_mix", "w_out", "rope", "small_bc", "w_up", "b_gate", "permT", "rmask"):
        sch.untracked.add(n)

    ARENA_BYTES = 206 * 1024
    from contextlib import ExitStack
    es = ExitStack()
    arena_t = es.enter_context(nc.sbuf_tensor("arena", [128, ARENA_BYTES // 4], F32))
    ps_t = es.enter_context(nc.psum_tensor("ps", [128, 8, 512], F32))
    A = Arena(arena_t, ARENA_BYTES)

    def psb(b, n=512, dt=F32):
        v = ps_t[:, b, :]
        if dt != F32:
            v = v.bitcast(dt)
        return v[:, 0:n]

    xT = A.alloc([128, 8, T], F32, "xT")
    hT = A.alloc([128, 8, T], BF16, "hT")
    ident = A.alloc([128, 128], F32, "ident")
    ones_bf = A.alloc([128, 128], BF16, "ones_bf")
    ones_f = A.alloc([128, 128], F32, "ones_f")
    gT = A.alloc([128, 7, 8], F32, "gT")
    cT = A.alloc([128, 8, 2], F32, "cT")
    scT = A.alloc([128, 8, 2], BF16, "scT")
    modTs = [A.alloc([128, 72, 2], F32, "modT%d" % i) for i in range(2)]
    gsTs = [A.alloc([128, 3, 8, 2], F32, "gsT%d" % i) for i in range(2)]
    ghTs = [A.alloc([128, 3, 8, 2], F32, "ghT%d" % i) for i in range(2)]
    bmTs = [A.alloc([128, 72], F32, "bmT%d" % i) for i in range(2)]
    LP = {"p": 0}
    ident_bf = A.alloc([128, 128], BF16, "ident_bf")
    triA = [A.alloc([128, 128], F32, "triA%d" % i) for i in range(2)]
    triB = [A.alloc([128, 128], F32, "triB%d" % i) for i in range(2)]
    msk = [A.alloc([128, 1, 128], BF16, "msk%d" % i) for i in range(2)]
    bmask = A.alloc([128, 4, 64], F32, "bmask")
    bmask4 = A.alloc([128, 4, 128], BF16, "bmask4")
    PERSIST_TOP = A.top
    m16 = A.alloc([128, 128], F32, "m16")
    ones4 = A.alloc([128, 4, 128], F32, "ones4")
    XT_BYTES = 8 * T * 4
    HT_OFF = XT_BYTES
    HT_BYTES = 8 * T * 2

    P = sch.add

    P("pool", lambda e: e.memset(ident, 0.0), outs=[ident])
    P("pool", lambda e: e.memset(ones_f, 1.0), outs=[ones_f])
    P("pool", lambda e: e.affine_select(ident, ones_f, [[-1, 128]], ALU.is_equal, 0.0, base=0, channel_multiplier=1),
      outs=[ident], ins=[ones_f])
    P("pool", lambda e: e.memset(ones_bf, 1.0), outs=[ones_bf])
    P("dve", lambda e: e.tensor_copy(out=ident_bf, in_=ident), outs=[ident_bf], ins=[ident])
    P("pool", lambda e: e.memset(m16, -1.0 / 16.0), outs=[m16])
    P("pool", lambda e: e.memset(ones4, 1.0), outs=[ones4])
    P("pool", lambda e: e.affine_select(triA[0], m16, [[1, 128]], ALU.is_ge, 0.0, base=0, channel_multiplier=-1),
      outs=[triA[0]], ins=[m16])
    P("pool", lambda e: e.affine_select(triA[1], m16, [[-1, 128]], ALU.is_ge, 0.0, base=0, channel_multiplier=1),
      outs=[triA[1]], ins=[m16])
    P("pool", lambda e: e.affine_select(triB[0], m16, [[-1, 128]], ALU.is_gt, 0.0, base=0, channel_multiplier=1),
      outs=[triB[0]], ins=[m16])
    P("pool", lambda e: e.affine_select(triB[1], m16, [[1, 128]], ALU.is_gt, 0.0, base=0, channel_multiplier=-1),
      outs=[triB[1]], ins=[m16])
    P("pool", lambda e: e.affine_select(msk[0][:, 0, :], ones_f, [[1, 128]], ALU.is_ge, 0.0, base=0, channel_multiplier=-1),
      outs=[msk[0]], ins=[ones_f])
    P("pool", lambda e: e.affine_select(msk[1][:, 0, :], ones_f, [[-1, 128]], ALU.is_ge, 0.0, base=0, channel_multiplier=1),
      outs=[msk[1]], ins=[ones_f])
    tmpm = A.alloc([128, 4, 128], F32, "tmpm")
    P("pool", lambda e: e.affine_select(tmpm, ones4, [[-32, 4], [0, 128]], ALU.is_ge, 0.0, base=0, channel_multiplier=1),
      outs=[tmpm], ins=[ones4])
    P("pool", lambda e: e.affine_select(bmask4, tmpm, [[32, 4], [0, 128]], ALU.is_ge, 0.0, base=31, channel_multiplier=-1),
      outs=[bmask4], ins=[tmpm])
    P("pool", lambda e: e.affine_select(bmask, tmpm[:, :, 0:64], [[32, 4], [0, 64]], ALU.is_ge, 0.0, base=31, channel_multiplier=-1),
      outs=[bmask], ins=[tmpm])
    P("sp", lambda e: e.dma_start(out=gT, in_=gT_d), outs=[gT], dma=True)
    P("sp", lambda e: e.dma_start(out=cT, in_=cT_d), outs=[cT], dma=True)
    sig = A.alloc([128, 8, 2], F32, "sig")
    P("act", lambda e: e.activation(out=sig, in_=cT, func=AF.Silu), outs=[sig], ins=[cT])
    P("dve", lambda e: e.tensor_copy(out=scT, in_=sig), outs=[scT], ins=[sig])

    ldb = [A.alloc([128, D], F32, "ld%d" % i) for i in range(2)]
    for t in range(NT):
        src = x_d[t * 128:(t + 1) * 128, :] if t < 16 else ctx_d[(t - 16) * 128:(t - 15) * 128, :]
        lb = ldb[t % 2]
        P("sp", lambda e, lb=lb, src=src: e.dma_start(out=lb, in_=src), outs=[lb], dma=True)
        for h in range(2):
            bank = (t * 2 + h) % 8
            pv = psb(bank).rearrange("p (a b) -> p a b", a=4)
            for c4 in range(4):
                c = h * 4 + c4
                P("pe", lambda e, o=pv[:, c4, :], i=lb[:, c * 128:(c + 1) * 128]: e.transpose(out=o, in_=i, identity=ident),
                  outs=[pv[:, c4, :]], ins=[lb[:, c * 128:(c + 1) * 128], ident])
            dst = xT[:, h * 4:(h + 1) * 4, t * 128:(t + 1) * 128]
            if (t + h) % 2 == 0:
                P("dve", lambda e, o=dst, i=pv: e.tensor_copy(out=o, in_=i), outs=[dst], ins=[pv])
            else:
                P("act", lambda e, o=dst, i=pv: e.copy(out=o, in_=i), outs=[dst], ins=[pv])

    A.top = PERSIST_TOP
    def norm_bufs():
        return dict(sq=[A.alloc([128, 512], BF16) for _ in range(2)], rs=A.alloc([128, 512], F32),
                    t1=[A.alloc([128, 512], F32) for _ in range(2)])

    def norm_groups(tok0, ntok, gs_idx, which, nb, final=False, dst=None):
        sq, rs, t1b = nb["sq"], nb["rs"], nb["t1"]
        lp = LP["p"]

        def emit_group(g0, n, gi):
            bank = 7 - (gi % 2)
            acc = psb(bank, n)
            for c in range(8):
                s_ = sq[c % 2][:, 0:n]
                src = xT[:, c, g0:g0 + n]
                if c % 2 == 0:
                    P("act", lambda e, o=s_, i=src: e.activation(out=o, in_=i, func=AF.Square), outs=[s_], ins=[src])
                else:
                    P("pool", lambda e, o=s_, i=src: e.tensor_tensor(out=o, in0=i, in1=i, op=ALU.mult), outs=[s_], ins=[src])
                P("pe", lambda e, o=acc, r=s_, c=c: e.matmul(o, lhsT=ones_bf, rhs=r, start=(c == 0), stop=(c == 7)),
                  outs=[acc], ins=[ones_bf, s_])
            r = rs[:, 0:n]
            P("act", lambda e, o=r, i=acc: e.activation(out=o, in_=i, func=AF.Sqrt, bias=EPS, scale=1.0 / D), outs=[r], ins=[acc])
            P("dve", lambda e, o=r: e.reciprocal(out=o, in_=o), outs=[r], ins=[r])
            for c in range(8):
                src = xT[:, c, g0:g0 + n]
                if final:
                    o = dst[:, c, g0 - tok0:g0 - tok0 + n]
                    P("dve", lambda e, o=o, i=src, r=r, c=c: e.scalar_tensor_tensor(
                        out=o, in0=i, scalar=gT[:, 6, c:c + 1], in1=r, op0=ALU.mult, op1=ALU.mult), outs=[o], ins=[src, r, gT])
                else:
                    o = hT[:, c, g0:g0 + n]
                    t1 = t1b[c % 2][:, 0:n]
                    gs_ = gsTs[lp][:, gs_idx, c, which:which + 1]
                    P("dve", lambda e, o=t1, i=src, r=r, gs_=gs_: e.scalar_tensor_tensor(
                        out=o, in0=i, scalar=gs_, in1=r, op0=ALU.mult, op1=ALU.mult),
                      outs=[t1], ins=[src, r, gs_])
                    sh = modTs[lp][:, (3 * gs_idx) * 8 + c, which:which + 1]
                    P("act", lambda e, o=o, i=t1, sh=sh: e.activation(out=o, in_=i, func=AF.Identity, bias=sh, scale=1.0),
                      outs=[o], ins=[t1, sh])

        out = []
        g0 = tok0
        gi = 0
        while g0 < tok0 + ntok:
            n = min(512, tok0 + ntok - g0)
            out.append(lambda g0=g0, n=n, gi=gi: emit_group(g0, n, gi))
            g0 += n
            gi += 1
        return out

    def rms_norm_to_hT(tok0, ntok, gs_idx, which, final=False, dst=None):
        tmp_top = A.top
        nb = norm_bufs()
        for g in norm_groups(tok0, ntok, gs_idx, which, nb, final, dst):
            g()
        A.top = tmp_top

    def mod_emitters(l, wbufs, bank):
        mT, gS, gH, bm = modTs[l % 2], gsTs[l % 2], ghTs[l % 2], bmTs[l % 2]
        acc = psb(bank, 144).rearrange("p (a b) -> p a b", a=72)

        def chunk(fc):
            w = wbufs[fc % len(wbufs)]
            src = wmod_d[l, fc]
            P("pool", lambda e, o=w, i=src: e.dma_start(out=o, in_=i), outs=[w], dma=True)
            for k in range(8):
                P("pe", lambda e, o=acc[:, fc, :], w_=w[:, k, :], r=scT[:, k, :], k=k:
                  e.matmul(o, lhsT=w_, rhs=r, start=(k == 0), stop=(k == 7)),
                  outs=[acc[:, fc, :]], ins=[w[:, k, :], scT[:, k, :]])

        def fin():
            P("sp", lambda e: e.dma_start(out=bm, in_=bmodT_d[l]), outs=[bm], dma=True)
            for w_ in range(2):
                P("dve", lambda e, w_=w_: e.tensor_tensor(out=mT[:, :, w_], in0=acc[:, :, w_], in1=bm, op=ALU.add),
                  outs=[mT[:, :, w_]], ins=[acc[:, :, w_], bm])
            for n in range(3):
                for w_ in range(2):
                    sc = mT[:, (3 * n + 1) * 8:(3 * n + 2) * 8, w_]
                    o = gS[:, n, :, w_]
                    P("dve", lambda e, o=o, sc=sc, n=n: e.scalar_tensor_tensor(
                        out=o, in0=sc, scalar=1.0, in1=gT[:, 3 * l + n, :], op0=ALU.add, op1=ALU.mult),
                      outs=[o], ins=[sc, gT])
                    ga = mT[:, (3 * n + 2) * 8:(3 * n + 3) * 8, w_]
                    o2 = gH[:, n, :, w_]
                    P("dve", lambda e, o=o2, ga=ga, n=n: e.tensor_scalar(
                        out=o, in0=ga, scalar1=(1.0 if n == 1 else 0.5), scalar2=None, op0=ALU.mult),
                      outs=[o2], ins=[ga])

        return [(lambda fc=fc: chunk(fc)) for fc in range(72)] + [fin]

    def compute_mod(l):
        tmp_top = A.top
        wb = [A.alloc([128, 8, 128], BF16) for _ in range(4)]
        for em in mod_emitters(l, wb, 6):
            em()
        A.top = tmp_top

    def ffn(l, n_idx, w_in_d, w_out_d, groups):
        tmp_top = A.top
        aT = A.alloc([128, NJ, 1152], BF16, "aT")
        wi = [A.alloc([128, 8, 256], BF16, "wi%d" % i) for i in range(3)]
        wo = [A.alloc([128, NJ, 128], BF16, "wo%d" % i) for i in range(3)]
        sg = [A.alloc([128, 384], F32, "sg%d" % i) for i in range(2)]
        nbuf = norm_bufs()
        for pi, (p0, pn, segs) in enumerate(groups):
            if pi == 0:
                for (t0, tn, which) in segs:
                    for g in norm_groups(t0, tn, n_idx, which, nbuf):
                        g()
            nxt_norm = []
            if pi + 1 < len(groups):
                for (t0, tn, which) in groups[pi + 1][2]:
                    nxt_norm += norm_groups(t0, tn, n_idx, which, nbuf)
            regs = []
            r0 = p0
            while r0 < p0 + pn:
                rn = min(384, p0 + pn - r0)
                regs.append((r0, rn))
                r0 += rn
            assert len(regs) <= 3
            for j in range(NJ):
                w = wi[j % 3]
                srcg = w_in_d[l, :, j * 128:(j + 1) * 128].rearrange("(kc p) f -> p kc f", p=128)
                srcu = w_in_d[l, :, DFF + j * 128:DFF + (j + 1) * 128].rearrange("(kc p) f -> p kc f", p=128)
                P("pool", lambda e, o=w[:, :, 0:128], i=srcg: e.dma_start(out=o, in_=i), outs=[w[:, :, 0:128]], dma=True)
                P("pool", lambda e, o=w[:, :, 128:256], i=srcu: e.dma_start(out=o, in_=i), outs=[w[:, :, 128:256]], dma=True)
                for ri, (r0, rn) in enumerate(regs):
                    pg = psb(ri, rn)
                    pu = psb(3 + ri, rn)
                    for k in range(8):
                        P("pe", lambda e, o=pg, w_=w[:, k, 0:128], r=hT[:, k, r0:r0 + rn], k=k:
                          e.matmul(o, lhsT=w_, rhs=r, start=(k == 0), stop=(k == 7)),
                          outs=[pg], ins=[w[:, k, 0:128], hT[:, k, r0:r0 + rn]])
                    for k in range(8):
                        P("pe", lambda e, o=pu, w_=w[:, k, 128:256], r=hT[:, k, r0:r0 + rn], k=k:
                          e.matmul(o, lhsT=w_, rhs=r, start=(k == 0), stop=(k == 7)),
                          outs=[pu], ins=[w[:, k, 128:256], hT[:, k, r0:r0 + rn]])
                    s = sg[(j * 3 + ri) % 2][:, 0:rn]
                    P("act", lambda e, o=s, i=pg: e.activation(out=o, in_=i, func=AF.Silu), outs=[s], ins=[pg])
                    o = aT[:, j, r0 - p0:r0 - p0 + rn]
                    P("dve", lambda e, o=o, a=pu, b=s: e.tensor_tensor(out=o, in0=a, in1=b, op=ALU.mult), outs=[o], ins=[pu, s])
            bi = 0
            for c in range(8):
                w = wo[c % 3]
                src = w_out_d[l, :, c * 128:(c + 1) * 128].rearrange("(j p) f -> p j f", p=128)
                P("pool", lambda e, o=w, i=src: e.dma_start(out=o, in_=i), outs=[w], dma=True)
                for (r0, rn) in regs:
                    py = psb(bi % 6, rn)
                    bi += 1
                    for j in range(NJ):
                        P("pe", lambda e, o=py, w_=w[:, j, :], r=aT[:, j, r0 - p0:r0 - p0 + rn], j=j:
                          e.matmul(o, lhsT=w_, rhs=r, start=(j == 0), stop=(j == NJ - 1)),
                          outs=[py], ins=[w[:, j, :], aT[:, j, r0 - p0:r0 - p0 + rn]])
                    for (t0, tn, which) in segs:
                        a0 = max(t0, r0)
                        a1 = min(t0 + tn, r0 + rn)
                        if a1 <= a0:
                            continue
                        xs = xT[:, c, a0:a1]
                        ys = py[:, a0 - r0:a1 - r0]
                        gh_ = ghTs[LP["p"]][:, n_idx, c, which:which + 1]
                        P("dve", lambda e, xs=xs, ys=ys, gh_=gh_: e.scalar_tensor_tensor(
                            out=xs, in0=ys, scalar=gh_, in1=xs, op0=ALU.mult, op1=ALU.add),
                          outs=[xs], ins=[ys, xs, gh_])
                if nxt_norm and c % 2 == 1:
                    nxt_norm.pop(0)()
            while nxt_norm:
                nxt_norm.pop(0)()
        A.top = tmp_top

    def mixer(l):
        ctx_out = (l < L - 1)
        lam_init = 0.8 - 0.6 * math.exp(-0.3 * l)
        top0 = A.top
        out_tiles = list(range(NT)) if ctx_out else list(range(16))
        tmp_top_n = A.top
        nb_ = norm_bufs()
        for (t0_, tn_, wh_) in ((0, S, 0), (S, C, 1)):
            g0_ = t0_
            for gfn in norm_groups(t0_, tn_, 1, wh_, nb_):
                gfn()
                n_ = min(512, t0_ + tn_ - g0_)
                P("sp", lambda e, o=xsp_d[:, :, g0_:g0_ + n_], i=xT[:, :, g0_:g0_ + n_]: e.dma_start(out=o, in_=i),
                  outs=[xsp_d[:, :, g0_:g0_ + n_]], ins=[xT[:, :, g0_:g0_ + n_]], dma=True)
                g0_ += n_
        A.top = tmp_top_n
        AXa = Arena(arena_t, XT_BYTES, 0)
        AH = Arena(arena_t, HT_BYTES, HT_OFF)
        glaQT = AXa.alloc([128, 1, T], BF16)
        glaKT = AXa.alloc([128, 1, T], BF16)
        gfbT = AXa.alloc([128, 1, T], BF16)
        swaQT = AXa.alloc([128, 4, T], BF16)
        swaKT = AXa.alloc([128, 2, T], BF16)
        diffQT = AXa.alloc([128, 3, T], BF16)
        diffKT = AXa.alloc([128, 3, T], BF16)
        glaTM = A.alloc([128, NT, 640], BF16)
        swaV = A.alloc([128, NT, 2, 65], BF16)
        diffV = A.alloc([128, NT, 4, 65], BF16)
        small = A.alloc([128, 648], F32)
        wup = A.alloc([32, 2, 128], BF16)
        bg = A.alloc([1, 256], BF16)
        esink = A.alloc([128, 8], F32)
        nlam = A.alloc([128, 1], F32)
        gdiff = A.alloc([128, 4, 64], F32)
        lamt = A.alloc([128, 2, 32], F32)
        lams = A.alloc([128, 2], F32)
        mix_top = A.top
        ropeT = A.alloc([128, 4, S], BF16)
        P("pool", lambda e: e.dma_start(out=ropeT, in_=rope_d), outs=[ropeT], dma=True)
        P("sp", lambda e: e.dma_start(out=small, in_=small_d[l]), outs=[small], dma=True)
        P("pool", lambda e: e.dma_start(out=wup, in_=wup_d[l]), outs=[wup], dma=True)
        P("pool", lambda e: e.dma_start(out=bg, in_=bg_d[l]), outs=[bg], dma=True)
        P("pool", lambda e: e.memset(swaV[:, :, :, 64:65], 1.0), outs=[swaV[:, :, :, 64:65]])
        P("pool", lambda e: e.memset(diffV[:, :, :, 64:65], 1.0), outs=[diffV[:, :, :, 64:65]])
        P("act", lambda e: e.activation(out=esink, in_=small[:, 512:520], func=AF.Exp), outs=[esink], ins=[small])
        dl = small[:, 520:648].rearrange("p (a b) -> p a b", a=4)
        P("dve", lambda e: e.tensor_tensor(out=lamt[:, 0, :], in0=dl[:, 0, :], in1=dl[:, 1, :], op=ALU.mult),
          outs=[lamt[:, 0, :]], ins=[small])
        P("dve", lambda e: e.tensor_tensor(out=lamt[:, 1, :], in0=dl[:, 2, :], in1=dl[:, 3, :], op=ALU.mult),
          outs=[lamt[:, 1, :]], ins=[small])
        P("dve", lambda e: e.reduce_sum(out=lams, in_=lamt, axis=AX.X), outs=[lams], ins=[lamt])
        P("act", lambda e: e.activation(out=lams, in_=lams, func=AF.Exp), outs=[lams], ins=[lams])
        P("dve", lambda e: e.scalar_tensor_tensor(out=nlam, in0=lams[:, 1:2], scalar=-lam_init, in1=lams[:, 0:1],
                                                  op0=ALU.add, op1=ALU.subtract), outs=[nlam], ins=[lams])
        P("dve", lambda e: e.tensor_scalar(out=gdiff, in0=small[:, 256:512].rearrange("p (a b) -> p a b", a=4),
                                           scalar1=1.0 - lam_init, scalar2=None, op0=ALU.mult), outs=[gdiff], ins=[small])

        wfm = [A.alloc([128, 2, 8, 128], BF16) for _ in range(2)]
        permT = A.alloc([128, 2, 128], BF16)
        xb = [A.alloc([128, 512], BF16) for _ in range(2)]
        P("pool", lambda e: e.dma_start(out=permT, in_=perm_d), outs=[permT], dma=True)
        wtm = [A.alloc([128, 8, 512], BF16) for _ in range(2)]
        rt = [A.alloc([128, 512], F32) for _ in range(2)]
        fm = [(0, None, glaQT[:, 0, :], None), (1, None, glaKT[:, 0, :], None), (2, None, gfbT[:, 0, :], None)]
        for c in range(4):
            fm.append((3 + c, 7 + c, swaQT[:, c, :], 0))
        for c in range(2):
            fm.append((11 + c, 13 + c, swaKT[:, c, :], 0))
        for c in range(3):
            fm.append((15 + c, 18 + c, diffQT[:, c, :], 2))
        for c in range(3):
            fm.append((21 + c, 24 + c, diffKT[:, c, :], 2))
        groups = [(0, 512), (512, 512), (1024, 512), (1536, 512), (2048, 256)]
        gi = 0
        def load_fm(fi):
            cm, cp, dest, tb = fm[fi]
            w = wfm[fi % 2]
            src = wmix_d[l, :, cm * 128:(cm + 1) * 128].rearrange("(kc p) f -> p kc f", p=128)
            P("pool", lambda e, o=w[:, 0, :, :], i=src: e.dma_start(out=o, in_=i), outs=[w[:, 0, :, :]], dma=True)

        load_fm(0)
        for g in range(2):
            src = wmix_d[l, :, NFM * 128 + g * 512:NFM * 128 + (g + 1) * 512].rearrange("(kc p) f -> p kc f", p=128)
            P("pool", lambda e, o=wtm[g], i=src: e.dma_start(out=o, in_=i), outs=[wtm[g]], dma=True)
        for fi, (cm, cp, dest, tb) in enumerate(fm):
            w = wfm[fi % 2]
            if fi + 1 < len(fm):
                load_fm(fi + 1)
            for (g0, n) in groups:
                b0 = (2 * gi) % 4
                gi += 1
                pm = psb(b0, n)
                pp = psb(b0 + 1, n)
                rope = (cp is not None) and g0 < S
                for k in range(8):
                    P("pe", lambda e, o=pm, w_=w[:, 0, k, :], r=hT[:, k, g0:g0 + n], k=k:
                      e.matmul(o, lhsT=w_, rhs=r, start=(k == 0), stop=(k == 7)),
                      outs=[pm], ins=[w[:, 0, k, :], hT[:, k, g0:g0 + n]])
                if rope:
                    xb_ = xb[gi % 2][:, 0:n]
                    P("act", lambda e, o=xb_, i=pm: e.copy(out=o, in_=i), outs=[xb_], ins=[pm])
                    pmx = permT[:, (0 if tb == 0 else 1), :]
                    P("pe", lambda e, o=pp, w_=pmx, r=xb_: e.matmul(o, lhsT=w_, rhs=r, start=True, stop=True),
                      outs=[pp], ins=[pmx, xb_])
                    t1 = rt[0][:, 0:n]
                    t2 = rt[1][:, 0:n]
                    cs = ropeT[:, tb, g0:g0 + n]
                    sn = ropeT[:, tb + 1, g0:g0 + n]
                    P("dve", lambda e, o=t1, a=pm, b=cs: e.tensor_tensor(out=o, in0=a, in1=b, op=ALU.mult), outs=[t1], ins=[pm, cs])
                    P("dve", lambda e, o=t2, a=pp, b=sn: e.tensor_tensor(out=o, in0=a, in1=b, op=ALU.mult), outs=[t2], ins=[pp, sn])
                    d_ = dest[:, g0:g0 + n]
                    P("pool", lambda e, o=d_, a=t1, b=t2: e.tensor_tensor(out=o, in0=a, in1=b, op=ALU.add), outs=[d_], ins=[t1, t2])
                else:
                    d_ = dest[:, g0:g0 + n]
                    P("act", lambda e, o=d_, i=pm: e.copy(out=o, in_=i), outs=[d_], ins=[pm])
        for t in range(NT):
            for g in range(2):
                pb = psb(4 + (t * 2 + g) % 4)
                for k in range(8):
                    P("pe", lambda e, o=pb, a=hT[:, k, t * 128:(t + 1) * 128], w_=wtm[g][:, k, :], k=k:
                      e.matmul(o, lhsT=a, rhs=w_, start=(k == 0), stop=(k == 7)),
                      outs=[pb], ins=[hT[:, k, t * 128:(t + 1) * 128], wtm[g][:, k, :]])
                if g == 0:
                    d_ = glaTM[:, t, 0:512]
                    P("act", lambda e, o=d_, i=pb: e.copy(out=o, in_=i), outs=[d_], ins=[pb])
                else:
                    d_ = glaTM[:, t, 512:640]
                    P("dve", lambda e, o=d_, i=pb[:, 0:128]: e.tensor_copy(out=o, in_=i), outs=[d_], ins=[pb[:, 0:128]])
                    d2 = swaV[:, t, :, 0:64]
                    s2 = pb[:, 128:256].rearrange("p (a b) -> p a b", a=2)
                    P("act", lambda e, o=d2, i=s2: e.copy(out=o, in_=i), outs=[d2], ins=[s2])
                    d3 = diffV[:, t, :, 0:64]
                    s3 = pb[:, 256:512].rearrange("p (a b) -> p a b", a=4)
                    P("dve", lambda e, o=d3, i=s3: e.tensor_copy(out=o, in_=i), outs=[d3], ins=[s3])
        A.top = mix_top
        for nm_, ap_ in (("glaQT", glaQT), ("glaKT", glaKT), ("gfbT", gfbT), ("swaQT", swaQT), ("swaKT", swaKT),
                         ("diffQT", diffQT), ("diffKT", diffKT), ("glaTM", glaTM), ("swaV", swaV), ("diffV", diffV), ("hT", hT)):
            dump(nm_, ap_)

        _cut = os.environ.get("MK_MIXCUT", "")
        if _cut == "inproj":
            P("sp", lambda e: e.dma_start(out=xT, in_=xsp_d), outs=[xT], ins=[xsp_d], dma=True)
            A.top = top0
            return
        esp = AH.alloc([128, 2, NT, 128], F32)
        ost = AH.alloc([128, NT, 256], F32)
        gla_out = A.alloc([128, NT, 256], BF16)
        gla_top = A.top
        bi = 0
        for d_ in range(2):
            for t0 in range(0, NT, 4):
                nb = min(4, NT - t0)
                pb = psb(bi % 4).rearrange("p (a b) -> p a b", a=4)
                bi += 1
                for s_ in range(nb):
                    t = t0 + s_
                    P("pe", lambda e, o=pb[:, s_, :], a=gfbT[0:32, 0, t * 128:(t + 1) * 128], w_=wup[:, d_, :]:
                      e.matmul(o, lhsT=a, rhs=w_, start=True, stop=False),
                      outs=[pb[:, s_, :]], ins=[gfbT[0:32, 0, t * 128:(t + 1) * 128], wup[:, d_, :]])
                    P("pe", lambda e, o=pb[:, s_, :], a=ones_bf[0:1, :], w_=bg[0:1, d_ * 128:(d_ + 1) * 128]:
                      e.matmul(o, lhsT=a, rhs=w_, start=False, stop=True),
                      outs=[pb[:, s_, :]], ins=[ones_bf[0:1, :], bg[0:1, d_ * 128:(d_ + 1) * 128]])
                o_ = esp[:, d_, t0:t0 + nb, :]
                P("act", lambda e, o=o_, i=pb[:, 0:nb, :]: e.activation(out=o, in_=i, func=AF.Exp, scale=-1.0),
                  outs=[o_], ins=[pb[:, 0:nb, :]])
        for d_ in range(2):
            P("act", lambda e, o=esp[:, d_, :, :]: e.activation(out=o, in_=o, func=AF.Ln, bias=1.0, scale=1.0),
              outs=[esp[:, d_, :, :]], ins=[esp[:, d_, :, :]])
        E1 = [A.alloc([128, 128], F32) for _ in range(2)]
        E2 = [A.alloc([128, 128], F32) for _ in range(2)]
        E3 = [A.alloc([128, 128], F32) for _ in range(2)]
        KtT = [A.alloc([128, 128], BF16) for _ in range(2)]
        Kh = [A.alloc([128, 128], BF16) for _ in range(2)]
        Qexp = [A.alloc([128, 4, 128], BF16) for _ in range(2)]
        QtT = [[A.alloc([128, 1, 128], BF16) for _ in range(2)] for _ in range(2)]
        attm = [[A.alloc([128, 4, 128], BF16) for _ in range(2)] for _ in range(2)]
        Um = [[A.alloc([128, 4, 64], F32) for _ in range(2)] for _ in range(2)]
        decs = [[A.alloc([128, 1], F32) for _ in range(2)] for _ in range(2)]
        Sf = [A.alloc([128, 4, 64], F32) for _ in range(2)]
        Sb = [[A.alloc([128, 4, 64], BF16) for _ in range(2)] for _ in range(2)]
        orders = [[16, 17] + list(range(16)), [17, 16] + list(range(15, -1, -1))]
        visited = set()
        for d_ in range(2):
            P("pool", lambda e, d_=d_: e.memset(Sf[d_], 0.0), outs=[Sf[d_]])
            P("pool", lambda e, d_=d_: e.memset(Sb[d_][0], 0.0), outs=[Sb[d_][0]])

        def gla_prep(st, d_):
            t = orders[d_][st]
            par = st % 2
            sp_t = esp[:, d_, t, :]
            bA = psb(3 * d_)
            pc, pr, pu = bA[:, 0:128], bA[:, 128:256], bA[:, 256:512]
            P("pe", lambda e, o=pc, a=sp_t, b=triA[d_]: e.matmul(o, lhsT=a, rhs=b, start=True, stop=True),
              outs=[pc], ins=[sp_t, triA[d_]])
            P("pe", lambda e, o=pr, a=triB[d_], b=sp_t: e.matmul(o, lhsT=a, rhs=b, start=True, stop=True),
              outs=[pr], ins=[triB[d_], sp_t])
            e1, e2, e3 = E1[d_], E2[d_], E3[d_]
            P("act", lambda e, o=e3, i=pr: e.activation(out=o, in_=i, func=AF.Exp), outs=[e3], ins=[pr])
            P("act", lambda e, o=e1, i=pc: e.activation(out=o, in_=i, func=AF.Exp), outs=[e1], ins=[pc])
            if t in out_tiles:
                P("act", lambda e, o=e2, i=pc: e.activation(out=o, in_=i, func=AF.Exp, scale=-1.0), outs=[e2], ins=[pc])
            kh = Kh[d_]
            P("pool", lambda e, o=kh, a=glaTM[:, t, 512:640], b=e3: e.tensor_tensor(out=o, in0=a, in1=b, op=ALU.mult),
              outs=[kh], ins=[glaTM[:, t, 512:640], e3])
            P("pe", lambda e, o=pu, a=kh, b=glaTM[:, t, 0:256]: e.matmul(o, lhsT=a, rhs=b, start=True, stop=True),
              outs=[pu], ins=[kh, glaTM[:, t, 0:256]])
            dsrc = e1[:, 127:128] if d_ == 0 else e1[:, 0:1]
            dc = decs[d_][par]
            P("pool", lambda e, o=dc, i=dsrc: e.tensor_copy(out=o, in_=i), outs=[dc], ins=[dsrc])
            um = Um[d_][par]
            P("dve", lambda e, o=um, a=pu.rearrange("p (a b) -> p a b", a=4): e.tensor_tensor(out=o, in0=a, in1=bmask, op=ALU.mult),
              outs=[um], ins=[pu, bmask])
            if t in out_tiles:
                qt_, kt_ = QtT[d_][par], KtT[d_]
                tq = glaQT[:, :, t * 128:(t + 1) * 128]
                P("dve", lambda e, o=qt_, a=tq, b=e1: e.scalar_tensor_tensor(
                    out=o[:, 0, :], in0=a[:, 0, :], scalar=float(32 ** -0.5), in1=b, op0=ALU.mult, op1=ALU.mult),
                  outs=[qt_], ins=[tq, e1])
                tk = glaKT[:, 0, t * 128:(t + 1) * 128]
                P("pool", lambda e, o=kt_, a=tk, b=e2: e.tensor_tensor(out=o, in0=a, in1=b, op=ALU.mult), outs=[kt_], ins=[tk, e2])
                qe = Qexp[d_]
                P("pool", lambda e, o=qe, a=qt_: e.tensor_tensor(out=o, in0=bmask4, in1=a.broadcast_to([128, 4, 128]), op=ALU.mult),
                  outs=[qe], ins=[bmask4, qt_])
                pa = psb(3 * d_ + 1)
                P("pe", lambda e, o=pa, a=kt_, b=qe: e.matmul(o, lhsT=a, rhs=b.rearrange("p a b -> p (a b)"), start=True, stop=True),
                  outs=[pa], ins=[kt_, qe])
                am = attm[d_][par]
                P("dve", lambda e, o=am, a=pa.rearrange("p (a b) -> p a b", a=4), m_=msk[d_]:
                  e.tensor_tensor(out=o, in0=a, in1=m_.broadcast_to([128, 4, 128]), op=ALU.mult), outs=[am], ins=[pa, msk[d_]])

        def gla_chain(st, d_):
            t = orders[d_][st]
            par = st % 2
            cur, nxt = st % 2, (st + 1) % 2
            sb_cur, sb_nxt = Sb[d_][cur], Sb[d_][nxt]
            if t in out_tiles:
                qt_, am = QtT[d_][par], attm[d_][par]
                po = psb(3 * d_ + 2, 256)
                for h in range(4):
                    oh = po[:, h * 64:(h + 1) * 64]
                    P("pe", lambda e, o=oh, a=qt_[:, 0, :], b=sb_cur[:, h, :]: e.matmul(o, lhsT=a, rhs=b, start=True, stop=False),
                      outs=[oh], ins=[qt_, sb_cur[:, h, :]])
                    P("pe", lambda e, o=oh, a=am[:, h, :], b=glaTM[:, t, h * 64:(h + 1) * 64]: e.matmul(o, lhsT=a, rhs=b, start=False, stop=True),
                      outs=[oh], ins=[am[:, h, :], glaTM[:, t, h * 64:(h + 1) * 64]])
                if t not in visited:
                    visited.add(t)
                    P("act", lambda e, o=ost[:, t, :], i=po: e.copy(out=o, in_=i), outs=[ost[:, t, :]], ins=[po])
                else:
                    P("dve", lambda e, o=ost[:, t, :], i=po: e.tensor_tensor(out=o, in0=i, in1=o, op=ALU.add),
                      outs=[ost[:, t, :]], ins=[po, ost[:, t, :]])
            P("dve", lambda e, o=Sf[d_], sc=decs[d_][par], b=Um[d_][par]: e.scalar_tensor_tensor(
                out=o, in0=o, scalar=sc, in1=b, op0=ALU.mult, op1=ALU.add), outs=[Sf[d_]], ins=[Sf[d_], decs[d_][par], Um[d_][par]])
            P("act", lambda e, o=sb_nxt, i=Sf[d_]: e.copy(out=o, in_=i), outs=[sb_nxt], ins=[Sf[d_]])

        mod_ems = []
        if l + 1 < L and dbg_stage == "full":
            wb2 = [A.alloc([128, 8, 128], BF16) for _ in range(2)]
            mod_ems = mod_emitters(l + 1, wb2, 6)
        for st in range(NT + 1):
            for d_ in range(2):
                if st < NT:
                    gla_prep(st, d_)
            for d_ in range(2):
                if st >= 1:
                    gla_chain(st - 1, d_)
            for _ in range(4):
                if len(mod_ems) > 1:
                    mod_ems.pop(0)()
        while mod_ems:
            mod_ems.pop(0)()
        A.top = gla_top
        no = len(out_tiles)
        ssg = A.alloc([128, NT, 4, 1], F32)
        sqg = [A.alloc([128, 4, 64], F32) for _ in range(2)]
        sgl = [A.alloc([128, 256], F32) for _ in range(2)]
        for t in out_tiles:
            o4 = ost[:, t, :].rearrange("p (a b) -> p a b", a=4)
            q_ = sqg[t % 2]
            P("pool", lambda e, o=q_, a=o4: e.tensor_tensor(out=o, in0=a, in1=a, op=ALU.mult), outs=[q_], ins=[o4])
            P("dve", lambda e, o=ssg[:, t, :, 0], i=q_: e.reduce_sum(out=o, in_=i, axis=AX.X), outs=[ssg[:, t, :, :]], ins=[q_])
        sv = ssg[:, 0:no, :, :]
        P("act", lambda e, o=sv: e.activation(out=o, in_=o, func=AF.Sqrt, bias=EPS, scale=1.0 / 64.0), outs=[sv], ins=[sv])
        P("dve", lambda e, o=sv: e.reciprocal(out=o, in_=o), outs=[sv], ins=[sv])
        ggla = small[:, 0:256].rearrange("p (a b) -> p a b", a=4)
        for t in out_tiles:
            o4 = ost[:, t, :].rearrange("p (a b) -> p a b", a=4)
            q_ = sqg[t % 2]
            sg_ = sgl[t % 2]
            P("act", lambda e, o=sg_, i=glaTM[:, t, 256:512]: e.activation(out=o, in_=i, func=AF.Silu), outs=[sg_], ins=[glaTM[:, t, 256:512]])
            P("dve", lambda e, o=q_, a=o4, r=ssg[:, t, :, :]: e.tensor_tensor(out=o, in0=a, in1=r.broadcast_to([128, 4, 64]), op=ALU.mult),
              outs=[q_], ins=[o4, ssg[:, t, :, :]])
            P("pool", lambda e, o=q_: e.tensor_tensor(out=o, in0=o, in1=ggla, op=ALU.mult), outs=[q_], ins=[q_, small])
            go = gla_out[:, t, :]
            P("dve", lambda e, o=go, a=q_, b=sg_: e.tensor_tensor(out=o, in0=a.rearrange("p a b -> p (a b)"), in1=b, op=ALU.mult),
              outs=[go], ins=[q_, sg_])
        A.top = gla_top

        dump("gla_out", gla_out)
        dump("ost", ost)
        if _cut == "gla":
            P("sp", lambda e: e.dma_start(out=xT, in_=xsp_d), outs=[xT], ins=[xsp_d, gla_out], dma=True)
            A.top = top0
            return
        AH2 = Arena(arena_t, HT_BYTES, HT_OFF)
        wo_sb = AH2.alloc([128, 8, D], BF16)
        P("pool", lambda e: e.dma_start(out=wo_sb, in_=wout_d[l].rearrange("(kc p) f -> p kc f", p=128)), outs=[wo_sb], dma=True)
        pT = [AH2.alloc([128, 8, 128], BF16) for _ in range(3)]
        otok = [A.alloc([128, 768], BF16) for _ in range(2)]
        dstore = AH2.alloc([128, 4, 8, 64], F32)
        rmask = A.alloc([128, 4], F32)
        qmb = [A.alloc([128, 512], BF16) for _ in range(2)]
        P("sp", lambda e: e.dma_start(out=rmask, in_=rmask_d), outs=[rmask], dma=True)
        oTblk = A.alloc([128, 8, 512], BF16)
        xcs = [A.alloc([128, 512], F32) for _ in range(2)]
        den = [AH2.alloc([128, 4, 1], F32) for _ in range(2)]
        dctx = AH2.alloc([128, 8, 64], F32)
        od = AH2.alloc([128, 4, 64], F32)
        od2 = AH2.alloc([128, 4, 64], F32)
        ssd = AH2.alloc([128, 4, 1], F32)
        state = {"bank": 0, "pt": 0, "xc": 0, "yb": 0}
        dump("wo_sb", wo_sb[:, :, 0:2048] if False else wo_sb)

        def run_blocks(blocks, scale):
            batches = []
            for blk in blocks:
                rg = blk[0].base_partition()
                w = blk[1].shape[-1]
                if (batches and batches[-1][0][0].base_partition() == rg and batches[-1][0][1].shape[-1] == w
                        and (len(batches[-1]) + 1) * w <= 1024):
                    batches[-1].append(blk)
                else:
                    batches.append([blk])
            pend = None
            for bl in batches + [None]:
                cur = None
                if bl is not None:
                    w = bl[0][1].shape[-1]
                    ncol = len(bl) * w
                    b2 = (state["bank"] % 2) * 2
                    state["bank"] += 1
                    bank = ps_t[:, b2:b2 + 2, :].rearrange("p a c -> p (a c)")
                    p_ = pT[state["pt"] % 3].rearrange("p a c -> p (a c)")
                    state["pt"] += 1
                    for i, (kT, qT, m_, v, accs, first, last) in enumerate(bl):
                        P("pe", lambda e, o=bank[:, i * w:(i + 1) * w], a=kT, b=qT: e.matmul(o, lhsT=a, rhs=b, start=True, stop=True),
                          outs=[bank[:, i * w:(i + 1) * w]], ins=[kT, qT])
                    P("act", lambda e, o=p_[:, 0:ncol], i=bank[:, 0:ncol]: e.activation(out=o, in_=i, func=AF.Exp, scale=scale),
                      outs=[p_[:, 0:ncol]], ins=[bank[:, 0:ncol]])
                    for i, (kT, qT, m_, v, accs, first, last) in enumerate(bl):
                        if m_ is not None:
                            P("pool", lambda e, o=p_[:, i * w:(i + 1) * w], m_=m_: e.tensor_tensor(out=o, in0=o, in1=m_[:, 0, :], op=ALU.mult),
                              outs=[p_[:, i * w:(i + 1) * w]], ins=[p_[:, i * w:(i + 1) * w], m_])
                    cur = (bl, p_, w)
                if pend is not None:
                    pbl, pp_, pw = pend
                    for i, (kT, qT, m_, v, accs, first, last) in enumerate(pbl):
                        for su, acc in enumerate(accs):
                            lh = pp_[:, i * pw + su * 128:i * pw + (su + 1) * 128]
                            st_ = first and su == 0
                            P("pe", lambda e, o=acc, a=lh, b=v, st_=st_, last=last: e.matmul(
                                o, lhsT=a, rhs=b, start=st_, stop=last, skip_group_check=(len(accs) > 1)),
                              outs=[acc], ins=[lh, v])
                pend = cur

        def diff_finish(src4, ot):
            t4 = src4.rearrange("p (h w) d -> p h w d", w=2)
            P("dve", lambda e, a=t4[:, :, 1, :], b=t4[:, :, 0, :]: e.scalar_tensor_tensor(
                out=od, in0=a, scalar=nlam[:, 0:1], in1=b, op0=ALU.mult, op1=ALU.add), outs=[od], ins=[src4, nlam])
            P("pool", lambda e: e.tensor_tensor(out=od2, in0=od, in1=od, op=ALU.mult), outs=[od2], ins=[od])
            P("dve", lambda e: e.reduce_sum(out=ssd[:, :, 0], in_=od2, axis=AX.X), outs=[ssd], ins=[od2])
            P("act", lambda e: e.activation(out=ssd, in_=ssd, func=AF.Sqrt, bias=EPS, scale=1.0 / 64.0), outs=[ssd], ins=[ssd])
            P("dve", lambda e: e.reciprocal(out=ssd, in_=ssd), outs=[ssd], ins=[ssd])
            P("dve", lambda e: e.tensor_tensor(out=od2, in0=od, in1=ssd.broadcast_to([128, 4, 64]), op=ALU.mult), outs=[od2], ins=[od, ssd])
            o_ = ot[:, 512:768].rearrange("p (a b) -> p a b", a=4)
            P("pool", lambda e, o=o_: e.tensor_tensor(out=o, in0=od2, in1=gdiff, op=ALU.mult), outs=[o_], ins=[od2, gdiff])

        def diff_qblock(qb):
            q0 = qb * 512
            for g in range(8):
                h = g // 2
                ch, base = g // 3, (g % 3) * 32
                accb = psb(6 + g % 2)[:, 0:260].rearrange("p (a b) -> p a b", a=4)
                qm = qmb[g % 2]
                P("dve", lambda e, o=qm, a=diffQT[:, ch, q0:q0 + 512], m_=rmask[:, (g % 3):(g % 3) + 1]: e.tensor_scalar(
                    out=o, in0=a, scalar1=m_, scalar2=None, op0=ALU.mult), outs=[qm], ins=[diffQT[:, ch, q0:q0 + 512], rmask])
                blocks = []
                for kt in range(NT):
                    blocks.append((diffKT[:, ch, kt * 128:(kt + 1) * 128], qm, None,
                                   diffV[:, kt, h, :], [accb[:, su, :] for su in range(4)], kt == 0, kt == NT - 1))
                run_blocks(blocks, float(32 ** -0.5))
                dn = den[g % 2]
                P("dve", lambda e, o=dn, a=accb[:, :, 64:65]: e.reciprocal(out=o, in_=a), outs=[dn], ins=[accb[:, :, 64:65]])
                P("dve", lambda e, o=dstore[:, :, g, :], a=accb[:, :, 0:64], r=dn: e.tensor_tensor(
                    out=o, in0=a, in1=r.broadcast_to([128, 4, 64]), op=ALU.mult), outs=[dstore[:, :, g, :]], ins=[accb[:, :, 0:64], dn])

        for qi, qt in enumerate(out_tiles):
            which = 0 if qt < 16 else 1
            ot = otok[qi % 2]
            qs = slice(qt * 128, (qt + 1) * 128)
            if qt < 16 and qt % 4 == 0:
                diff_qblock(qt // 4)
            blocks = []
            for h in range(8):
                kg, kq, base = h // 4, h // 2, (h % 2) * 64
                if qt < 16:
                    kts = [(kt, (None if kt == qt else (msk[1] if kt < qt else msk[0])))
                           for kt in (qt - 1, qt, qt + 1) if 0 <= kt < 16] + [(16, None), (17, None)]
                else:
                    kts = [(16, None), (17, None)]
                acc = psb(4 + h // 4)[:, (h % 4) * 65:(h % 4) * 65 + 65]
                for j, (kt, m_) in enumerate(kts):
                    blocks.append((swaKT[base:base + 64, kg, kt * 128:(kt + 1) * 128], swaQT[base:base + 64, kq, qs], m_,
                                   swaV[:, kt, kg, :], [acc], j == 0, j == len(kts) - 1))
            run_blocks(blocks, 0.125)
            for b_ in range(2):
                av = psb(4 + b_)[:, 0:260].rearrange("p (a b) -> p a b", a=4)
                dn = den[b_]
                es_ = esink[:, 4 * b_:4 * b_ + 4].rearrange("p (a b) -> p a b", b=1)
                P("dve", lambda e, o=dn, a=av[:, :, 64:65], b=es_: e.tensor_tensor(out=o, in0=a, in1=b, op=ALU.add),
                  outs=[dn], ins=[av[:, :, 64:65], esink])
                P("dve", lambda e, o=dn: e.reciprocal(out=o, in_=o), outs=[dn], ins=[dn])
                o_ = ot[:, b_ * 256:(b_ + 1) * 256].rearrange("p (a b) -> p a b", a=4)
                P("dve", lambda e, o=o_, a=av[:, :, 0:64], r=dn: e.tensor_tensor(out=o, in0=a, in1=r.broadcast_to([128, 4, 64]), op=ALU.mult),
                  outs=[o_], ins=[av[:, :, 0:64], dn])
            if qt < 16:
                diff_finish(dstore[:, qt % 4, :, :], ot)
            else:
                blocks = []
                for g in range(8):
                    h = g // 2
                    ch, base = g // 3, (g % 3) * 32
                    kts = [16, 17]
                    acc = psb(6 + g // 4)[:, (g % 4) * 65:(g % 4) * 65 + 65]
                    for j, kt in enumerate(kts):
                        blocks.append((diffKT[base:base + 32, ch, kt * 128:(kt + 1) * 128], diffQT[base:base + 32, ch, qs], None,
                                       diffV[:, kt, h, :], [acc], j == 0, j == len(kts) - 1))
                run_blocks(blocks, float(32 ** -0.5))
                for b_ in range(2):
                    av = psb(6 + b_)[:, 0:260].rearrange("p (a b) -> p a b", a=4)
                    dn = den[b_]
                    P("dve", lambda e, o=dn, a=av[:, :, 64:65]: e.reciprocal(out=o, in_=a), outs=[dn], ins=[av[:, :, 64:65]])
                    tm_ = dctx[:, 4 * b_:4 * b_ + 4, :]
                    P("dve", lambda e, o=tm_, a=av[:, :, 0:64], r=dn: e.tensor_tensor(out=o, in0=a, in1=r.broadcast_to([128, 4, 64]), op=ALU.mult),
                      outs=[tm_], ins=[av[:, :, 0:64], dn])
                diff_finish(dctx, ot)
            _sel = os.environ.get("MK_SEL", "gsd")
            if "g" not in _sel:
                P("pool", lambda e, o=gla_out[:, qt, :]: e.memset(o, 0.0), outs=[gla_out[:, qt, :]])
            if "s" not in _sel:
                P("pool", lambda e, o=ot[:, 0:512]: e.memset(o, 0.0), outs=[ot[:, 0:512]])
            if "d" not in _sel:
                P("pool", lambda e, o=ot[:, 512:768]: e.memset(o, 0.0), outs=[ot[:, 512:768]])
            if qt == int(os.environ.get("MK_DUMP_QT", "3")):
                dump("otok", ot)
            tb_ = psb((state["bank"] % 2) * 2, 1024, BF16).rearrange("p (a b) -> p a b", a=8)
            for c in range(8):
                src = gla_out[:, qt, c * 128:(c + 1) * 128] if c < 2 else ot[:, (c - 2) * 128:(c - 1) * 128]
                P("pe", lambda e, o=tb_[:, c, :], i=src: e.transpose(out=o, in_=i, identity=ident_bf),
                  outs=[tb_[:, c, :]], ins=[src, ident_bf])
            sblk = qt % 4 if qt < 16 else qt - 16
            ob_ = oTblk[:, :, sblk * 128:(sblk + 1) * 128]
            P("act", lambda e, o=ob_, i=tb_: e.copy(out=o, in_=i), outs=[ob_], ins=[tb_])
            last_in_blk = (qt % 4 == 3) if qt < 16 else (qt == 17)
            if not last_in_blk:
                continue
            q0 = (qt // 4) * 512 if qt < 16 else S
            ntok = 512 if qt < 16 else C
            for c in range(8):
                xc = xcs[state["xc"] % 2][:, 0:ntok]
                state["xc"] += 1
                src = xsp_d[:, c, q0:q0 + ntok]
                P("sp", lambda e, o=xc, i=src: e.dma_start(out=o, in_=i), outs=[xc], ins=[src], dma=True)
                yb = psb(state["yb"] % 4, ntok)
                state["yb"] += 1
                for k in range(8):
                    P("pe", lambda e, o=yb, w_=wo_sb[:, k, c * 128:(c + 1) * 128], r=oTblk[:, k, 0:ntok], k=k:
                      e.matmul(o, lhsT=w_, rhs=r, start=(k == 0), stop=(k == 7)),
                      outs=[yb], ins=[wo_sb[:, k, c * 128:(c + 1) * 128], oTblk[:, k, 0:ntok]])
                gh_ = ghTs[l % 2][:, 1, c, which:which + 1]
                P("dve", lambda e, o=xc, y=yb, gh_=gh_: e.scalar_tensor_tensor(
                    out=o, in0=y, scalar=gh_, in1=o, op0=ALU.mult, op1=ALU.add),
                  outs=[xc], ins=[yb, xc, gh_])
                P("sp", lambda e, o=src, i=xc: e.dma_start(out=o, in_=i), outs=[src], ins=[xc], dma=True)
        for g0_ in range(0, T, 512):
            n_ = min(512, T - g0_)
            P("sp", lambda e, o=xT[:, :, g0_:g0_ + n_], i=xsp_d[:, :, g0_:g0_ + n_]: e.dma_start(out=o, in_=i),
              outs=[xT[:, :, g0_:g0_ + n_]], ins=[xsp_d[:, :, g0_:g0_ + n_]], dma=True)
        A.top = top0

    n_layers = L
    if dbg_stage == "ident":
        n_layers = 0
    for l in range(n_layers):
        LP["p"] = l % 2
        if l == 0:
            compute_mod(l)
        ffn(l, 0, w1i_d, w1o_d, [(0, 1152, [(0, 1152, 0)]), (1152, 1152, [(1152, 896, 0), (2048, 256, 1)])])
        if dbg_stage == "ffn1":
            break
        mixer(l)
        if dbg_stage == "mix":
            break
        last = (l == L - 1)
        if last:
            ffn(l, 2, w2i_d, w2o_d, [(0, 1024, [(0, 1024, 0)]), (1024, 1024, [(1024, 1024, 0)])])
        else:
            ffn(l, 2, w2i_d, w2o_d, [(0, 1152, [(0, 1152, 0)]), (1152, 1152, [(1152, 896, 0), (2048, 256, 1)])])

    tmp_top = A.top
    yT = A.alloc([128, 8, 512], F32, "yT")
    ob = [A.alloc([128, D], F32, "ob%d" % i) for i in range(2)]
    bi = 0
    for g in range(S // 512):
        rms_norm_to_hT(g * 512, 512, 0, 0, final=True, dst=yT)
        for tt in range(4):
            t = g * 4 + tt
            o_sb = ob[t % 2]
            for h in range(2):
                bank = bi % 6
                bi += 1
                pv = psb(bank).rearrange("p (a b) -> p a b", a=4)
                for c4 in range(4):
                    c = h * 4 + c4
                    src = yT[:, c, tt * 128:(tt + 1) * 128]
                    P("pe", lambda e, o=pv[:, c4, :], i=src: e.transpose(out=o, in_=i, identity=ident),
                      outs=[pv[:, c4, :]], ins=[src, ident])
                dst = o_sb[:, h * 512:(h + 1) * 512]
                pvf = psb(bank)
                if h == 0:
                    P("dve", lambda e, o=dst, i=pvf: e.tensor_copy(out=o, in_=i), outs=[dst], ins=[pvf])
                else:
                    P("act", lambda e, o=dst, i=pvf: e.copy(out=o, in_=i), outs=[dst], ins=[pvf])
            dd = out_d[t * 128:(t + 1) * 128, :]
            P("sp", lambda e, o=dd, i=o_sb: e.dma_start(out=o, in_=i), outs=[dd], ins=[o_sb], dma=True)
    A.top = tmp_top
    for e_ in ENGS:
        P(e_, None, ins=[out_d])

    with ExitStack() as st:
        sems = {e: st.enter_context(nc.semaphore("s_" + e)) for e in ENGS}
        dsems = {e: [st.enter_context(nc.semaphore("d_%s%d" % (e, i))) for i in range(NSLOT)] for e in ("sp", "pool", "act")}
        block = st.enter_context(nc.Block())
        sch.emit(nc, block, sems, dsems)
    es.close()
    return nc, sch


def _mix_cols():
    GQ, GK, GV, GF, GB, OG = 0, 128, 256, 512, 528, 544
    SQ, SK, SV = 800, 1312, 1440
    DQ, DK_, DV = 1568, 1824, 2080

    def rng(a, n):
        return list(range(a, a + n))
    perm64 = rng(16, 16) + rng(0, 16) + rng(48, 16) + rng(32, 16)
    perm32 = rng(8, 8) + rng(0, 8) + rng(24, 8) + rng(16, 8)

    def partner64(cl):
        return [cl[h * 64 + perm64[i]] for h in range(2) for i in range(64)]
    fm = [rng(GQ, 128), rng(GK, 128), rng(GF, 16) + rng(GB, 16) + [-1] * 96]
    swaq = [rng(SQ + c * 128, 128) for c in range(4)]
    fm += swaq
    fm += [partner64(c) for c in swaq]
    swak = [rng(SK, 64) * 2, rng(SK + 64, 64) * 2]
    fm += swak
    fm += [partner64(c) for c in swak]

    def dgroups(base):
        return [rng(base + h * 64 + w * 32, 32) for h in range(4) for w in range(2)]

    def dchunks(gr):
        out = []
        for gs in ((0, 1, 2), (3, 4, 5), (6, 7)):
            cc = []
            for g in gs:
                cc += gr[g]
            cc += [-1] * (128 - len(cc))
            out.append(cc)
        return out

    def partner32(gr):
        return [[g[perm32[i]] for i in range(32)] for g in gr]
    dq, dk = dgroups(DQ), dgroups(DK_)
    fm += dchunks(dq) + dchunks(partner32(dq)) + dchunks(dk) + dchunks(partner32(dk))
    assert len(fm) == NFM
    cols = []
    for c in fm:
        assert len(c) == 128
        cols += c
    cols += rng(GV, 256) + rng(OG, 256) + rng(GK, 128) + rng(SV, 128) + rng(DV, 256)
    return np.asarray(cols, np.int64)


def _rope_tables():
    rows = S // 64

    def tables(hd):
        half = hd // 2
        row = np.repeat(np.arange(rows, dtype=np.float32), 64)
        col = np.tile(np.arange(64, dtype=np.float32), rows)
        inv = (1.0 / (np.float32(10000.0) ** (np.arange(0, half, 2, dtype=np.float32) / np.float32(half)))).astype(np.float32)

        def ang(pos):
            a = (pos[:, None] * inv[None, :]).astype(np.float32)
            return np.concatenate([a, a], -1)
        an = np.concatenate([ang(row), ang(col)], -1)
        q = hd // 4
        sign = np.concatenate([-np.ones(q), np.ones(q), -np.ones(q), np.ones(q)]).astype(np.float32)
        return np.cos(an).astype(np.float32).T, (np.sin(an).astype(np.float32) * sign[None, :]).T
    c64, s64 = tables(64)
    c32, s32 = tables(32)
    out = np.zeros((128, 4, S), np.float32)
    out[:, 0] = np.tile(c64, (2, 1))
    out[:, 1] = np.tile(s64, (2, 1))
    out[:, 2] = np.tile(c32, (4, 1))
    out[:, 3] = np.tile(s32, (4, 1))
    return out


def _shared_inputs(inputs):
    f = np.float32
    d = {}
    wm = np.asarray(inputs["w_mod"], dtype=f).reshape(L, 8, 128, 72, 128)
    d["w_mod"] = np.ascontiguousarray(wm.transpose(0, 3, 2, 1, 4))
    bm = np.asarray(inputs["b_mod"], f)
    d["b_modT"] = np.ascontiguousarray(bm.reshape(L, 72, 128).transpose(0, 2, 1))
    gs = [inputs["g_ffn1"][0], inputs["g_mix"][0], inputs["g_ffn2"][0],
          inputs["g_ffn1"][1], inputs["g_mix"][1], inputs["g_ffn2"][1], inputs["g_final"]]
    g = np.stack([np.asarray(v, f) for v in gs], 0)
    d["gT"] = np.ascontiguousarray(g.reshape(7, 8, 128).transpose(2, 0, 1))
    for k in ("w_ffn1_in", "w_ffn1_out", "w_ffn2_in", "w_ffn2_out", "w_out"):
        d[k] = np.ascontiguousarray(inputs[k], dtype=f)
    cols = _mix_cols()
    w_in = np.asarray(inputs["w_in"], f)
    wz = np.concatenate([w_in, np.zeros((L, D, 1), f)], axis=-1)
    d["w_mix"] = np.ascontiguousarray(wz[:, :, cols])
    d["rope"] = _rope_tables()
    pm = np.zeros((128, 2, 128), f)
    perm64 = list(range(16, 32)) + list(range(0, 16)) + list(range(48, 64)) + list(range(32, 48))
    perm32 = list(range(8, 16)) + list(range(0, 8)) + list(range(24, 32)) + list(range(16, 24))
    for m in range(128):
        pm[(m // 64) * 64 + perm64[m % 64], 0, m] = 1.0
        pm[(m // 32) * 32 + perm32[m % 32], 1, m] = 1.0
    d["permT"] = pm
    rm = np.zeros((128, 4), f)
    for p_ in range(128):
        rm[p_, p_ // 32] = 1.0
    d["rmask"] = rm
    sm = np.concatenate([np.asarray(inputs["g_gla_norm"], f), np.asarray(inputs["g_diff_norm"], f),
                         np.asarray(inputs["swa_sink"], f), np.asarray(inputs["diff_lambda"], f).reshape(L, 128)], axis=-1)
    d["small_bc"] = np.ascontiguousarray(np.broadcast_to(sm[:, None, :], (L, 128, 648)))
    wg = np.asarray(inputs["w_gla_gate"], f)
    wup = np.zeros((L, 32, 2, 128), f)
    wup[:, 0:16, 0, :] = wg[:, 0]
    wup[:, 16:32, 1, :] = wg[:, 1]
    d["w_up"] = wup
    d["b_gate"] = np.ascontiguousarray(np.asarray(inputs["b_gla_gate"], f).reshape(L, 1, 256))
    return d


def _prep_inputs(inputs, core, shared):
    f = np.float32
    d = dict(shared)
    d["x"] = np.ascontiguousarray(inputs["x"][core], dtype=f)
    d["ctx"] = np.ascontiguousarray(inputs["ctx"][core], dtype=f)
    cc = np.stack([np.asarray(inputs["c"][core], f), np.asarray(inputs["c_ctx"], f)], axis=-1)
    d["cT"] = np.ascontiguousarray(cc.reshape(8, 128, 2).transpose(1, 0, 2))
    return d


_CACHE = {}


def kernel(**inputs):
    stage = os.environ.get("MK_STAGE", "full")
    if stage not in _CACHE:
        _CACHE[stage] = build(stage)[0]
    nc = _CACHE[stage]
    shared = _shared_inputs(inputs)
    in_maps = [_prep_inputs(inputs, core, shared) for core in range(8)]
    res = run_bass_kernel_spmd(nc, in_maps, core_ids=list(range(8)))
    out = np.stack([np.asarray(r["out"], dtype=np.float32) for r in res.results], axis=0)
    return out
```

```python
import math
import os
import numpy as np
import concourse.bass as bass
import concourse.mybir as mybir
from concourse.bass_utils import run_bass_kernel_spmd

F32 = mybir.dt.float32
BF16 = mybir.dt.bfloat16
ALU = mybir.AluOpType
AF = mybir.ActivationFunctionType
AX = mybir.AxisListType

D = 1024
S = 2048
C = 256
T = S + C
NT = T // 128
DFF = 2816
NJ = DFF // 128
L = 2
EPS = 1e-6

ENGS = ("pe", "act", "dve", "pool", "sp")
NSLOT = 8
NFM = 27
NMIX = NFM * 128 + 1024


def _esz(dt):
    return mybir.dt.size(dt)


class _Op:
    __slots__ = ("eng", "fn", "waits", "signal", "ev", "dma")


class Sched:
    def __init__(self):
        self.ops = {e: [] for e in ENGS}
        self.clock = {e: {} for e in ENGS}
        self.snap = {}
        self.opof = {}
        self.recs = {}
        self.slot_cnt = {e: [0] * NSLOT for e in ENGS}
        self.slot_rr = {e: 0 for e in ENGS}
        self.untracked = set()
        self.GR = 2048

    def boxes(self, ap):
        name = ap.tensor.name
        if name in self.untracked:
            return []
        es = _esz(ap.dtype)
        aps = list(ap.ap)
        off = ap.offset
        if str(ap.space) not in ("SB", "PSUM"):
            ext = sum((c - 1) * abs(s) for s, c in aps)
            return [(name, 0, 1, off * es, (off + ext + 1) * es)]
        pstep, pcnt = aps[0]
        p0 = off // pstep
        f0 = off % pstep
        free = aps[1:]
        out = []

        def rec(base, dims):
            if not dims:
                out.append((name, p0, p0 + pcnt, base * es, (base + 1) * es))
                return
            inner_ext = sum((c - 1) * abs(s) for s, c in dims[1:]) + 1
            s0, c0 = dims[0]
            if len(dims) > 1 and abs(s0) >= inner_ext and 1 < c0 <= 64 and s0 > 0:
                for i in range(c0):
                    rec(base + i * s0, dims[1:])
            else:
                ext = sum((c - 1) * abs(s) for s, c in dims) + 1
                out.append((name, p0, p0 + pcnt, base * es, (base + ext) * es))

        rec(f0, free)
        return out

    def _conf(self, b, kind, deps, eng):
        name, p0, p1, f0, f1 = b
        for g in range(f0 // self.GR, (f1 - 1) // self.GR + 1):
            for r in self.recs.get((name, g), ()):
                rb = r[0]
                if rb[1] < p1 and p0 < rb[2] and rb[3] < f1 and f0 < rb[4]:
                    rk = r[1]
                    if kind == "R" and rk == "R":
                        continue
                    if kind == "X" and rk == "X" and r[3] == eng:
                        continue
                    deps.add(r[2])

    def _reg(self, b, kind, ev, eng, dma):
        name, p0, p1, f0, f1 = b
        for g in range(f0 // self.GR, (f1 - 1) // self.GR + 1):
            lst = self.recs.setdefault((name, g), [])
            g0 = max(f0, g * self.GR)
            g1 = min(f1, (g + 1) * self.GR)
            keep = []
            for r in lst:
                rb = r[0]
                r0 = max(rb[3], g * self.GR)
                r1 = min(rb[4], (g + 1) * self.GR)
                contained = rb[1] >= p0 and rb[2] <= p1 and r0 >= g0 and r1 <= g1
                if contained:
                    if kind == "W":
                        continue
                    if kind == r[1] and r[3] == eng and not dma and not r[4]:
                        continue
                keep.append(r)
            keep.append((b, kind, ev, eng, dma))
            self.recs[(name, g)] = keep

    def add(self, eng, fn, outs=(), ins=(), dma=False):
        op = _Op()
        op.eng, op.fn, op.dma, op.signal = eng, fn, dma, dma
        idx = len(self.ops[eng])
        deps = set()
        acc = []
        for ap in ins:
            for b in self.boxes(ap):
                if b[0] == "ps":
                    acc.append(((b[0], 0, 128, (b[3] // 2048) * 2048, ((b[4] - 1) // 2048 + 1) * 2048), "X"))
                else:
                    acc.append((b, "R"))
        for ap in outs:
            for b in self.boxes(ap):
                if b[0] == "ps":
                    acc.append(((b[0], 0, 128, (b[3] // 2048) * 2048, ((b[4] - 1) // 2048 + 1) * 2048), "W"))
                else:
                    acc.append((b, "W"))
        for b, kind in acc:
            self._conf(b, kind, deps, eng)
        if dma:
            s = self.slot_rr[eng]
            self.slot_rr[eng] = (s + 1) % NSLOT
            k = self.slot_cnt[eng][s]
            if k > 0:
                deps.add(((eng, s), k))
            self.slot_cnt[eng][s] = k + 1
            ev = ((eng, s), k + 1)
        else:
            ev = (eng, idx + 1)
        op.ev = ev
        clk = self.clock[eng]
        waits = []
        for key, val in sorted(deps, key=lambda d: -d[1]):
            if eng == "pe" and key == "pe":
                continue
            if clk.get(key, 0) >= val:
                continue
            waits.append((key, val))
            self.opof[(key, val)].signal = True
            for k2, v2 in self.snap[(key, val)].items():
                if clk.get(k2, 0) < v2:
                    clk[k2] = v2
            clk[key] = max(clk.get(key, 0), val)
        op.waits = waits
        self.snap[ev] = dict(clk)
        self.opof[ev] = op
        for b, kind in acc:
            self._reg(b, kind, ev, eng, dma)
        self.ops[eng].append(op)
        return op

    def emit(self, nc, block, sems, dsems):
        rank = {}
        for e in ENGS:
            n = 0
            for i, op in enumerate(self.ops[e]):
                if op.signal and not op.dma:
                    n += 1
                    rank[(e, i + 1)] = n

        def run(eng_name, eng):
            for op in self.ops[eng_name]:
                for key, val in op.waits:
                    if isinstance(key, tuple):
                        eng.wait_ge(dsems[key[0]][key[1]], 16 * val)
                    else:
                        eng.wait_ge(sems[key], rank[(key, val)])
                if op.fn is None:
                    continue
                inst = op.fn(eng)
                if op.dma:
                    inst.then_inc(dsems[op.ev[0][0]][op.ev[0][1]], 16)
                elif op.signal:
                    inst.then_inc(sems[eng_name], 1)

        @block.tensor
        def _(e):
            run("pe", e)

        @block.scalar
        def _(e):
            run("act", e)

        @block.vector
        def _(e):
            run("dve", e)

        @block.gpsimd
        def _(e):
            run("pool", e)

        @block.sync
        def _(e):
            run("sp", e)


class Arena:
    def __init__(self, t, nbytes, base=0):
        self.t = t
        self.nbytes = base + nbytes
        self.base = base
        self.top = base

    def alloc(self, shape, dt, name=None):
        es = _esz(dt)
        n = 1
        for s in shape[1:]:
            n *= s
        nb = (n * es + 63) // 64 * 64
        off = self.top
        self.top += nb
        assert self.top <= self.nbytes, ("arena overflow", name, self.top)
        v = self.t[0:shape[0], off // 4:(off + nb) // 4]
        if dt != F32:
            v = v.bitcast(dt)
        v = v[:, 0:n]
        if len(shape) == 3:
            v = v.rearrange("p (a b) -> p a b", a=shape[1])
        elif len(shape) == 4:
            v = v.rearrange("p (a b c) -> p a b c", a=shape[1], b=shape[2])
        return v


def build(dbg_stage="full"):
    nc = bass.Bass("TRN2", target_bir_lowering=False)
    dram = {}

    def din(name, shape, dt=F32):
        dram[name] = nc.dram_tensor(name, list(shape), dt, kind="ExternalInput").ap()
        return dram[name]

    x_d = din("x", [S, D])
    ctx_d = din("ctx", [C, D])
    cT_d = din("cT", [128, 8, 2])
    wmod_d = din("w_mod", [L, 72, 128, 8, 128])
    bmodT_d = din("b_modT", [L, 128, 72])
    gT_d = din("gT", [128, 7, 8])
    w1i_d = din("w_ffn1_in", [L, NJ, 128, 8, 256])
    w1o_d = din("w_ffn1_out", [L, 8, 128, NJ, 128])
    w2i_d = din("w_ffn2_in", [L, NJ, 128, 8, 256])
    w2o_d = din("w_ffn2_out", [L, 8, 128, NJ, 128])
    wmix_d = din("w_mix", [L, D, NMIX])
    wout_d = din("w_out", [L, D, D])
    rope_d = din("rope", [128, 4, S])
    small_d = din("small_bc", [L, 128, 648])
    wup_d = din("w_up", [L, 32, 2, 128])
    bg_d = din("b_gate", [L, 1, 256])
    perm_d = din("permT", [128, 2, 128])
    rmask_d = din("rmask", [128, 4])
    xsp_d = nc.dram_tensor("xspill", [128, 8, T], F32, kind="Internal").ap()
    out_d = nc.dram_tensor("out", [S, D], F32, kind="ExternalOutput").ap()
    DUMP = os.environ.get("MK_DUMP", "")
    if DUMP:
        dbg_d = nc.dram_tensor("dbg", [128, 16384], F32, kind="ExternalOutput").ap()

    def dump(name, ap):
        if not DUMP or name != DUMP:
            return
        shp = list(ap.shape)
        n = 1
        for v_ in shp[1:]:
            n *= v_
        dv = dbg_d[0:shp[0], 0:n]
        if len(shp) == 3:
            dv = dv.rearrange("p (a b) -> p a b", a=shp[1])
        elif len(shp) == 4:
            dv = dv.rearrange("p (a b c) -> p a b c", a=shp[1], b=shp[2])
        sch.add("pool", lambda e, o=dv, i=ap: e.dma_start(out=o, in_=i), outs=[dv], ins=[ap], dma=True)

    sch = Sched()
    for n in ("x", "ctx", "cT", "w_mod", "b_modT", "gT", "w_ffn1_in", "w_ffn1_out", "w_ffn2_in", "w_ffn2_out",
              "w_mix", "w_out", "rope", "small_bc", "w_up", "b_gate", "permT", "rmask"):
        sch.untracked.add(n)

    ARENA_BYTES = 206 * 1024
    from contextlib import ExitStack
    es = ExitStack()
    arena_t = es.enter_context(nc.sbuf_tensor("arena", [128, ARENA_BYTES // 4], F32))
    ps_t = es.enter_context(nc.psum_tensor("ps", [128, 8, 512], F32))
    A = Arena(arena_t, ARENA_BYTES)

    def psb(b, n=512, dt=F32):
        v = ps_t[:, b, :]
        if dt != F32:
            v = v.bitcast(dt)
        return v[:, 0:n]

    xT = A.alloc([128, 8, T], F32, "xT")
    hT = A.alloc([128, 8, T], BF16, "hT")
    ident = A.alloc([128, 128], F32, "ident")
    ones_bf = A.alloc([128, 128], BF16, "ones_bf")
    ones_f = A.alloc([128, 128], F32, "ones_f")
    gT = A.alloc([128, 7, 8], F32, "gT")
    cT = A.alloc([128, 8, 2], F32, "cT")
    scT = A.alloc([128, 8, 2], BF16, "scT")
    modTs = [A.alloc([128, 72, 2], F32, "modT%d" % i) for i in range(2)]
    gsTs = [A.alloc([128, 3, 8, 2], F32, "gsT%d" % i) for i in range(2)]
    ghTs = [A.alloc([128, 3, 8, 2], F32, "ghT%d" % i) for i in range(2)]
    bmTs = [A.alloc([128, 72], F32, "bmT%d" % i) for i in range(2)]
    LP = {"p": 0}
    ident_bf = A.alloc([128, 128], BF16, "ident_bf")
    triA = [A.alloc([128, 128], F32, "triA%d" % i) for i in range(2)]
    triB = [A.alloc([128, 128], F32, "triB%d" % i) for i in range(2)]
    msk = [A.alloc([128, 1, 128], BF16, "msk%d" % i) for i in range(2)]
    bmask = A.alloc([128, 4, 64], F32, "bmask")
    bmask4 = A.alloc([128, 4, 128], BF16, "bmask4")
    PERSIST_TOP = A.top
    m16 = A.alloc([128, 128], F32, "m16")
    ones4 = A.alloc([128, 4, 128], F32, "ones4")
    XT_BYTES = 8 * T * 4
    HT_OFF = XT_BYTES
    HT_BYTES = 8 * T * 2

    P = sch.add

    P("pool", lambda e: e.memset(ident, 0.0), outs=[ident])
    P("pool", lambda e: e.memset(ones_f, 1.0), outs=[ones_f])
    P("pool", lambda e: e.affine_select(ident, ones_f, [[-1, 128]], ALU.is_equal, 0.0, base=0, channel_multiplier=1),
      outs=[ident], ins=[ones_f])
    P("pool", lambda e: e.memset(ones_bf, 1.0), outs=[ones_bf])
    P("dve", lambda e: e.tensor_copy(out=ident_bf, in_=ident), outs=[ident_bf], ins=[ident])
    P("pool", lambda e: e.memset(m16, -1.0 / 16.0), outs=[m16])
    P("pool", lambda e: e.memset(ones4, 1.0), outs=[ones4])
    P("pool", lambda e: e.affine_select(triA[0], m16, [[1, 128]], ALU.is_ge, 0.0, base=0, channel_multiplier=-1),
      outs=[triA[0]], ins=[m16])
    P("pool", lambda e: e.affine_select(triA[1], m16, [[-1, 128]], ALU.is_ge, 0.0, base=0, channel_multiplier=1),
      outs=[triA[1]], ins=[m16])
    P("pool", lambda e: e.affine_select(triB[0], m16, [[-1, 128]], ALU.is_gt, 0.0, base=0, channel_multiplier=1),
      outs=[triB[0]], ins=[m16])
    P("pool", lambda e: e.affine_select(triB[1], m16, [[1, 128]], ALU.is_gt, 0.0, base=0, channel_multiplier=-1),
      outs=[triB[1]], ins=[m16])
    P("pool", lambda e: e.affine_select(msk[0][:, 0, :], ones_f, [[1, 128]], ALU.is_ge, 0.0, base=0, channel_multiplier=-1),
      outs=[msk[0]], ins=[ones_f])
    P("pool", lambda e: e.affine_select(msk[1][:, 0, :], ones_f, [[-1, 128]], ALU.is_ge, 0.0, base=0, channel_multiplier=1),
      outs=[msk[1]], ins=[ones_f])
    tmpm = A.alloc([128, 4, 128], F32, "tmpm")
    P("pool", lambda e: e.affine_select(tmpm, ones4, [[-32, 4], [0, 128]], ALU.is_ge, 0.0, base=0, channel_multiplier=1),
      outs=[tmpm], ins=[ones4])
    P("pool", lambda e: e.affine_select(bmask4, tmpm, [[32, 4], [0, 128]], ALU.is_ge, 0.0, base=31, channel_multiplier=-1),
      outs=[bmask4], ins=[tmpm])
    P("pool", lambda e: e.affine_select(bmask, tmpm[:, :, 0:64], [[32, 4], [0, 64]], ALU.is_ge, 0.0, base=31, channel_multiplier=-1),
      outs=[bmask], ins=[tmpm])
    P("sp", lambda e: e.dma_start(out=gT, in_=gT_d), outs=[gT], dma=True)
    P("sp", lambda e: e.dma_start(out=cT, in_=cT_d), outs=[cT], dma=True)
    sig = A.alloc([128, 8, 2], F32, "sig")
    P("act", lambda e: e.activation(out=sig, in_=cT, func=AF.Silu), outs=[sig], ins=[cT])
    P("dve", lambda e: e.tensor_copy(out=scT, in_=sig), outs=[scT], ins=[sig])

    ldb = [A.alloc([128, D], F32, "ld%d" % i) for i in range(2)]
    for t in range(NT):
        src = x_d[t * 128:(t + 1) * 128, :] if t < 16 else ctx_d[(t - 16) * 128:(t - 15) * 128, :]
        lb = ldb[t % 2]
        P("sp", lambda e, lb=lb, src=src: e.dma_start(out=lb, in_=src), outs=[lb], dma=True)
        for h in range(2):
            bank = (t * 2 + h) % 8
            pv = psb(bank).rearrange("p (a b) -> p a b", a=4)
            for c4 in range(4):
                c = h * 4 + c4
                P("pe", lambda e, o=pv[:, c4, :], i=lb[:, c * 128:(c + 1) * 128]: e.transpose(out=o, in_=i, identity=ident),
                  outs=[pv[:, c4, :]], ins=[lb[:, c * 128:(c + 1) * 128], ident])
            dst = xT[:, h * 4:(h + 1) * 4, t * 128:(t + 1) * 128]
            if (t + h) % 2 == 0:
                P("dve", lambda e, o=dst, i=pv: e.tensor_copy(out=o, in_=i), outs=[dst], ins=[pv])
            else:
                P("act", lambda e, o=dst, i=pv: e.copy(out=o, in_=i), outs=[dst], ins=[pv])

    A.top = PERSIST_TOP
    def norm_bufs():
        return dict(sq=[A.alloc([128, 512], BF16) for _ in range(2)], rs=A.alloc([128, 512], F32),
                    t1=[A.alloc([128, 512], F32) for _ in range(2)])

    def norm_groups(tok0, ntok, gs_idx, which, nb, final=False, dst=None):
        sq, rs, t1b = nb["sq"], nb["rs"], nb["t1"]
        lp = LP["p"]

        def emit_group(g0, n, gi):
            bank = 7 - (gi % 2)
            acc = psb(bank, n)
            for c in range(8):
                s_ = sq[c % 2][:, 0:n]
                src = xT[:, c, g0:g0 + n]
                if c % 2 == 0:
                    P("act", lambda e, o=s_, i=src: e.activation(out=o, in_=i, func=AF.Square), outs=[s_], ins=[src])
                else:
                    P("pool", lambda e, o=s_, i=src: e.tensor_tensor(out=o, in0=i, in1=i, op=ALU.mult), outs=[s_], ins=[src])
                P("pe", lambda e, o=acc, r=s_, c=c: e.matmul(o, lhsT=ones_bf, rhs=r, start=(c == 0), stop=(c == 7)),
                  outs=[acc], ins=[ones_bf, s_])
            r = rs[:, 0:n]
            P("act", lambda e, o=r, i=acc: e.activation(out=o, in_=i, func=AF.Sqrt, bias=EPS, scale=1.0 / D), outs=[r], ins=[acc])
            P("dve", lambda e, o=r: e.reciprocal(out=o, in_=o), outs=[r], ins=[r])
            for c in range(8):
                src = xT[:, c, g0:g0 + n]
                if final:
                    o = dst[:, c, g0 - tok0:g0 - tok0 + n]
                    P("dve", lambda e, o=o, i=src, r=r, c=c: e.scalar_tensor_tensor(
                        out=o, in0=i, scalar=gT[:, 6, c:c + 1], in1=r, op0=ALU.mult, op1=ALU.mult), outs=[o], ins=[src, r, gT])
                else:
                    o = hT[:, c, g0:g0 + n]
                    t1 = t1b[c % 2][:, 0:n]
                    gs_ = gsTs[lp][:, gs_idx, c, which:which + 1]
                    P("dve", lambda e, o=t1, i=src, r=r, gs_=gs_: e.scalar_tensor_tensor(
                        out=o, in0=i, scalar=gs_, in1=r, op0=ALU.mult, op1=ALU.mult),
                      outs=[t1], ins=[src, r, gs_])
                    sh = modTs[lp][:, (3 * gs_idx) * 8 + c, which:which + 1]
                    P("act", lambda e, o=o, i=t1, sh=sh: e.activation(out=o, in_=i, func=AF.Identity, bias=sh, scale=1.0),
                      outs=[o], ins=[t1, sh])

        out = []
        g0 = tok0
        gi = 0
        while g0 < tok0 + ntok:
            n = min(512, tok0 + ntok - g0)
            out.append(lambda g0=g0, n=n, gi=gi: emit_group(g0, n, gi))
            g0 += n
            gi += 1
        return out

    def rms_norm_to_hT(tok0, ntok, gs_idx, which, final=False, dst=None):
        tmp_top = A.top
        nb = norm_bufs()
        for g in norm_groups(tok0, ntok, gs_idx, which, nb, final, dst):
            g()
        A.top = tmp_top

    def mod_emitters(l, wbufs, bank):
        mT, gS, gH, bm = modTs[l % 2], gsTs[l % 2], ghTs[l % 2], bmTs[l % 2]
        acc = psb(bank, 144).rearrange("p (a b) -> p a b", a=72)

        def chunk(fc):
            w = wbufs[fc % len(wbufs)]
            src = wmod_d[l, fc]
            P("pool", lambda e, o=w, i=src: e.dma_start(out=o, in_=i), outs=[w], dma=True)
            for k in range(8):
                P("pe", lambda e, o=acc[:, fc, :], w_=w[:, k, :], r=scT[:, k, :], k=k:
                  e.matmul(o, lhsT=w_, rhs=r, start=(k == 0), stop=(k == 7)),
                  outs=[acc[:, fc, :]], ins=[w[:, k, :], scT[:, k, :]])

        def fin():
            P("sp", lambda e: e.dma_start(out=bm, in_=bmodT_d[l]), outs=[bm], dma=True)
            for w_ in range(2):
                P("dve", lambda e, w_=w_: e.tensor_tensor(out=mT[:, :, w_], in0=acc[:, :, w_], in1=bm, op=ALU.add),
                  outs=[mT[:, :, w_]], ins=[acc[:, :, w_], bm])
            for n in range(3):
                for w_ in range(2):
                    sc = mT[:, (3 * n + 1) * 8:(3 * n + 2) * 8, w_]
                    o = gS[:, n, :, w_]
                    P("dve", lambda e, o=o, sc=sc, n=n: e.scalar_tensor_tensor(
                        out=o, in0=sc, scalar=1.0, in1=gT[:, 3 * l + n, :], op0=ALU.add, op1=ALU.mult),
                      outs=[o], ins=[sc, gT])
                    ga = mT[:, (3 * n + 2) * 8:(3 * n + 3) * 8, w_]
                    o2 = gH[:, n, :, w_]
                    P("dve", lambda e, o=o2, ga=ga, n=n: e.tensor_scalar(
                        out=o, in0=ga, scalar1=(1.0 if n == 1 else 0.5), scalar2=None, op0=ALU.mult),
                      outs=[o2], ins=[ga])

        return [(lambda fc=fc: chunk(fc)) for fc in range(72)] + [fin]

    def compute_mod(l):
        tmp_top = A.top
        wb = [A.alloc([128, 8, 128], BF16) for _ in range(4)]
        for em in mod_emitters(l, wb, 6):
            em()
        A.top = tmp_top

    def ffn(l, n_idx, w_in_d, w_out_d, groups):
        tmp_top = A.top
        aT = A.alloc([128, NJ, 1152], BF16, "aT")
        wi = [A.alloc([128, 8, 256], BF16, "wi%d" % i) for i in range(3)]
        wo = [A.alloc([128, NJ, 128], BF16, "wo%d" % i) for i in range(3)]
        sg = [A.alloc([128, 384], F32, "sg%d" % i) for i in range(2)]
        nbuf = norm_bufs()
        for pi, (p0, pn, segs) in enumerate(groups):
            if pi == 0:
                for (t0, tn, which) in segs:
                    for g in norm_groups(t0, tn, n_idx, which, nbuf):
                        g()
            nxt_norm = []
            if pi + 1 < len(groups):
                for (t0, tn, which) in groups[pi + 1][2]:
                    nxt_norm += norm_groups(t0, tn, n_idx, which, nbuf)
            regs = []
            r0 = p0
            while r0 < p0 + pn:
                rn = min(384, p0 + pn - r0)
                regs.append((r0, rn))
                r0 += rn
            assert len(regs) <= 3
            for j in range(NJ):
                w = wi[j % 3]
                P("pool", lambda e, o=w, i=w_in_d[l, j]: e.dma_start(out=o, in_=i), outs=[w], dma=True)
                for ri, (r0, rn) in enumerate(regs):
                    pg = psb(ri, rn)
                    pu = psb(3 + ri, rn)
                    for k in range(8):
                        P("pe", lambda e, o=pg, w_=w[:, k, 0:128], r=hT[:, k, r0:r0 + rn], k=k:
                          e.matmul(o, lhsT=w_, rhs=r, start=(k == 0), stop=(k == 7)),
                          outs=[pg], ins=[w[:, k, 0:128], hT[:, k, r0:r0 + rn]])
                    for k in range(8):
                        P("pe", lambda e, o=pu, w_=w[:, k, 128:256], r=hT[:, k, r0:r0 + rn], k=k:
                          e.matmul(o, lhsT=w_, rhs=r, start=(k == 0), stop=(k == 7)),
                          outs=[pu], ins=[w[:, k, 128:256], hT[:, k, r0:r0 + rn]])
                    s = sg[(j * 3 + ri) % 2][:, 0:rn]
                    P("act", lambda e, o=s, i=pg: e.activation(out=o, in_=i, func=AF.Silu), outs=[s], ins=[pg])
                    o = aT[:, j, r0 - p0:r0 - p0 + rn]
                    P("dve", lambda e, o=o, a=pu, b=s: e.tensor_tensor(out=o, in0=a, in1=b, op=ALU.mult), outs=[o], ins=[pu, s])
            bi = 0
            for c in range(8):
                w = wo[c % 3]
                src = w_out_d[l, c]
                P("pool", lambda e, o=w, i=src: e.dma_start(out=o, in_=i), outs=[w], dma=True)
                for (r0, rn) in regs:
                    py = psb(bi % 6, rn)
                    bi += 1
                    for j in range(NJ):
                        P("pe", lambda e, o=py, w_=w[:, j, :], r=aT[:, j, r0 - p0:r0 - p0 + rn], j=j:
                          e.matmul(o, lhsT=w_, rhs=r, start=(j == 0), stop=(j == NJ - 1)),
                          outs=[py], ins=[w[:, j, :], aT[:, j, r0 - p0:r0 - p0 + rn]])
                    for (t0, tn, which) in segs:
                        a0 = max(t0, r0)
                        a1 = min(t0 + tn, r0 + rn)
                        if a1 <= a0:
                            continue
                        xs = xT[:, c, a0:a1]
                        ys = py[:, a0 - r0:a1 - r0]
                        gh_ = ghTs[LP["p"]][:, n_idx, c, which:which + 1]
                        P("dve", lambda e, xs=xs, ys=ys, gh_=gh_: e.scalar_tensor_tensor(
                            out=xs, in0=ys, scalar=gh_, in1=xs, op0=ALU.mult, op1=ALU.add),
                          outs=[xs], ins=[ys, xs, gh_])
                if nxt_norm and c % 2 == 1:
                    nxt_norm.pop(0)()
            while nxt_norm:
                nxt_norm.pop(0)()
        A.top = tmp_top

    def mixer(l):
        ctx_out = (l < L - 1)
        lam_init = 0.8 - 0.6 * math.exp(-0.3 * l)
        top0 = A.top
        out_tiles = list(range(NT)) if ctx_out else list(range(16))
        tmp_top_n = A.top
        nb_ = norm_bufs()
        for (t0_, tn_, wh_) in ((0, S, 0), (S, C, 1)):
            g0_ = t0_
            for gfn in norm_groups(t0_, tn_, 1, wh_, nb_):
                gfn()
                n_ = min(512, t0_ + tn_ - g0_)
                P("sp", lambda e, o=xsp_d[:, :, g0_:g0_ + n_], i=xT[:, :, g0_:g0_ + n_]: e.dma_start(out=o, in_=i),
                  outs=[xsp_d[:, :, g0_:g0_ + n_]], ins=[xT[:, :, g0_:g0_ + n_]], dma=True)
                g0_ += n_
        A.top = tmp_top_n
        AXa = Arena(arena_t, XT_BYTES, 0)
        AH = Arena(arena_t, HT_BYTES, HT_OFF)
        glaQT = AXa.alloc([128, 1, T], BF16)
        glaKT = AXa.alloc([128, 1, T], BF16)
        gfbT = AXa.alloc([128, 1, T], BF16)
        swaQT = AXa.alloc([128, 4, T], BF16)
        swaKT = AXa.alloc([128, 2, T], BF16)
        diffQT = AXa.alloc([128, 3, T], BF16)
        diffKT = AXa.alloc([128, 3, T], BF16)
        glaTM = A.alloc([128, NT, 640], BF16)
        swaV = A.alloc([128, NT, 2, 65], BF16)
        diffV = A.alloc([128, NT, 4, 65], BF16)
        small = A.alloc([128, 648], F32)
        wup = A.alloc([32, 2, 128], BF16)
        bg = A.alloc([1, 256], BF16)
        esink = A.alloc([128, 8], F32)
        nlam = A.alloc([128, 1], F32)
        gdiff = A.alloc([128, 4, 64], F32)
        lamt = A.alloc([128, 2, 32], F32)
        lams = A.alloc([128, 2], F32)
        mix_top = A.top
        ropeT = A.alloc([128, 4, S], BF16)
        P("pool", lambda e: e.dma_start(out=ropeT, in_=rope_d), outs=[ropeT], dma=True)
        P("sp", lambda e: e.dma_start(out=small, in_=small_d[l]), outs=[small], dma=True)
        P("pool", lambda e: e.dma_start(out=wup, in_=wup_d[l]), outs=[wup], dma=True)
        P("pool", lambda e: e.dma_start(out=bg, in_=bg_d[l]), outs=[bg], dma=True)
        P("pool", lambda e: e.memset(swaV[:, :, :, 64:65], 1.0), outs=[swaV[:, :, :, 64:65]])
        P("pool", lambda e: e.memset(diffV[:, :, :, 64:65], 1.0), outs=[diffV[:, :, :, 64:65]])
        P("act", lambda e: e.activation(out=esink, in_=small[:, 512:520], func=AF.Exp), outs=[esink], ins=[small])
        dl = small[:, 520:648].rearrange("p (a b) -> p a b", a=4)
        P("dve", lambda e: e.tensor_tensor(out=lamt[:, 0, :], in0=dl[:, 0, :], in1=dl[:, 1, :], op=ALU.mult),
          outs=[lamt[:, 0, :]], ins=[small])
        P("dve", lambda e: e.tensor_tensor(out=lamt[:, 1, :], in0=dl[:, 2, :], in1=dl[:, 3, :], op=ALU.mult),
          outs=[lamt[:, 1, :]], ins=[small])
        P("dve", lambda e: e.reduce_sum(out=lams, in_=lamt, axis=AX.X), outs=[lams], ins=[lamt])
        P("act", lambda e: e.activation(out=lams, in_=lams, func=AF.Exp), outs=[lams], ins=[lams])
        P("dve", lambda e: e.scalar_tensor_tensor(out=nlam, in0=lams[:, 1:2], scalar=-lam_init, in1=lams[:, 0:1],
                                                  op0=ALU.add, op1=ALU.subtract), outs=[nlam], ins=[lams])
        P("dve", lambda e: e.tensor_scalar(out=gdiff, in0=small[:, 256:512].rearrange("p (a b) -> p a b", a=4),
                                           scalar1=1.0 - lam_init, scalar2=None, op0=ALU.mult), outs=[gdiff], ins=[small])

        wfm = [A.alloc([128, 2, 8, 128], BF16) for _ in range(2)]
        permT = A.alloc([128, 2, 128], BF16)
        xb = [A.alloc([128, 512], BF16) for _ in range(2)]
        P("pool", lambda e: e.dma_start(out=permT, in_=perm_d), outs=[permT], dma=True)
        wtm = [A.alloc([128, 8, 512], BF16) for _ in range(2)]
        rt = [A.alloc([128, 512], F32) for _ in range(2)]
        fm = [(0, None, glaQT[:, 0, :], None), (1, None, glaKT[:, 0, :], None), (2, None, gfbT[:, 0, :], None)]
        for c in range(4):
            fm.append((3 + c, 7 + c, swaQT[:, c, :], 0))
        for c in range(2):
            fm.append((11 + c, 13 + c, swaKT[:, c, :], 0))
        for c in range(3):
            fm.append((15 + c, 18 + c, diffQT[:, c, :], 2))
        for c in range(3):
            fm.append((21 + c, 24 + c, diffKT[:, c, :], 2))
        groups = [(0, 512), (512, 512), (1024, 512), (1536, 512), (2048, 256)]
        gi = 0
        def load_fm(fi):
            cm, cp, dest, tb = fm[fi]
            w = wfm[fi % 2]
            src = wmix_d[l, :, cm * 128:(cm + 1) * 128].rearrange("(kc p) f -> p kc f", p=128)
            P("pool", lambda e, o=w[:, 0, :, :], i=src: e.dma_start(out=o, in_=i), outs=[w[:, 0, :, :]], dma=True)

        load_fm(0)
        for g in range(2):
            src = wmix_d[l, :, NFM * 128 + g * 512:NFM * 128 + (g + 1) * 512].rearrange("(kc p) f -> p kc f", p=128)
            P("pool", lambda e, o=wtm[g], i=src: e.dma_start(out=o, in_=i), outs=[wtm[g]], dma=True)
        for fi, (cm, cp, dest, tb) in enumerate(fm):
            w = wfm[fi % 2]
            if fi + 1 < len(fm):
                load_fm(fi + 1)
            for (g0, n) in groups:
                b0 = (2 * gi) % 4
                gi += 1
                pm = psb(b0, n)
                pp = psb(b0 + 1, n)
                rope = (cp is not None) and g0 < S
                for k in range(8):
                    P("pe", lambda e, o=pm, w_=w[:, 0, k, :], r=hT[:, k, g0:g0 + n], k=k:
                      e.matmul(o, lhsT=w_, rhs=r, start=(k == 0), stop=(k == 7)),
                      outs=[pm], ins=[w[:, 0, k, :], hT[:, k, g0:g0 + n]])
                if rope:
                    xb_ = xb[gi % 2][:, 0:n]
                    P("act", lambda e, o=xb_, i=pm: e.copy(out=o, in_=i), outs=[xb_], ins=[pm])
                    pmx = permT[:, (0 if tb == 0 else 1), :]
                    P("pe", lambda e, o=pp, w_=pmx, r=xb_: e.matmul(o, lhsT=w_, rhs=r, start=True, stop=True),
                      outs=[pp], ins=[pmx, xb_])
                    t1 = rt[0][:, 0:n]
                    t2 = rt[1][:, 0:n]
                    cs = ropeT[:, tb, g0:g0 + n]
                    sn = ropeT[:, tb + 1, g0:g0 + n]
                    P("dve", lambda e, o=t1, a=pm, b=cs: e.tensor_tensor(out=o, in0=a, in1=b, op=ALU.mult), outs=[t1], ins=[pm, cs])
                    P("dve", lambda e, o=t2, a=pp, b=sn: e.tensor_tensor(out=o, in0=a, in1=b, op=ALU.mult), outs=[t2], ins=[pp, sn])
                    d_ = dest[:, g0:g0 + n]
                    P("pool", lambda e, o=d_, a=t1, b=t2: e.tensor_tensor(out=o, in0=a, in1=b, op=ALU.add), outs=[d_], ins=[t1, t2])
                else:
                    d_ = dest[:, g0:g0 + n]
                    P("act", lambda e, o=d_, i=pm: e.copy(out=o, in_=i), outs=[d_], ins=[pm])
        for t in range(NT):
            for g in range(2):
                pb = psb(4 + (t * 2 + g) % 4)
                for k in range(8):
                    P("pe", lambda e, o=pb, a=hT[:, k, t * 128:(t + 1) * 128], w_=wtm[g][:, k, :], k=k:
                      e.matmul(o, lhsT=a, rhs=w_, start=(k == 0), stop=(k == 7)),
                      outs=[pb], ins=[hT[:, k, t * 128:(t + 1) * 128], wtm[g][:, k, :]])
                if g == 0:
                    d_ = glaTM[:, t, 0:512]
                    P("act", lambda e, o=d_, i=pb: e.copy(out=o, in_=i), outs=[d_], ins=[pb])
                else:
                    d_ = glaTM[:, t, 512:640]
                    P("dve", lambda e, o=d_, i=pb[:, 0:128]: e.tensor_copy(out=o, in_=i), outs=[d_], ins=[pb[:, 0:128]])
                    d2 = swaV[:, t, :, 0:64]
                    s2 = pb[:, 128:256].rearrange("p (a b) -> p a b", a=2)
                    P("act", lambda e, o=d2, i=s2: e.copy(out=o, in_=i), outs=[d2], ins=[s2])
                    d3 = diffV[:, t, :, 0:64]
                    s3 = pb[:, 256:512].rearrange("p (a b) -> p a b", a=4)
                    P("dve", lambda e, o=d3, i=s3: e.tensor_copy(out=o, in_=i), outs=[d3], ins=[s3])
        A.top = mix_top
        for nm_, ap_ in (("glaQT", glaQT), ("glaKT", glaKT), ("gfbT", gfbT), ("swaQT", swaQT), ("swaKT", swaKT),
                         ("diffQT", diffQT), ("diffKT", diffKT), ("glaTM", glaTM), ("swaV", swaV), ("diffV", diffV), ("hT", hT)):
            dump(nm_, ap_)

        _cut = os.environ.get("MK_MIXCUT", "")
        if _cut == "inproj":
            P("sp", lambda e: e.dma_start(out=xT, in_=xsp_d), outs=[xT], ins=[xsp_d], dma=True)
            A.top = top0
            return
        esp = AH.alloc([128, 2, NT, 128], F32)
        ost = AH.alloc([128, NT, 256], F32)
        gla_out = A.alloc([128, NT, 256], BF16)
        gla_top = A.top
        bi = 0
        for d_ in range(2):
            for t0 in range(0, NT, 4):
                nb = min(4, NT - t0)
                pb = psb(bi % 4).rearrange("p (a b) -> p a b", a=4)
                bi += 1
                for s_ in range(nb):
                    t = t0 + s_
                    P("pe", lambda e, o=pb[:, s_, :], a=gfbT[0:32, 0, t * 128:(t + 1) * 128], w_=wup[:, d_, :]:
                      e.matmul(o, lhsT=a, rhs=w_, start=True, stop=False),
                      outs=[pb[:, s_, :]], ins=[gfbT[0:32, 0, t * 128:(t + 1) * 128], wup[:, d_, :]])
                    P("pe", lambda e, o=pb[:, s_, :], a=ones_bf[0:1, :], w_=bg[0:1, d_ * 128:(d_ + 1) * 128]:
                      e.matmul(o, lhsT=a, rhs=w_, start=False, stop=True),
                      outs=[pb[:, s_, :]], ins=[ones_bf[0:1, :], bg[0:1, d_ * 128:(d_ + 1) * 128]])
                o_ = esp[:, d_, t0:t0 + nb, :]
                P("act", lambda e, o=o_, i=pb[:, 0:nb, :]: e.activation(out=o, in_=i, func=AF.Exp, scale=-1.0),
                  outs=[o_], ins=[pb[:, 0:nb, :]])
        for d_ in range(2):
            P("act", lambda e, o=esp[:, d_, :, :]: e.activation(out=o, in_=o, func=AF.Ln, bias=1.0, scale=1.0),
              outs=[esp[:, d_, :, :]], ins=[esp[:, d_, :, :]])
        E1 = [A.alloc([128, 128], F32) for _ in range(2)]
        E2 = [A.alloc([128, 128], F32) for _ in range(2)]
        E3 = [A.alloc([128, 128], F32) for _ in range(2)]
        KtT = [A.alloc([128, 128], BF16) for _ in range(2)]
        Kh = [A.alloc([128, 128], BF16) for _ in range(2)]
        Qexp = [A.alloc([128, 4, 128], BF16) for _ in range(2)]
        QtT = [[A.alloc([128, 1, 128], BF16) for _ in range(2)] for _ in range(2)]
        attm = [[A.alloc([128, 4, 128], BF16) for _ in range(2)] for _ in range(2)]
        Um = [[A.alloc([128, 4, 64], F32) for _ in range(2)] for _ in range(2)]
        decs = [[A.alloc([128, 1], F32) for _ in range(2)] for _ in range(2)]
        Sf = [A.alloc([128, 4, 64], F32) for _ in range(2)]
        Sb = [[A.alloc([128, 4, 64], BF16) for _ in range(2)] for _ in range(2)]
        orders = [[16, 17] + list(range(16)), [17, 16] + list(range(15, -1, -1))]
        visited = set()
        for d_ in range(2):
            P("pool", lambda e, d_=d_: e.memset(Sf[d_], 0.0), outs=[Sf[d_]])
            P("pool", lambda e, d_=d_: e.memset(Sb[d_][0], 0.0), outs=[Sb[d_][0]])

        def gla_prep(st, d_):
            t = orders[d_][st]
            par = st % 2
            sp_t = esp[:, d_, t, :]
            bA = psb(3 * d_)
            pc, pr, pu = bA[:, 0:128], bA[:, 128:256], bA[:, 256:512]
            P("pe", lambda e, o=pc, a=sp_t, b=triA[d_]: e.matmul(o, lhsT=a, rhs=b, start=True, stop=True),
              outs=[pc], ins=[sp_t, triA[d_]])
            P("pe", lambda e, o=pr, a=triB[d_], b=sp_t: e.matmul(o, lhsT=a, rhs=b, start=True, stop=True),
              outs=[pr], ins=[triB[d_], sp_t])
            e1, e2, e3 = E1[d_], E2[d_], E3[d_]
            P("act", lambda e, o=e3, i=pr: e.activation(out=o, in_=i, func=AF.Exp), outs=[e3], ins=[pr])
            P("act", lambda e, o=e1, i=pc: e.activation(out=o, in_=i, func=AF.Exp), outs=[e1], ins=[pc])
            if t in out_tiles:
                P("act", lambda e, o=e2, i=pc: e.activation(out=o, in_=i, func=AF.Exp, scale=-1.0), outs=[e2], ins=[pc])
            kh = Kh[d_]
            P("pool", lambda e, o=kh, a=glaTM[:, t, 512:640], b=e3: e.tensor_tensor(out=o, in0=a, in1=b, op=ALU.mult),
              outs=[kh], ins=[glaTM[:, t, 512:640], e3])
            P("pe", lambda e, o=pu, a=kh, b=glaTM[:, t, 0:256]: e.matmul(o, lhsT=a, rhs=b, start=True, stop=True),
              outs=[pu], ins=[kh, glaTM[:, t, 0:256]])
            dsrc = e1[:, 127:128] if d_ == 0 else e1[:, 0:1]
            dc = decs[d_][par]
            P("pool", lambda e, o=dc, i=dsrc: e.tensor_copy(out=o, in_=i), outs=[dc], ins=[dsrc])
            um = Um[d_][par]
            P("dve", lambda e, o=um, a=pu.rearrange("p (a b) -> p a b", a=4): e.tensor_tensor(out=o, in0=a, in1=bmask, op=ALU.mult),
              outs=[um], ins=[pu, bmask])
            if t in out_tiles:
                qt_, kt_ = QtT[d_][par], KtT[d_]
                tq = glaQT[:, :, t * 128:(t + 1) * 128]
                P("dve", lambda e, o=qt_, a=tq, b=e1: e.scalar_tensor_tensor(
                    out=o[:, 0, :], in0=a[:, 0, :], scalar=float(32 ** -0.5), in1=b, op0=ALU.mult, op1=ALU.mult),
                  outs=[qt_], ins=[tq, e1])
                tk = glaKT[:, 0, t * 128:(t + 1) * 128]
                P("pool", lambda e, o=kt_, a=tk, b=e2: e.tensor_tensor(out=o, in0=a, in1=b, op=ALU.mult), outs=[kt_], ins=[tk, e2])
                qe = Qexp[d_]
                P("pool", lambda e, o=qe, a=qt_: e.tensor_tensor(out=o, in0=bmask4, in1=a.broadcast_to([128, 4, 128]), op=ALU.mult),
                  outs=[qe], ins=[bmask4, qt_])
                pa = psb(3 * d_ + 1)
                P("pe", lambda e, o=pa, a=kt_, b=qe: e.matmul(o, lhsT=a, rhs=b.rearrange("p a b -> p (a b)"), start=True, stop=True),
                  outs=[pa], ins=[kt_, qe])
                am = attm[d_][par]
                P("dve", lambda e, o=am, a=pa.rearrange("p (a b) -> p a b", a=4), m_=msk[d_]:
                  e.tensor_tensor(out=o, in0=a, in1=m_.broadcast_to([128, 4, 128]), op=ALU.mult), outs=[am], ins=[pa, msk[d_]])

        def gla_chain(st, d_):
            t = orders[d_][st]
            par = st % 2
            cur, nxt = st % 2, (st + 1) % 2
            sb_cur, sb_nxt = Sb[d_][cur], Sb[d_][nxt]
            if t in out_tiles:
                qt_, am = QtT[d_][par], attm[d_][par]
                po = psb(3 * d_ + 2, 256)
                for h in range(4):
                    oh = po[:, h * 64:(h + 1) * 64]
                    P("pe", lambda e, o=oh, a=qt_[:, 0, :], b=sb_cur[:, h, :]: e.matmul(o, lhsT=a, rhs=b, start=True, stop=False),
                      outs=[oh], ins=[qt_, sb_cur[:, h, :]])
                    P("pe", lambda e, o=oh, a=am[:, h, :], b=glaTM[:, t, h * 64:(h + 1) * 64]: e.matmul(o, lhsT=a, rhs=b, start=False, stop=True),
                      outs=[oh], ins=[am[:, h, :], glaTM[:, t, h * 64:(h + 1) * 64]])
                if t not in visited:
                    visited.add(t)
                    P("act", lambda e, o=ost[:, t, :], i=po: e.copy(out=o, in_=i), outs=[ost[:, t, :]], ins=[po])
                else:
                    P("dve", lambda e, o=ost[:, t, :], i=po: e.tensor_tensor(out=o, in0=i, in1=o, op=ALU.add),
                      outs=[ost[:, t, :]], ins=[po, ost[:, t, :]])
            P("dve", lambda e, o=Sf[d_], sc=decs[d_][par], b=Um[d_][par]: e.scalar_tensor_tensor(
                out=o, in0=o, scalar=sc, in1=b, op0=ALU.mult, op1=ALU.add), outs=[Sf[d_]], ins=[Sf[d_], decs[d_][par], Um[d_][par]])
            P("act", lambda e, o=sb_nxt, i=Sf[d_]: e.copy(out=o, in_=i), outs=[sb_nxt], ins=[Sf[d_]])

        mod_ems = []
        if l + 1 < L and dbg_stage == "full":
            wb2 = [A.alloc([128, 8, 128], BF16) for _ in range(2)]
            mod_ems = mod_emitters(l + 1, wb2, 6)
        for st in range(NT + 1):
            for d_ in range(2):
                if st < NT:
                    gla_prep(st, d_)
            for d_ in range(2):
                if st >= 1:
                    gla_chain(st - 1, d_)
            for _ in range(4):
                if len(mod_ems) > 1:
                    mod_ems.pop(0)()
        while mod_ems:
            mod_ems.pop(0)()
        A.top = gla_top
        no = len(out_tiles)
        ssg = A.alloc([128, NT, 4, 1], F32)
        sqg = [A.alloc([128, 4, 64], F32) for _ in range(2)]
        sgl = [A.alloc([128, 256], F32) for _ in range(2)]
        for t in out_tiles:
            o4 = ost[:, t, :].rearrange("p (a b) -> p a b", a=4)
            q_ = sqg[t % 2]
            P("pool", lambda e, o=q_, a=o4: e.tensor_tensor(out=o, in0=a, in1=a, op=ALU.mult), outs=[q_], ins=[o4])
            P("dve", lambda e, o=ssg[:, t, :, 0], i=q_: e.reduce_sum(out=o, in_=i, axis=AX.X), outs=[ssg[:, t, :, :]], ins=[q_])
        sv = ssg[:, 0:no, :, :]
        P("act", lambda e, o=sv: e.activation(out=o, in_=o, func=AF.Sqrt, bias=EPS, scale=1.0 / 64.0), outs=[sv], ins=[sv])
        P("dve", lambda e, o=sv: e.reciprocal(out=o, in_=o), outs=[sv], ins=[sv])
        ggla = small[:, 0:256].rearrange("p (a b) -> p a b", a=4)
        for t in out_tiles:
            o4 = ost[:, t, :].rearrange("p (a b) -> p a b", a=4)
            q_ = sqg[t % 2]
            sg_ = sgl[t % 2]
            P("act", lambda e, o=sg_, i=glaTM[:, t, 256:512]: e.activation(out=o, in_=i, func=AF.Silu), outs=[sg_], ins=[glaTM[:, t, 256:512]])
            P("dve", lambda e, o=q_, a=o4, r=ssg[:, t, :, :]: e.tensor_tensor(out=o, in0=a, in1=r.broadcast_to([128, 4, 64]), op=ALU.mult),
              outs=[q_], ins=[o4, ssg[:, t, :, :]])
            P("pool", lambda e, o=q_: e.tensor_tensor(out=o, in0=o, in1=ggla, op=ALU.mult), outs=[q_], ins=[q_, small])
            go = gla_out[:, t, :]
            P("dve", lambda e, o=go, a=q_, b=sg_: e.tensor_tensor(out=o, in0=a.rearrange("p a b -> p (a b)"), in1=b, op=ALU.mult),
              outs=[go], ins=[q_, sg_])
        A.top = gla_top

        dump("gla_out", gla_out)
        dump("ost", ost)
        if _cut == "gla":
            P("sp", lambda e: e.dma_start(out=xT, in_=xsp_d), outs=[xT], ins=[xsp_d, gla_out], dma=True)
            A.top = top0
            return
        AH2 = Arena(arena_t, HT_BYTES, HT_OFF)
        wo_sb = AH2.alloc([128, 8, D], BF16)
        P("pool", lambda e: e.dma_start(out=wo_sb, in_=wout_d[l].rearrange("(kc p) f -> p kc f", p=128)), outs=[wo_sb], dma=True)
        pT = [AH2.alloc([128, 8, 128], BF16) for _ in range(3)]
        otok = [A.alloc([128, 768], BF16) for _ in range(2)]
        dstore = AH2.alloc([128, 4, 8, 64], F32)
        rmask = A.alloc([128, 4], F32)
        qmb = [A.alloc([128, 512], BF16) for _ in range(2)]
        P("sp", lambda e: e.dma_start(out=rmask, in_=rmask_d), outs=[rmask], dma=True)
        oTblk = A.alloc([128, 8, 512], BF16)
        xcs = [A.alloc([128, 512], F32) for _ in range(2)]
        den = [AH2.alloc([128, 4, 1], F32) for _ in range(2)]
        dctx = AH2.alloc([128, 8, 64], F32)
        od = AH2.alloc([128, 4, 64], F32)
        od2 = AH2.alloc([128, 4, 64], F32)
        ssd = AH2.alloc([128, 4, 1], F32)
        state = {"bank": 0, "pt": 0, "xc": 0, "yb": 0}
        dump("wo_sb", wo_sb[:, :, 0:2048] if False else wo_sb)

        def run_blocks(blocks, scale):
            batches = []
            for blk in blocks:
                rg = blk[0].base_partition()
                w = blk[1].shape[-1]
                if (batches and batches[-1][0][0].base_partition() == rg and batches[-1][0][1].shape[-1] == w
                        and (len(batches[-1]) + 1) * w <= 1024):
                    batches[-1].append(blk)
                else:
                    batches.append([blk])
            pend = None
            for bl in batches + [None]:
                cur = None
                if bl is not None:
                    w = bl[0][1].shape[-1]
                    ncol = len(bl) * w
                    b2 = (state["bank"] % 2) * 2
                    state["bank"] += 1
                    bank = ps_t[:, b2:b2 + 2, :].rearrange("p a c -> p (a c)")
                    p_ = pT[state["pt"] % 3].rearrange("p a c -> p (a c)")
                    state["pt"] += 1
                    for i, (kT, qT, m_, v, accs, first, last) in enumerate(bl):
                        P("pe", lambda e, o=bank[:, i * w:(i + 1) * w], a=kT, b=qT: e.matmul(o, lhsT=a, rhs=b, start=True, stop=True),
                          outs=[bank[:, i * w:(i + 1) * w]], ins=[kT, qT])
                    P("act", lambda e, o=p_[:, 0:ncol], i=bank[:, 0:ncol]: e.activation(out=o, in_=i, func=AF.Exp, scale=scale),
                      outs=[p_[:, 0:ncol]], ins=[bank[:, 0:ncol]])
                    for i, (kT, qT, m_, v, accs, first, last) in enumerate(bl):
                        if m_ is not None:
                            P("pool", lambda e, o=p_[:, i * w:(i + 1) * w], m_=m_: e.tensor_tensor(out=o, in0=o, in1=m_[:, 0, :], op=ALU.mult),
                              outs=[p_[:, i * w:(i + 1) * w]], ins=[p_[:, i * w:(i + 1) * w], m_])
                    cur = (bl, p_, w)
                if pend is not None:
                    pbl, pp_, pw = pend
                    for i, (kT, qT, m_, v, accs, first, last) in enumerate(pbl):
                        for su, acc in enumerate(accs):
                            lh = pp_[:, i * pw + su * 128:i * pw + (su + 1) * 128]
                            st_ = first and su == 0
                            P("pe", lambda e, o=acc, a=lh, b=v, st_=st_, last=last: e.matmul(
                                o, lhsT=a, rhs=b, start=st_, stop=last, skip_group_check=(len(accs) > 1)),
                              outs=[acc], ins=[lh, v])
                pend = cur

        def diff_finish(src4, ot):
            t4 = src4.rearrange("p (h w) d -> p h w d", w=2)
            P("dve", lambda e, a=t4[:, :, 1, :], b=t4[:, :, 0, :]: e.scalar_tensor_tensor(
                out=od, in0=a, scalar=nlam[:, 0:1], in1=b, op0=ALU.mult, op1=ALU.add), outs=[od], ins=[src4, nlam])
            P("pool", lambda e: e.tensor_tensor(out=od2, in0=od, in1=od, op=ALU.mult), outs=[od2], ins=[od])
            P("dve", lambda e: e.reduce_sum(out=ssd[:, :, 0], in_=od2, axis=AX.X), outs=[ssd], ins=[od2])
            P("act", lambda e: e.activation(out=ssd, in_=ssd, func=AF.Sqrt, bias=EPS, scale=1.0 / 64.0), outs=[ssd], ins=[ssd])
            P("dve", lambda e: e.reciprocal(out=ssd, in_=ssd), outs=[ssd], ins=[ssd])
            P("dve", lambda e: e.tensor_tensor(out=od2, in0=od, in1=ssd.broadcast_to([128, 4, 64]), op=ALU.mult), outs=[od2], ins=[od, ssd])
            o_ = ot[:, 512:768].rearrange("p (a b) -> p a b", a=4)
            P("pool", lambda e, o=o_: e.tensor_tensor(out=o, in0=od2, in1=gdiff, op=ALU.mult), outs=[o_], ins=[od2, gdiff])

        def diff_qblock(qb):
            q0 = qb * 512
            for g in range(8):
                h = g // 2
                ch, base = g // 3, (g % 3) * 32
                accb = psb(6 + g % 2)[:, 0:260].rearrange("p (a b) -> p a b", a=4)
                qm = qmb[g % 2]
                P("dve", lambda e, o=qm, a=diffQT[:, ch, q0:q0 + 512], m_=rmask[:, (g % 3):(g % 3) + 1]: e.tensor_scalar(
                    out=o, in0=a, scalar1=m_, scalar2=None, op0=ALU.mult), outs=[qm], ins=[diffQT[:, ch, q0:q0 + 512], rmask])
                blocks = []
                for kt in range(NT):
                    blocks.append((diffKT[:, ch, kt * 128:(kt + 1) * 128], qm, None,
                                   diffV[:, kt, h, :], [accb[:, su, :] for su in range(4)], kt == 0, kt == NT - 1))
                run_blocks(blocks, float(32 ** -0.5))
                dn = den[g % 2]
                P("dve", lambda e, o=dn, a=accb[:, :, 64:65]: e.reciprocal(out=o, in_=a), outs=[dn], ins=[accb[:, :, 64:65]])
                P("dve", lambda e, o=dstore[:, :, g, :], a=accb[:, :, 0:64], r=dn: e.tensor_tensor(
                    out=o, in0=a, in1=r.broadcast_to([128, 4, 64]), op=ALU.mult), outs=[dstore[:, :, g, :]], ins=[accb[:, :, 0:64], dn])

        for qi, qt in enumerate(out_tiles):
            which = 0 if qt < 16 else 1
            ot = otok[qi % 2]
            qs = slice(qt * 128, (qt + 1) * 128)
            if qt < 16 and qt % 4 == 0:
                diff_qblock(qt // 4)
            blocks = []
            for h in range(8):
                kg, kq, base = h // 4, h // 2, (h % 2) * 64
                if qt < 16:
                    kts = [(kt, (None if kt == qt else (msk[1] if kt < qt else msk[0])))
                           for kt in (qt - 1, qt, qt + 1) if 0 <= kt < 16] + [(16, None), (17, None)]
                else:
                    kts = [(16, None), (17, None)]
                acc = psb(4 + h // 4)[:, (h % 4) * 65:(h % 4) * 65 + 65]
                for j, (kt, m_) in enumerate(kts):
                    blocks.append((swaKT[base:base + 64, kg, kt * 128:(kt + 1) * 128], swaQT[base:base + 64, kq, qs], m_,
                                   swaV[:, kt, kg, :], [acc], j == 0, j == len(kts) - 1))
            run_blocks(blocks, 0.125)
            for b_ in range(2):
                av = psb(4 + b_)[:, 0:260].rearrange("p (a b) -> p a b", a=4)
                dn = den[b_]
                es_ = esink[:, 4 * b_:4 * b_ + 4].rearrange("p (a b) -> p a b", b=1)
                P("dve", lambda e, o=dn, a=av[:, :, 64:65], b=es_: e.tensor_tensor(out=o, in0=a, in1=b, op=ALU.add),
                  outs=[dn], ins=[av[:, :, 64:65], esink])
                P("dve", lambda e, o=dn: e.reciprocal(out=o, in_=o), outs=[dn], ins=[dn])
                o_ = ot[:, b_ * 256:(b_ + 1) * 256].rearrange("p (a b) -> p a b", a=4)
                P("dve", lambda e, o=o_, a=av[:, :, 0:64], r=dn: e.tensor_tensor(out=o, in0=a, in1=r.broadcast_to([128, 4, 64]), op=ALU.mult),
                  outs=[o_], ins=[av[:, :, 0:64], dn])
            if qt < 16:
                diff_finish(dstore[:, qt % 4, :, :], ot)
            else:
                blocks = []
                for g in range(8):
                    h = g // 2
                    ch, base = g // 3, (g % 3) * 32
                    kts = [16, 17]
                    acc = psb(6 + g // 4)[:, (g % 4) * 65:(g % 4) * 65 + 65]
                    for j, kt in enumerate(kts):
                        blocks.append((diffKT[base:base + 32, ch, kt * 128:(kt + 1) * 128], diffQT[base:base + 32, ch, qs], None,
                                       diffV[:, kt, h, :], [acc], j == 0, j == len(kts) - 1))
                run_blocks(blocks, float(32 ** -0.5))
                for b_ in range(2):
                    av = psb(6 + b_)[:, 0:260].rearrange("p (a b) -> p a b", a=4)
                    dn = den[b_]
                    P("dve", lambda e, o=dn, a=av[:, :, 64:65]: e.reciprocal(out=o, in_=a), outs=[dn], ins=[av[:, :, 64:65]])
                    tm_ = dctx[:, 4 * b_:4 * b_ + 4, :]
                    P("dve", lambda e, o=tm_, a=av[:, :, 0:64], r=dn: e.tensor_tensor(out=o, in0=a, in1=r.broadcast_to([128, 4, 64]), op=ALU.mult),
                      outs=[tm_], ins=[av[:, :, 0:64], dn])
                diff_finish(dctx, ot)
            _sel = os.environ.get("MK_SEL", "gsd")
            if "g" not in _sel:
                P("pool", lambda e, o=gla_out[:, qt, :]: e.memset(o, 0.0), outs=[gla_out[:, qt, :]])
            if "s" not in _sel:
                P("pool", lambda e, o=ot[:, 0:512]: e.memset(o, 0.0), outs=[ot[:, 0:512]])
            if "d" not in _sel:
                P("pool", lambda e, o=ot[:, 512:768]: e.memset(o, 0.0), outs=[ot[:, 512:768]])
            if qt == int(os.environ.get("MK_DUMP_QT", "3")):
                dump("otok", ot)
            tb_ = psb((state["bank"] % 2) * 2, 1024, BF16).rearrange("p (a b) -> p a b", a=8)
            for c in range(8):
                src = gla_out[:, qt, c * 128:(c + 1) * 128] if c < 2 else ot[:, (c - 2) * 128:(c - 1) * 128]
                P("pe", lambda e, o=tb_[:, c, :], i=src: e.transpose(out=o, in_=i, identity=ident_bf),
                  outs=[tb_[:, c, :]], ins=[src, ident_bf])
            sblk = qt % 4 if qt < 16 else qt - 16
            ob_ = oTblk[:, :, sblk * 128:(sblk + 1) * 128]
            P("act", lambda e, o=ob_, i=tb_: e.copy(out=o, in_=i), outs=[ob_], ins=[tb_])
            last_in_blk = (qt % 4 == 3) if qt < 16 else (qt == 17)
            if not last_in_blk:
                continue
            q0 = (qt // 4) * 512 if qt < 16 else S
            ntok = 512 if qt < 16 else C
            for c in range(8):
                xc = xcs[state["xc"] % 2][:, 0:ntok]
                state["xc"] += 1
                src = xsp_d[:, c, q0:q0 + ntok]
                P("sp", lambda e, o=xc, i=src: e.dma_start(out=o, in_=i), outs=[xc], ins=[src], dma=True)
                yb = psb(state["yb"] % 4, ntok)
                state["yb"] += 1
                for k in range(8):
                    P("pe", lambda e, o=yb, w_=wo_sb[:, k, c * 128:(c + 1) * 128], r=oTblk[:, k, 0:ntok], k=k:
                      e.matmul(o, lhsT=w_, rhs=r, start=(k == 0), stop=(k == 7)),
                      outs=[yb], ins=[wo_sb[:, k, c * 128:(c + 1) * 128], oTblk[:, k, 0:ntok]])
                gh_ = ghTs[l % 2][:, 1, c, which:which + 1]
                P("dve", lambda e, o=xc, y=yb, gh_=gh_: e.scalar_tensor_tensor(
                    out=o, in0=y, scalar=gh_, in1=o, op0=ALU.mult, op1=ALU.add),
                  outs=[xc], ins=[yb, xc, gh_])
                P("sp", lambda e, o=src, i=xc: e.dma_start(out=o, in_=i), outs=[src], ins=[xc], dma=True)
        for g0_ in range(0, T, 512):
            n_ = min(512, T - g0_)
            P("sp", lambda e, o=xT[:, :, g0_:g0_ + n_], i=xsp_d[:, :, g0_:g0_ + n_]: e.dma_start(out=o, in_=i),
              outs=[xT[:, :, g0_:g0_ + n_]], ins=[xsp_d[:, :, g0_:g0_ + n_]], dma=True)
        A.top = top0

    n_layers = L
    if dbg_stage == "ident":
        n_layers = 0
    for l in range(n_layers):
        LP["p"] = l % 2
        if l == 0:
            compute_mod(l)
        ffn(l, 0, w1i_d, w1o_d, [(0, 1152, [(0, 1152, 0)]), (1152, 1152, [(1152, 896, 0), (2048, 256, 1)])])
        if dbg_stage == "ffn1":
            break
        mixer(l)
        if dbg_stage == "mix":
            break
        last = (l == L - 1)
        if last:
            ffn(l, 2, w2i_d, w2o_d, [(0, 1024, [(0, 1024, 0)]), (1024, 1024, [(1024, 1024, 0)])])
        else:
            ffn(l, 2, w2i_d, w2o_d, [(0, 1152, [(0, 1152, 0)]), (1152, 1152, [(1152, 896, 0), (2048, 256, 1)])])

    tmp_top = A.top
    yT = A.alloc([128, 8, 512], F32, "yT")
    ob = [A.alloc([128, D], F32, "ob%d" % i) for i in range(2)]
    bi = 0
    for g in range(S // 512):
        rms_norm_to_hT(g * 512, 512, 0, 0, final=True, dst=yT)
        for tt in range(4):
            t = g * 4 + tt
            o_sb = ob[t % 2]
            for h in range(2):
                bank = bi % 6
                bi += 1
                pv = psb(bank).rearrange("p (a b) -> p a b", a=4)
                for c4 in range(4):
                    c = h * 4 + c4
                    src = yT[:, c, tt * 128:(tt + 1) * 128]
                    P("pe", lambda e, o=pv[:, c4, :], i=src: e.transpose(out=o, in_=i, identity=ident),
                      outs=[pv[:, c4, :]], ins=[src, ident])
                dst = o_sb[:, h * 512:(h + 1) * 512]
                pvf = psb(bank)
                if h == 0:
                    P("dve", lambda e, o=dst, i=pvf: e.tensor_copy(out=o, in_=i), outs=[dst], ins=[pvf])
                else:
                    P("act", lambda e, o=dst, i=pvf: e.copy(out=o, in_=i), outs=[dst], ins=[pvf])
            dd = out_d[t * 128:(t + 1) * 128, :]
            P("sp", lambda e, o=dd, i=o_sb: e.dma_start(out=o, in_=i), outs=[dd], ins=[o_sb], dma=True)
    A.top = tmp_top
    for e_ in ENGS:
        P(e_, None, ins=[out_d])

    with ExitStack() as st:
        sems = {e: st.enter_context(nc.semaphore("s_" + e)) for e in ENGS}
        dsems = {e: [st.enter_context(nc.semaphore("d_%s%d" % (e, i))) for i in range(NSLOT)] for e in ("sp", "pool", "act")}
        block = st.enter_context(nc.Block())
        sch.emit(nc, block, sems, dsems)
    es.close()
    return nc, sch


def _mix_cols():
    GQ, GK, GV, GF, GB, OG = 0, 128, 256, 512, 528, 544
    SQ, SK, SV = 800, 1312, 1440
    DQ, DK_, DV = 1568, 1824, 2080

    def rng(a, n):
        return list(range(a, a + n))
    perm64 = rng(16, 16) + rng(0, 16) + rng(48, 16) + rng(32, 16)
    perm32 = rng(8, 8) + rng(0, 8) + rng(24, 8) + rng(16, 8)

    def partner64(cl):
        return [cl[h * 64 + perm64[i]] for h in range(2) for i in range(64)]
    fm = [rng(GQ, 128), rng(GK, 128), rng(GF, 16) + rng(GB, 16) + [-1] * 96]
    swaq = [rng(SQ + c * 128, 128) for c in range(4)]
    fm += swaq
    fm += [partner64(c) for c in swaq]
    swak = [rng(SK, 64) * 2, rng(SK + 64, 64) * 2]
    fm += swak
    fm += [partner64(c) for c in swak]

    def dgroups(base):
        return [rng(base + h * 64 + w * 32, 32) for h in range(4) for w in range(2)]

    def dchunks(gr):
        out = []
        for gs in ((0, 1, 2), (3, 4, 5), (6, 7)):
            cc = []
            for g in gs:
                cc += gr[g]
            cc += [-1] * (128 - len(cc))
            out.append(cc)
        return out

    def partner32(gr):
        return [[g[perm32[i]] for i in range(32)] for g in gr]
    dq, dk = dgroups(DQ), dgroups(DK_)
    fm += dchunks(dq) + dchunks(partner32(dq)) + dchunks(dk) + dchunks(partner32(dk))
    assert len(fm) == NFM
    cols = []
    for c in fm:
        assert len(c) == 128
        cols += c
    cols += rng(GV, 256) + rng(OG, 256) + rng(GK, 128) + rng(SV, 128) + rng(DV, 256)
    return np.asarray(cols, np.int64)


def _rope_tables():
    rows = S // 64

    def tables(hd):
        half = hd // 2
        row = np.repeat(np.arange(rows, dtype=np.float32), 64)
        col = np.tile(np.arange(64, dtype=np.float32), rows)
        inv = (1.0 / (np.float32(10000.0) ** (np.arange(0, half, 2, dtype=np.float32) / np.float32(half)))).astype(np.float32)

        def ang(pos):
            a = (pos[:, None] * inv[None, :]).astype(np.float32)
            return np.concatenate([a, a], -1)
        an = np.concatenate([ang(row), ang(col)], -1)
        q = hd // 4
        sign = np.concatenate([-np.ones(q), np.ones(q), -np.ones(q), np.ones(q)]).astype(np.float32)
        return np.cos(an).astype(np.float32).T, (np.sin(an).astype(np.float32) * sign[None, :]).T
    c64, s64 = tables(64)
    c32, s32 = tables(32)
    out = np.zeros((128, 4, S), np.float32)
    out[:, 0] = np.tile(c64, (2, 1))
    out[:, 1] = np.tile(s64, (2, 1))
    out[:, 2] = np.tile(c32, (4, 1))
    out[:, 3] = np.tile(s32, (4, 1))
    return out


def _shared_inputs(inputs):
    f = np.float32
    d = {}
    wm = np.asarray(inputs["w_mod"], dtype=f).reshape(L, 8, 128, 72, 128)
    d["w_mod"] = np.ascontiguousarray(wm.transpose(0, 3, 2, 1, 4))
    bm = np.asarray(inputs["b_mod"], f)
    d["b_modT"] = np.ascontiguousarray(bm.reshape(L, 72, 128).transpose(0, 2, 1))
    gs = [inputs["g_ffn1"][0], inputs["g_mix"][0], inputs["g_ffn2"][0],
          inputs["g_ffn1"][1], inputs["g_mix"][1], inputs["g_ffn2"][1], inputs["g_final"]]
    g = np.stack([np.asarray(v, f) for v in gs], 0)
    d["gT"] = np.ascontiguousarray(g.reshape(7, 8, 128).transpose(2, 0, 1))
    d["w_out"] = np.ascontiguousarray(inputs["w_out"], dtype=f)
    for k in ("w_ffn1_in", "w_ffn2_in"):
        wi = np.asarray(inputs[k], f).reshape(L, 8, 128, 2, NJ, 128)
        d[k] = np.ascontiguousarray(wi.transpose(0, 4, 2, 1, 3, 5)).reshape(L, NJ, 128, 8, 256)
    for k in ("w_ffn1_out", "w_ffn2_out"):
        wo = np.asarray(inputs[k], f).reshape(L, NJ, 128, 8, 128)
        d[k] = np.ascontiguousarray(wo.transpose(0, 3, 2, 1, 4))
    cols = _mix_cols()
    w_in = np.asarray(inputs["w_in"], f)
    wz = np.concatenate([w_in, np.zeros((L, D, 1), f)], axis=-1)
    d["w_mix"] = np.ascontiguousarray(wz[:, :, cols])
    d["rope"] = _rope_tables()
    pm = np.zeros((128, 2, 128), f)
    perm64 = list(range(16, 32)) + list(range(0, 16)) + list(range(48, 64)) + list(range(32, 48))
    perm32 = list(range(8, 16)) + list(range(0, 8)) + list(range(24, 32)) + list(range(16, 24))
    for m in range(128):
        pm[(m // 64) * 64 + perm64[m % 64], 0, m] = 1.0
        pm[(m // 32) * 32 + perm32[m % 32], 1, m] = 1.0
    d["permT"] = pm
    rm = np.zeros((128, 4), f)
    for p_ in range(128):
        rm[p_, p_ // 32] = 1.0
    d["rmask"] = rm
    sm = np.concatenate([np.asarray(inputs["g_gla_norm"], f), np.asarray(inputs["g_diff_norm"], f),
                         np.asarray(inputs["swa_sink"], f), np.asarray(inputs["diff_lambda"], f).reshape(L, 128)], axis=-1)
    d["small_bc"] = np.ascontiguousarray(np.broadcast_to(sm[:, None, :], (L, 128, 648)))
    wg = np.asarray(inputs["w_gla_gate"], f)
    wup = np.zeros((L, 32, 2, 128), f)
    wup[:, 0:16, 0, :] = wg[:, 0]
    wup[:, 16:32, 1, :] = wg[:, 1]
    d["w_up"] = wup
    d["b_gate"] = np.ascontiguousarray(np.asarray(inputs["b_gla_gate"], f).reshape(L, 1, 256))
    return d


def _prep_inputs(inputs, core, shared):
    f = np.float32
    d = dict(shared)
    d["x"] = np.ascontiguousarray(inputs["x"][core], dtype=f)
    d["ctx"] = np.ascontiguousarray(inputs["ctx"][core], dtype=f)
    cc = np.stack([np.asarray(inputs["c"][core], f), np.asarray(inputs["c_ctx"], f)], axis=-1)
    d["cT"] = np.ascontiguousarray(cc.reshape(8, 128, 2).transpose(1, 0, 2))
    return d


_CACHE = {}


def kernel(**inputs):
    stage = os.environ.get("MK_STAGE", "full")
    if stage not in _CACHE:
        _CACHE[stage] = build(stage)[0]
    nc = _CACHE[stage]
    shared = _shared_inputs(inputs)
    in_maps = [_prep_inputs(inputs, core, shared) for core in range(8)]
    res = run_bass_kernel_spmd(nc, in_maps, core_ids=list(range(8)))
    out = np.stack([np.asarray(r["out"], dtype=np.float32) for r in res.results], axis=0)
    return out
```

```python
import math
import os
import numpy as np
import concourse.bass as bass
import concourse.mybir as mybir
from concourse.bass_utils import run_bass_kernel_spmd

F32 = mybir.dt.float32
BF16 = mybir.dt.bfloat16
ALU = mybir.AluOpType
AF = mybir.ActivationFunctionType
AX = mybir.AxisListType

D = 1024
S = 2048
C = 256
T = S + C
NT = T // 128
DFF = 2816
NJ = DFF // 128
L = 2
EPS = 1e-6

ENGS = ("pe", "act", "dve", "pool", "sp")
NSLOT = 8
NFM = 27
NMIX = NFM * 128 + 1024


def _esz(dt):
    return mybir.dt.size(dt)


class _Op:
    __slots__ = ("eng", "fn", "waits", "signal", "ev", "dma")


class Sched:
    def __init__(self):
        self.ops = {e: [] for e in ENGS}
        self.clock = {e: {} for e in ENGS}
        self.snap = {}
        self.opof = {}
        self.recs = {}
        self.slot_cnt = {e: [0] * NSLOT for e in ENGS}
        self.slot_rr = {e: 0 for e in ENGS}
        self.untracked = set()
        self.GR = 2048

    def boxes(self, ap):
        name = ap.tensor.name
        if name in self.untracked:
            return []
        es = _esz(ap.dtype)
        aps = list(ap.ap)
        off = ap.offset
        if str(ap.space) not in ("SB", "PSUM"):
            ext = sum((c - 1) * abs(s) for s, c in aps)
            return [(name, 0, 1, off * es, (off + ext + 1) * es)]
        pstep, pcnt = aps[0]
        p0 = off // pstep
        f0 = off % pstep
        free = aps[1:]
        out = []

        def rec(base, dims):
            if not dims:
                out.append((name, p0, p0 + pcnt, base * es, (base + 1) * es))
                return
            inner_ext = sum((c - 1) * abs(s) for s, c in dims[1:]) + 1
            s0, c0 = dims[0]
            if len(dims) > 1 and abs(s0) >= inner_ext and 1 < c0 <= 64 and s0 > 0:
                for i in range(c0):
                    rec(base + i * s0, dims[1:])
            else:
                ext = sum((c - 1) * abs(s) for s, c in dims) + 1
                out.append((name, p0, p0 + pcnt, base * es, (base + ext) * es))

        rec(f0, free)
        return out

    def _conf(self, b, kind, deps, eng):
        name, p0, p1, f0, f1 = b
        for g in range(f0 // self.GR, (f1 - 1) // self.GR + 1):
            for r in self.recs.get((name, g), ()):
                rb = r[0]
                if rb[1] < p1 and p0 < rb[2] and rb[3] < f1 and f0 < rb[4]:
                    rk = r[1]
                    if kind == "R" and rk == "R":
                        continue
                    if kind == "X" and rk == "X" and r[3] == eng:
                        continue
                    deps.add(r[2])

    def _reg(self, b, kind, ev, eng, dma):
        name, p0, p1, f0, f1 = b
        for g in range(f0 // self.GR, (f1 - 1) // self.GR + 1):
            lst = self.recs.setdefault((name, g), [])
            g0 = max(f0, g * self.GR)
            g1 = min(f1, (g + 1) * self.GR)
            keep = []
            for r in lst:
                rb = r[0]
                r0 = max(rb[3], g * self.GR)
                r1 = min(rb[4], (g + 1) * self.GR)
                contained = rb[1] >= p0 and rb[2] <= p1 and r0 >= g0 and r1 <= g1
                if contained:
                    if kind == "W":
                        continue
                    if kind == r[1] and r[3] == eng and not dma and not r[4]:
                        continue
                keep.append(r)
            keep.append((b, kind, ev, eng, dma))
            self.recs[(name, g)] = keep

    def add(self, eng, fn, outs=(), ins=(), dma=False):
        op = _Op()
        op.eng, op.fn, op.dma, op.signal = eng, fn, dma, dma
        idx = len(self.ops[eng])
        deps = set()
        acc = []
        for ap in ins:
            for b in self.boxes(ap):
                if b[0] == "ps":
                    acc.append(((b[0], 0, 128, (b[3] // 2048) * 2048, ((b[4] - 1) // 2048 + 1) * 2048), "X"))
                else:
                    acc.append((b, "R"))
        for ap in outs:
            for b in self.boxes(ap):
                if b[0] == "ps":
                    acc.append(((b[0], 0, 128, (b[3] // 2048) * 2048, ((b[4] - 1) // 2048 + 1) * 2048), "W"))
                else:
                    acc.append((b, "W"))
        for b, kind in acc:
            self._conf(b, kind, deps, eng)
        if dma:
            s = self.slot_rr[eng]
            self.slot_rr[eng] = (s + 1) % NSLOT
            k = self.slot_cnt[eng][s]
            if k > 0:
                deps.add(((eng, s), k))
            self.slot_cnt[eng][s] = k + 1
            ev = ((eng, s), k + 1)
        else:
            ev = (eng, idx + 1)
        op.ev = ev
        clk = self.clock[eng]
        waits = []
        for key, val in sorted(deps, key=lambda d: -d[1]):
            if eng == "pe" and key == "pe":
                continue
            if clk.get(key, 0) >= val:
                continue
            waits.append((key, val))
            self.opof[(key, val)].signal = True
            for k2, v2 in self.snap[(key, val)].items():
                if clk.get(k2, 0) < v2:
                    clk[k2] = v2
            clk[key] = max(clk.get(key, 0), val)
        op.waits = waits
        self.snap[ev] = dict(clk)
        self.opof[ev] = op
        for b, kind in acc:
            self._reg(b, kind, ev, eng, dma)
        self.ops[eng].append(op)
        return op

    def emit(self, nc, block, sems, dsems):
        rank = {}
        for e in ENGS:
            n = 0
            for i, op in enumerate(self.ops[e]):
                if op.signal and not op.dma:
                    n += 1
                    rank[(e, i + 1)] = n

        def run(eng_name, eng):
            for op in self.ops[eng_name]:
                for key, val in op.waits:
                    if isinstance(key, tuple):
                        eng.wait_ge(dsems[key[0]][key[1]], 16 * val)
                    else:
                        eng.wait_ge(sems[key], rank[(key, val)])
                if op.fn is None:
                    continue
                inst = op.fn(eng)
                if op.dma:
                    inst.then_inc(dsems[op.ev[0][0]][op.ev[0][1]], 16)
                elif op.signal:
                    inst.then_inc(sems[eng_name], 1)

        @block.tensor
        def _(e):
            run("pe", e)

        @block.scalar
        def _(e):
            run("act", e)

        @block.vector
        def _(e):
            run("dve", e)

        @block.gpsimd
        def _(e):
            run("pool", e)

        @block.sync
        def _(e):
            run("sp", e)


class Arena:
    def __init__(self, t, nbytes, base=0):
        self.t = t
        self.nbytes = base + nbytes
        self.base = base
        self.top = base

    def alloc(self, shape, dt, name=None):
        es = _esz(dt)
        n = 1
        for s in shape[1:]:
            n *= s
        nb = (n * es + 63) // 64 * 64
        off = self.top
        self.top += nb
        assert self.top <= self.nbytes, ("arena overflow", name, self.top)
        v = self.t[0:shape[0], off // 4:(off + nb) // 4]
        if dt != F32:
            v = v.bitcast(dt)
        v = v[:, 0:n]
        if len(shape) == 3:
            v = v.rearrange("p (a b) -> p a b", a=shape[1])
        elif len(shape) == 4:
            v = v.rearrange("p (a b c) -> p a b c", a=shape[1], b=shape[2])
        return v


def build(dbg_stage="full"):
    nc = bass.Bass("TRN2", target_bir_lowering=False)
    dram = {}

    def din(name, shape, dt=F32):
        dram[name] = nc.dram_tensor(name, list(shape), dt, kind="ExternalInput").ap()
        return dram[name]

    x_d = din("x", [S, D])
    ctx_d = din("ctx", [C, D])
    cT_d = din("cT", [128, 8, 2])
    wmod_d = din("w_mod", [L, 72, 128, 8, 128])
    bmodT_d = din("b_modT", [L, 128, 72])
    gT_d = din("gT", [128, 7, 8])
    w1i_d = din("w_ffn1_in", [L, NJ, 128, 8, 256])
    w1o_d = din("w_ffn1_out", [L, 8, 128, NJ, 128])
    w2i_d = din("w_ffn2_in", [L, NJ, 128, 8, 256])
    w2o_d = din("w_ffn2_out", [L, 8, 128, NJ, 128])
    wmix_d = din("w_mix", [L, D, NMIX])
    wout_d = din("w_out", [L, D, D])
    rope_d = din("rope", [128, 4, S])
    small_d = din("small_bc", [L, 128, 648])
    wup_d = din("w_up", [L, 32, 2, 128])
    bg_d = din("b_gate", [L, 1, 256])
    perm_d = din("permT", [128, 2, 128])
    rmask_d = din("rmask", [128, 4])
    xsp_d = nc.dram_tensor("xspill", [128, 8, T], F32, kind="Internal").ap()
    out_d = nc.dram_tensor("out", [S, D], F32, kind="ExternalOutput").ap()
    DUMP = os.environ.get("MK_DUMP", "")
    if DUMP:
        dbg_d = nc.dram_tensor("dbg", [128, 16384], F32, kind="ExternalOutput").ap()

    def dump(name, ap):
        if not DUMP or name != DUMP:
            return
        shp = list(ap.shape)
        n = 1
        for v_ in shp[1:]:
            n *= v_
        dv = dbg_d[0:shp[0], 0:n]
        if len(shp) == 3:
            dv = dv.rearrange("p (a b) -> p a b", a=shp[1])
        elif len(shp) == 4:
            dv = dv.rearrange("p (a b c) -> p a b c", a=shp[1], b=shp[2])
        sch.add("pool", lambda e, o=dv, i=ap: e.dma_start(out=o, in_=i), outs=[dv], ins=[ap], dma=True)

    sch = Sched()
    for n in ("x", "ctx", "cT", "w_mod", "b_modT", "gT", "w_ffn1_in", "w_ffn1_out", "w_ffn2_in", "w_ffn2_out",
              "w_mix", "w_out", "rope", "small_bc", "w_up", "b_gate", "permT", "rmask"):
        sch.untracked.add(n)

    ARENA_BYTES = 206 * 1024
    from contextlib import ExitStack
    es = ExitStack()
    arena_t = es.enter_context(nc.sbuf_tensor("arena", [128, ARENA_BYTES // 4], F32))
    ps_t = es.enter_context(nc.psum_tensor("ps", [128, 8, 512], F32))
    A = Arena(arena_t, ARENA_BYTES)

    def psb(b, n=512, dt=F32):
        v = ps_t[:, b, :]
        if dt != F32:
            v = v.bitcast(dt)
        return v[:, 0:n]

    xT = A.alloc([128, 8, T], F32, "xT")
    hT = A.alloc([128, 8, T], BF16, "hT")
    ident = A.alloc([128, 128], F32, "ident")
    ones_bf = A.alloc([128, 128], BF16, "ones_bf")
    ones_f = A.alloc([128, 128], F32, "ones_f")
    gT = A.alloc([128, 7, 8], F32, "gT")
    cT = A.alloc([128, 8, 2], F32, "cT")
    scT = A.alloc([128, 8, 2], BF16, "scT")
    modTs = [A.alloc([128, 72, 2], F32, "modT%d" % i) for i in range(2)]
    gsTs = [A.alloc([128, 3, 8, 2], F32, "gsT%d" % i) for i in range(2)]
    ghTs = [A.alloc([128, 3, 8, 2], F32, "ghT%d" % i) for i in range(2)]
    bmTs = [A.alloc([128, 72], F32, "bmT%d" % i) for i in range(2)]
    LP = {"p": 0}
    ident_bf = A.alloc([128, 128], BF16, "ident_bf")
    triA = [A.alloc([128, 128], F32, "triA%d" % i) for i in range(2)]
    triB = [A.alloc([128, 128], F32, "triB%d" % i) for i in range(2)]
    msk = [A.alloc([128, 1, 128], BF16, "msk%d" % i) for i in range(2)]
    bmask = A.alloc([128, 4, 64], F32, "bmask")
    bmask4 = A.alloc([128, 4, 128], BF16, "bmask4")
    PERSIST_TOP = A.top
    m16 = A.alloc([128, 128], F32, "m16")
    ones4 = A.alloc([128, 4, 128], F32, "ones4")
    XT_BYTES = 8 * T * 4
    HT_OFF = XT_BYTES
    HT_BYTES = 8 * T * 2

    P = sch.add

    P("pool", lambda e: e.memset(ident, 0.0), outs=[ident])
    P("pool", lambda e: e.memset(ones_f, 1.0), outs=[ones_f])
    P("pool", lambda e: e.affine_select(ident, ones_f, [[-1, 128]], ALU.is_equal, 0.0, base=0, channel_multiplier=1),
      outs=[ident], ins=[ones_f])
    P("pool", lambda e: e.memset(ones_bf, 1.0), outs=[ones_bf])
    P("dve", lambda e: e.tensor_copy(out=ident_bf, in_=ident), outs=[ident_bf], ins=[ident])
    P("pool", lambda e: e.memset(m16, -1.0 / 16.0), outs=[m16])
    P("pool", lambda e: e.memset(ones4, 1.0), outs=[ones4])
    P("pool", lambda e: e.affine_select(triA[0], m16, [[1, 128]], ALU.is_ge, 0.0, base=0, channel_multiplier=-1),
      outs=[triA[0]], ins=[m16])
    P("pool", lambda e: e.affine_select(triA[1], m16, [[-1, 128]], ALU.is_ge, 0.0, base=0, channel_multiplier=1),
      outs=[triA[1]], ins=[m16])
    P("pool", lambda e: e.affine_select(triB[0], m16, [[-1, 128]], ALU.is_gt, 0.0, base=0, channel_multiplier=1),
      outs=[triB[0]], ins=[m16])
    P("pool", lambda e: e.affine_select(triB[1], m16, [[1, 128]], ALU.is_gt, 0.0, base=0, channel_multiplier=-1),
      outs=[triB[1]], ins=[m16])
    P("pool", lambda e: e.affine_select(msk[0][:, 0, :], ones_f, [[1, 128]], ALU.is_ge, 0.0, base=0, channel_multiplier=-1),
      outs=[msk[0]], ins=[ones_f])
    P("pool", lambda e: e.affine_select(msk[1][:, 0, :], ones_f, [[-1, 128]], ALU.is_ge, 0.0, base=0, channel_multiplier=1),
      outs=[msk[1]], ins=[ones_f])
    tmpm = A.alloc([128, 4, 128], F32, "tmpm")
    P("pool", lambda e: e.affine_select(tmpm, ones4, [[-32, 4], [0, 128]], ALU.is_ge, 0.0, base=0, channel_multiplier=1),
      outs=[tmpm], ins=[ones4])
    P("pool", lambda e: e.affine_select(bmask4, tmpm, [[32, 4], [0, 128]], ALU.is_ge, 0.0, base=31, channel_multiplier=-1),
      outs=[bmask4], ins=[tmpm])
    P("pool", lambda e: e.affine_select(bmask, tmpm[:, :, 0:64], [[32, 4], [0, 64]], ALU.is_ge, 0.0, base=31, channel_multiplier=-1),
      outs=[bmask], ins=[tmpm])
    P("sp", lambda e: e.dma_start(out=gT, in_=gT_d), outs=[gT], dma=True)
    P("sp", lambda e: e.dma_start(out=cT, in_=cT_d), outs=[cT], dma=True)
    sig = A.alloc([128, 8, 2], F32, "sig")
    P("act", lambda e: e.activation(out=sig, in_=cT, func=AF.Silu), outs=[sig], ins=[cT])
    P("dve", lambda e: e.tensor_copy(out=scT, in_=sig), outs=[scT], ins=[sig])

    ldb = [A.alloc([128, D], F32, "ld%d" % i) for i in range(2)]
    for t in range(NT):
        src = x_d[t * 128:(t + 1) * 128, :] if t < 16 else ctx_d[(t - 16) * 128:(t - 15) * 128, :]
        lb = ldb[t % 2]
        P("sp", lambda e, lb=lb, src=src: e.dma_start(out=lb, in_=src), outs=[lb], dma=True)
        for h in range(2):
            bank = (t * 2 + h) % 8
            pv = psb(bank).rearrange("p (a b) -> p a b", a=4)
            for c4 in range(4):
                c = h * 4 + c4
                P("pe", lambda e, o=pv[:, c4, :], i=lb[:, c * 128:(c + 1) * 128]: e.transpose(out=o, in_=i, identity=ident),
                  outs=[pv[:, c4, :]], ins=[lb[:, c * 128:(c + 1) * 128], ident])
            dst = xT[:, h * 4:(h + 1) * 4, t * 128:(t + 1) * 128]
            if (t + h) % 2 == 0:
                P("dve", lambda e, o=dst, i=pv: e.tensor_copy(out=o, in_=i), outs=[dst], ins=[pv])
            else:
                P("act", lambda e, o=dst, i=pv: e.copy(out=o, in_=i), outs=[dst], ins=[pv])

    A.top = PERSIST_TOP
    def norm_bufs():
        return dict(sq=[A.alloc([128, 512], BF16) for _ in range(2)], rs=A.alloc([128, 512], F32),
                    t1=[A.alloc([128, 512], F32) for _ in range(2)])

    def norm_groups(tok0, ntok, gs_idx, which, nb, final=False, dst=None):
        sq, rs, t1b = nb["sq"], nb["rs"], nb["t1"]
        lp = LP["p"]

        def emit_group(g0, n, gi):
            bank = 7 - (gi % 2)
            acc = psb(bank, n)
            for c in range(8):
                s_ = sq[c % 2][:, 0:n]
                src = xT[:, c, g0:g0 + n]
                if c % 2 == 0:
                    P("act", lambda e, o=s_, i=src: e.activation(out=o, in_=i, func=AF.Square), outs=[s_], ins=[src])
                else:
                    P("pool", lambda e, o=s_, i=src: e.tensor_tensor(out=o, in0=i, in1=i, op=ALU.mult), outs=[s_], ins=[src])
                P("pe", lambda e, o=acc, r=s_, c=c: e.matmul(o, lhsT=ones_bf, rhs=r, start=(c == 0), stop=(c == 7)),
                  outs=[acc], ins=[ones_bf, s_])
            r = rs[:, 0:n]
            P("act", lambda e, o=r, i=acc: e.activation(out=o, in_=i, func=AF.Sqrt, bias=EPS, scale=1.0 / D), outs=[r], ins=[acc])
            P("dve", lambda e, o=r: e.reciprocal(out=o, in_=o), outs=[r], ins=[r])
            for c in range(8):
                src = xT[:, c, g0:g0 + n]
                if final:
                    o = dst[:, c, g0 - tok0:g0 - tok0 + n]
                    P("dve", lambda e, o=o, i=src, r=r, c=c: e.scalar_tensor_tensor(
                        out=o, in0=i, scalar=gT[:, 6, c:c + 1], in1=r, op0=ALU.mult, op1=ALU.mult), outs=[o], ins=[src, r, gT])
                else:
                    o = hT[:, c, g0:g0 + n]
                    t1 = t1b[c % 2][:, 0:n]
                    gs_ = gsTs[lp][:, gs_idx, c, which:which + 1]
                    P("dve", lambda e, o=t1, i=src, r=r, gs_=gs_: e.scalar_tensor_tensor(
                        out=o, in0=i, scalar=gs_, in1=r, op0=ALU.mult, op1=ALU.mult),
                      outs=[t1], ins=[src, r, gs_])
                    sh = modTs[lp][:, (3 * gs_idx) * 8 + c, which:which + 1]
                    P("act", lambda e, o=o, i=t1, sh=sh: e.activation(out=o, in_=i, func=AF.Identity, bias=sh, scale=1.0),
                      outs=[o], ins=[t1, sh])

        out = []
        g0 = tok0
        gi = 0
        while g0 < tok0 + ntok:
            n = min(512, tok0 + ntok - g0)
            out.append(lambda g0=g0, n=n, gi=gi: emit_group(g0, n, gi))
            g0 += n
            gi += 1
        return out

    def rms_norm_to_hT(tok0, ntok, gs_idx, which, final=False, dst=None):
        tmp_top = A.top
        nb = norm_bufs()
        for g in norm_groups(tok0, ntok, gs_idx, which, nb, final, dst):
            g()
        A.top = tmp_top

    def mod_emitters(l, wbufs, bank):
        mT, gS, gH, bm = modTs[l % 2], gsTs[l % 2], ghTs[l % 2], bmTs[l % 2]
        acc = psb(bank, 144).rearrange("p (a b) -> p a b", a=72)

        def chunk(fc):
            w = wbufs[fc % len(wbufs)]
            src = wmod_d[l, fc]
            P("pool", lambda e, o=w, i=src: e.dma_start(out=o, in_=i), outs=[w], dma=True)
            for k in range(8):
                P("pe", lambda e, o=acc[:, fc, :], w_=w[:, k, :], r=scT[:, k, :], k=k:
                  e.matmul(o, lhsT=w_, rhs=r, start=(k == 0), stop=(k == 7)),
                  outs=[acc[:, fc, :]], ins=[w[:, k, :], scT[:, k, :]])

        def fin():
            P("sp", lambda e: e.dma_start(out=bm, in_=bmodT_d[l]), outs=[bm], dma=True)
            for w_ in range(2):
                P("dve", lambda e, w_=w_: e.tensor_tensor(out=mT[:, :, w_], in0=acc[:, :, w_], in1=bm, op=ALU.add),
                  outs=[mT[:, :, w_]], ins=[acc[:, :, w_], bm])
            for n in range(3):
                for w_ in range(2):
                    sc = mT[:, (3 * n + 1) * 8:(3 * n + 2) * 8, w_]
                    o = gS[:, n, :, w_]
                    P("dve", lambda e, o=o, sc=sc, n=n: e.scalar_tensor_tensor(
                        out=o, in0=sc, scalar=1.0, in1=gT[:, 3 * l + n, :], op0=ALU.add, op1=ALU.mult),
                      outs=[o], ins=[sc, gT])
                    ga = mT[:, (3 * n + 2) * 8:(3 * n + 3) * 8, w_]
                    o2 = gH[:, n, :, w_]
                    P("dve", lambda e, o=o2, ga=ga, n=n: e.tensor_scalar(
                        out=o, in0=ga, scalar1=(1.0 if n == 1 else 0.5), scalar2=None, op0=ALU.mult),
                      outs=[o2], ins=[ga])

        return [(lambda fc=fc: chunk(fc)) for fc in range(72)] + [fin]

    def compute_mod(l):
        tmp_top = A.top
        wb = [A.alloc([128, 8, 128], BF16) for _ in range(4)]
        for em in mod_emitters(l, wb, 6):
            em()
        A.top = tmp_top

    def ffn(l, n_idx, w_in_d, w_out_d, groups):
        tmp_top = A.top
        aT = A.alloc([128, NJ, 1152], BF16, "aT")
        wi = [A.alloc([128, 8, 256], BF16, "wi%d" % i) for i in range(3)]
        wo = [A.alloc([128, NJ, 128], BF16, "wo%d" % i) for i in range(3)]
        sg = [A.alloc([128, 384], F32, "sg%d" % i) for i in range(2)]
        nbuf = norm_bufs()
        for pi, (p0, pn, segs) in enumerate(groups):
            if pi == 0:
                for (t0, tn, which) in segs:
                    for g in norm_groups(t0, tn, n_idx, which, nbuf):
                        g()
            nxt_norm = []
            if pi + 1 < len(groups):
                for (t0, tn, which) in groups[pi + 1][2]:
                    nxt_norm += norm_groups(t0, tn, n_idx, which, nbuf)
            regs = []
            r0 = p0
            while r0 < p0 + pn:
                rn = min(384, p0 + pn - r0)
                regs.append((r0, rn))
                r0 += rn
            assert len(regs) <= 3
            for j in range(NJ):
                w = wi[j % 3]
                P("pool", lambda e, o=w, i=w_in_d[l, j]: e.dma_start(out=o, in_=i), outs=[w], dma=True)
                for ri, (r0, rn) in enumerate(regs):
                    pg = psb(ri, rn)
                    pu = psb(3 + ri, rn)
                    for k in range(8):
                        P("pe", lambda e, o=pg, w_=w[:, k, 0:128], r=hT[:, k, r0:r0 + rn], k=k:
                          e.matmul(o, lhsT=w_, rhs=r, start=(k == 0), stop=(k == 7)),
                          outs=[pg], ins=[w[:, k, 0:128], hT[:, k, r0:r0 + rn]])
                    for k in range(8):
                        P("pe", lambda e, o=pu, w_=w[:, k, 128:256], r=hT[:, k, r0:r0 + rn], k=k:
                          e.matmul(o, lhsT=w_, rhs=r, start=(k == 0), stop=(k == 7)),
                          outs=[pu], ins=[w[:, k, 128:256], hT[:, k, r0:r0 + rn]])
                    s = sg[(j * 3 + ri) % 2][:, 0:rn]
                    P("act", lambda e, o=s, i=pg: e.activation(out=o, in_=i, func=AF.Silu), outs=[s], ins=[pg])
                    o = aT[:, j, r0 - p0:r0 - p0 + rn]
                    P("dve", lambda e, o=o, a=pu, b=s: e.tensor_tensor(out=o, in0=a, in1=b, op=ALU.mult), outs=[o], ins=[pu, s])
            bi = 0
            for c in range(8):
                w = wo[c % 3]
                src = w_out_d[l, c]
                P("pool", lambda e, o=w, i=src: e.dma_start(out=o, in_=i), outs=[w], dma=True)
                for (r0, rn) in regs:
                    py = psb(bi % 6, rn)
                    bi += 1
                    for j in range(NJ):
                        P("pe", lambda e, o=py, w_=w[:, j, :], r=aT[:, j, r0 - p0:r0 - p0 + rn], j=j:
                          e.matmul(o, lhsT=w_, rhs=r, start=(j == 0), stop=(j == NJ - 1)),
                          outs=[py], ins=[w[:, j, :], aT[:, j, r0 - p0:r0 - p0 + rn]])
                    for (t0, tn, which) in segs:
                        a0 = max(t0, r0)
                        a1 = min(t0 + tn, r0 + rn)
                        if a1 <= a0:
                            continue
                        xs = xT[:, c, a0:a1]
                        ys = py[:, a0 - r0:a1 - r0]
                        gh_ = ghTs[LP["p"]][:, n_idx, c, which:which + 1]
                        P("dve", lambda e, xs=xs, ys=ys, gh_=gh_: e.scalar_tensor_tensor(
                            out=xs, in0=ys, scalar=gh_, in1=xs, op0=ALU.mult, op1=ALU.add),
                          outs=[xs], ins=[ys, xs, gh_])
                if nxt_norm and c % 2 == 1:
                    nxt_norm.pop(0)()
            while nxt_norm:
                nxt_norm.pop(0)()
        A.top = tmp_top

    def mixer(l):
        ctx_out = (l < L - 1)
        lam_init = 0.8 - 0.6 * math.exp(-0.3 * l)
        top0 = A.top
        out_tiles = list(range(NT)) if ctx_out else list(range(16))
        tmp_top_n = A.top
        nb_ = norm_bufs()
        for (t0_, tn_, wh_) in ((0, S, 0), (S, C, 1)):
            g0_ = t0_
            for gfn in norm_groups(t0_, tn_, 1, wh_, nb_):
                gfn()
                n_ = min(512, t0_ + tn_ - g0_)
                P("sp", lambda e, o=xsp_d[:, :, g0_:g0_ + n_], i=xT[:, :, g0_:g0_ + n_]: e.dma_start(out=o, in_=i),
                  outs=[xsp_d[:, :, g0_:g0_ + n_]], ins=[xT[:, :, g0_:g0_ + n_]], dma=True)
                g0_ += n_
        A.top = tmp_top_n
        AXa = Arena(arena_t, XT_BYTES, 0)
        AH = Arena(arena_t, HT_BYTES, HT_OFF)
        glaQT = AXa.alloc([128, 1, T], BF16)
        glaKT = AXa.alloc([128, 1, T], BF16)
        gfbT = AXa.alloc([128, 1, T], BF16)
        swaQT = AXa.alloc([128, 4, T], BF16)
        swaKT = AXa.alloc([128, 2, T], BF16)
        diffQT = AXa.alloc([128, 3, T], BF16)
        diffKT = AXa.alloc([128, 3, T], BF16)
        glaTM = A.alloc([128, NT, 640], BF16)
        swaV = A.alloc([128, NT, 2, 65], BF16)
        diffV = A.alloc([128, NT, 4, 65], BF16)
        small = A.alloc([128, 648], F32)
        wup = A.alloc([32, 2, 128], BF16)
        bg = A.alloc([1, 256], BF16)
        esink = A.alloc([128, 8], F32)
        nlam = A.alloc([128, 1], F32)
        gdiff = A.alloc([128, 4, 64], F32)
        lamt = A.alloc([128, 2, 32], F32)
        lams = A.alloc([128, 2], F32)
        mix_top = A.top
        ropeT = A.alloc([128, 4, S], BF16)
        P("pool", lambda e: e.dma_start(out=ropeT, in_=rope_d), outs=[ropeT], dma=True)
        P("sp", lambda e: e.dma_start(out=small, in_=small_d[l]), outs=[small], dma=True)
        P("pool", lambda e: e.dma_start(out=wup, in_=wup_d[l]), outs=[wup], dma=True)
        P("pool", lambda e: e.dma_start(out=bg, in_=bg_d[l]), outs=[bg], dma=True)
        P("pool", lambda e: e.memset(swaV[:, :, :, 64:65], 1.0), outs=[swaV[:, :, :, 64:65]])
        P("pool", lambda e: e.memset(diffV[:, :, :, 64:65], 1.0), outs=[diffV[:, :, :, 64:65]])
        P("act", lambda e: e.activation(out=esink, in_=small[:, 512:520], func=AF.Exp), outs=[esink], ins=[small])
        dl = small[:, 520:648].rearrange("p (a b) -> p a b", a=4)
        P("dve", lambda e: e.tensor_tensor(out=lamt[:, 0, :], in0=dl[:, 0, :], in1=dl[:, 1, :], op=ALU.mult),
          outs=[lamt[:, 0, :]], ins=[small])
        P("dve", lambda e: e.tensor_tensor(out=lamt[:, 1, :], in0=dl[:, 2, :], in1=dl[:, 3, :], op=ALU.mult),
          outs=[lamt[:, 1, :]], ins=[small])
        P("dve", lambda e: e.reduce_sum(out=lams, in_=lamt, axis=AX.X), outs=[lams], ins=[lamt])
        P("act", lambda e: e.activation(out=lams, in_=lams, func=AF.Exp), outs=[lams], ins=[lams])
        P("dve", lambda e: e.scalar_tensor_tensor(out=nlam, in0=lams[:, 1:2], scalar=-lam_init, in1=lams[:, 0:1],
                                                  op0=ALU.add, op1=ALU.subtract), outs=[nlam], ins=[lams])
        P("dve", lambda e: e.tensor_scalar(out=gdiff, in0=small[:, 256:512].rearrange("p (a b) -> p a b", a=4),
                                           scalar1=1.0 - lam_init, scalar2=None, op0=ALU.mult), outs=[gdiff], ins=[small])

        wfm = [A.alloc([128, 2, 8, 128], BF16) for _ in range(2)]
        permT = A.alloc([128, 2, 128], BF16)
        xb = [A.alloc([128, 512], BF16) for _ in range(2)]
        P("pool", lambda e: e.dma_start(out=permT, in_=perm_d), outs=[permT], dma=True)
        wtm = [A.alloc([128, 8, 512], BF16) for _ in range(2)]
        rt = [A.alloc([128, 512], F32) for _ in range(2)]
        fm = [(0, None, glaQT[:, 0, :], None), (1, None, glaKT[:, 0, :], None), (2, None, gfbT[:, 0, :], None)]
        for c in range(4):
            fm.append((3 + c, 7 + c, swaQT[:, c, :], 0))
        for c in range(2):
            fm.append((11 + c, 13 + c, swaKT[:, c, :], 0))
        for c in range(3):
            fm.append((15 + c, 18 + c, diffQT[:, c, :], 2))
        for c in range(3):
            fm.append((21 + c, 24 + c, diffKT[:, c, :], 2))
        groups = [(0, 512), (512, 512), (1024, 512), (1536, 512), (2048, 256)]
        gi = 0
        def load_fm(fi):
            cm, cp, dest, tb = fm[fi]
            w = wfm[fi % 2]
            src = wmix_d[l, :, cm * 128:(cm + 1) * 128].rearrange("(kc p) f -> p kc f", p=128)
            P("pool", lambda e, o=w[:, 0, :, :], i=src: e.dma_start(out=o, in_=i), outs=[w[:, 0, :, :]], dma=True)

        load_fm(0)
        for g in range(2):
            src = wmix_d[l, :, NFM * 128 + g * 512:NFM * 128 + (g + 1) * 512].rearrange("(kc p) f -> p kc f", p=128)
            P("pool", lambda e, o=wtm[g], i=src: e.dma_start(out=o, in_=i), outs=[wtm[g]], dma=True)
        for fi, (cm, cp, dest, tb) in enumerate(fm):
            w = wfm[fi % 2]
            if fi + 1 < len(fm):
                load_fm(fi + 1)
            for (g0, n) in groups:
                b0 = (2 * gi) % 4
                gi += 1
                pm = psb(b0, n)
                pp = psb(b0 + 1, n)
                rope = (cp is not None) and g0 < S
                for k in range(8):
                    P("pe", lambda e, o=pm, w_=w[:, 0, k, :], r=hT[:, k, g0:g0 + n], k=k:
                      e.matmul(o, lhsT=w_, rhs=r, start=(k == 0), stop=(k == 7)),
                      outs=[pm], ins=[w[:, 0, k, :], hT[:, k, g0:g0 + n]])
                if rope:
                    xb_ = xb[gi % 2][:, 0:n]
                    P("act", lambda e, o=xb_, i=pm: e.copy(out=o, in_=i), outs=[xb_], ins=[pm])
                    pmx = permT[:, (0 if tb == 0 else 1), :]
                    P("pe", lambda e, o=pp, w_=pmx, r=xb_: e.matmul(o, lhsT=w_, rhs=r, start=True, stop=True),
                      outs=[pp], ins=[pmx, xb_])
                    t1 = rt[0][:, 0:n]
                    t2 = rt[1][:, 0:n]
                    cs = ropeT[:, tb, g0:g0 + n]
                    sn = ropeT[:, tb + 1, g0:g0 + n]
                    P("dve", lambda e, o=t1, a=pm, b=cs: e.tensor_tensor(out=o, in0=a, in1=b, op=ALU.mult), outs=[t1], ins=[pm, cs])
                    P("dve", lambda e, o=t2, a=pp, b=sn: e.tensor_tensor(out=o, in0=a, in1=b, op=ALU.mult), outs=[t2], ins=[pp, sn])
                    d_ = dest[:, g0:g0 + n]
                    P("pool", lambda e, o=d_, a=t1, b=t2: e.tensor_tensor(out=o, in0=a, in1=b, op=ALU.add), outs=[d_], ins=[t1, t2])
                else:
                    d_ = dest[:, g0:g0 + n]
                    P("act", lambda e, o=d_, i=pm: e.copy(out=o, in_=i), outs=[d_], ins=[pm])
        for t in range(NT):
            for g in range(2):
                pb = psb(4 + (t * 2 + g) % 4)
                for k in range(8):
                    P("pe", lambda e, o=pb, a=hT[:, k, t * 128:(t + 1) * 128], w_=wtm[g][:, k, :], k=k:
                      e.matmul(o, lhsT=a, rhs=w_, start=(k == 0), stop=(k == 7)),
                      outs=[pb], ins=[hT[:, k, t * 128:(t + 1) * 128], wtm[g][:, k, :]])
                if g == 0:
                    d_ = glaTM[:, t, 0:512]
                    P("act", lambda e, o=d_, i=pb: e.copy(out=o, in_=i), outs=[d_], ins=[pb])
                else:
                    d_ = glaTM[:, t, 512:640]
                    P("dve", lambda e, o=d_, i=pb[:, 0:128]: e.tensor_copy(out=o, in_=i), outs=[d_], ins=[pb[:, 0:128]])
                    d2 = swaV[:, t, :, 0:64]
                    s2 = pb[:, 128:256].rearrange("p (a b) -> p a b", a=2)
                    P("act", lambda e, o=d2, i=s2: e.copy(out=o, in_=i), outs=[d2], ins=[s2])
                    d3 = diffV[:, t, :, 0:64]
                    s3 = pb[:, 256:512].rearrange("p (a b) -> p a b", a=4)
                    P("dve", lambda e, o=d3, i=s3: e.tensor_copy(out=o, in_=i), outs=[d3], ins=[s3])
        A.top = mix_top
        for nm_, ap_ in (("glaQT", glaQT), ("glaKT", glaKT), ("gfbT", gfbT), ("swaQT", swaQT), ("swaKT", swaKT),
                         ("diffQT", diffQT), ("diffKT", diffKT), ("glaTM", glaTM), ("swaV", swaV), ("diffV", diffV), ("hT", hT)):
            dump(nm_, ap_)

        _cut = os.environ.get("MK_MIXCUT", "")
        if _cut == "inproj":
            P("sp", lambda e: e.dma_start(out=xT, in_=xsp_d), outs=[xT], ins=[xsp_d], dma=True)
            A.top = top0
            return
        esp = AH.alloc([128, 2, NT, 128], F32)
        ost = AH.alloc([128, NT, 256], F32)
        gla_out = A.alloc([128, NT, 256], BF16)
        gla_top = A.top
        bi = 0
        for d_ in range(2):
            for t0 in range(0, NT, 4):
                nb = min(4, NT - t0)
                pb = psb(bi % 4).rearrange("p (a b) -> p a b", a=4)
                bi += 1
                for s_ in range(nb):
                    t = t0 + s_
                    P("pe", lambda e, o=pb[:, s_, :], a=gfbT[0:32, 0, t * 128:(t + 1) * 128], w_=wup[:, d_, :]:
                      e.matmul(o, lhsT=a, rhs=w_, start=True, stop=False),
                      outs=[pb[:, s_, :]], ins=[gfbT[0:32, 0, t * 128:(t + 1) * 128], wup[:, d_, :]])
                    P("pe", lambda e, o=pb[:, s_, :], a=ones_bf[0:1, :], w_=bg[0:1, d_ * 128:(d_ + 1) * 128]:
                      e.matmul(o, lhsT=a, rhs=w_, start=False, stop=True),
                      outs=[pb[:, s_, :]], ins=[ones_bf[0:1, :], bg[0:1, d_ * 128:(d_ + 1) * 128]])
                o_ = esp[:, d_, t0:t0 + nb, :]
                P("act", lambda e, o=o_, i=pb[:, 0:nb, :]: e.activation(out=o, in_=i, func=AF.Exp, scale=-1.0),
                  outs=[o_], ins=[pb[:, 0:nb, :]])
        for d_ in range(2):
            P("act", lambda e, o=esp[:, d_, :, :]: e.activation(out=o, in_=o, func=AF.Ln, bias=1.0, scale=1.0),
              outs=[esp[:, d_, :, :]], ins=[esp[:, d_, :, :]])
        E1 = [A.alloc([128, 128], F32) for _ in range(2)]
        E2 = [A.alloc([128, 128], F32) for _ in range(2)]
        E3 = [A.alloc([128, 128], F32) for _ in range(2)]
        KtT = [A.alloc([128, 128], BF16) for _ in range(2)]
        Kh = [A.alloc([128, 128], BF16) for _ in range(2)]
        Qexp = [A.alloc([128, 4, 128], BF16) for _ in range(2)]
        QtT = [[A.alloc([128, 1, 128], BF16) for _ in range(2)] for _ in range(2)]
        attm = [[A.alloc([128, 4, 128], BF16) for _ in range(2)] for _ in range(2)]
        Um = [[A.alloc([128, 4, 64], F32) for _ in range(2)] for _ in range(2)]
        decs = [[A.alloc([128, 1], F32) for _ in range(2)] for _ in range(2)]
        Sf = [A.alloc([128, 4, 64], F32) for _ in range(2)]
        Sb = [[A.alloc([128, 4, 64], BF16) for _ in range(2)] for _ in range(2)]
        orders = [[16, 17] + list(range(16)), [17, 16] + list(range(15, -1, -1))]
        visited = set()
        for d_ in range(2):
            P("pool", lambda e, d_=d_: e.memset(Sf[d_], 0.0), outs=[Sf[d_]])
            P("pool", lambda e, d_=d_: e.memset(Sb[d_][0], 0.0), outs=[Sb[d_][0]])

        def gla_prep(st, d_):
            t = orders[d_][st]
            par = st % 2
            sp_t = esp[:, d_, t, :]
            bA = psb(3 * d_)
            pc, pr, pu = bA[:, 0:128], bA[:, 128:256], bA[:, 256:512]
            P("pe", lambda e, o=pc, a=sp_t, b=triA[d_]: e.matmul(o, lhsT=a, rhs=b, start=True, stop=True),
              outs=[pc], ins=[sp_t, triA[d_]])
            P("pe", lambda e, o=pr, a=triB[d_], b=sp_t: e.matmul(o, lhsT=a, rhs=b, start=True, stop=True),
              outs=[pr], ins=[triB[d_], sp_t])
            e1, e2, e3 = E1[d_], E2[d_], E3[d_]
            P("act", lambda e, o=e3, i=pr: e.activation(out=o, in_=i, func=AF.Exp), outs=[e3], ins=[pr])
            P("act", lambda e, o=e1, i=pc: e.activation(out=o, in_=i, func=AF.Exp), outs=[e1], ins=[pc])
            if t in out_tiles:
                P("act", lambda e, o=e2, i=pc: e.activation(out=o, in_=i, func=AF.Exp, scale=-1.0), outs=[e2], ins=[pc])
            kh = Kh[d_]
            P("pool", lambda e, o=kh, a=glaTM[:, t, 512:640], b=e3: e.tensor_tensor(out=o, in0=a, in1=b, op=ALU.mult),
              outs=[kh], ins=[glaTM[:, t, 512:640], e3])
            P("pe", lambda e, o=pu, a=kh, b=glaTM[:, t, 0:256]: e.matmul(o, lhsT=a, rhs=b, start=True, stop=True),
              outs=[pu], ins=[kh, glaTM[:, t, 0:256]])
            dsrc = e1[:, 127:128] if d_ == 0 else e1[:, 0:1]
            dc = decs[d_][par]
            P("pool", lambda e, o=dc, i=dsrc: e.tensor_copy(out=o, in_=i), outs=[dc], ins=[dsrc])
            um = Um[d_][par]
            P("dve", lambda e, o=um, a=pu.rearrange("p (a b) -> p a b", a=4): e.tensor_tensor(out=o, in0=a, in1=bmask, op=ALU.mult),
              outs=[um], ins=[pu, bmask])
            if t in out_tiles:
                qt_, kt_ = QtT[d_][par], KtT[d_]
                tq = glaQT[:, :, t * 128:(t + 1) * 128]
                P("dve", lambda e, o=qt_, a=tq, b=e1: e.scalar_tensor_tensor(
                    out=o[:, 0, :], in0=a[:, 0, :], scalar=float(32 ** -0.5), in1=b, op0=ALU.mult, op1=ALU.mult),
                  outs=[qt_], ins=[tq, e1])
                tk = glaKT[:, 0, t * 128:(t + 1) * 128]
                P("dve", lambda e, o=kt_, a=tk, b=e2: e.tensor_tensor(out=o, in0=a, in1=b, op=ALU.mult), outs=[kt_], ins=[tk, e2])
                qe = Qexp[d_]
                P("dve", lambda e, o=qe, a=qt_: e.tensor_tensor(out=o, in0=bmask4, in1=a.broadcast_to([128, 4, 128]), op=ALU.mult),
                  outs=[qe], ins=[bmask4, qt_])
                pa = psb(3 * d_ + 1)
                P("pe", lambda e, o=pa, a=kt_, b=qe: e.matmul(o, lhsT=a, rhs=b.rearrange("p a b -> p (a b)"), start=True, stop=True),
                  outs=[pa], ins=[kt_, qe])
                am = attm[d_][par]
                P("dve", lambda e, o=am, a=pa.rearrange("p (a b) -> p a b", a=4), m_=msk[d_]:
                  e.tensor_tensor(out=o, in0=a, in1=m_.broadcast_to([128, 4, 128]), op=ALU.mult), outs=[am], ins=[pa, msk[d_]])

        def gla_chain(st, d_):
            t = orders[d_][st]
            par = st % 2
            cur, nxt = st % 2, (st + 1) % 2
            sb_cur, sb_nxt = Sb[d_][cur], Sb[d_][nxt]
            if t in out_tiles:
                qt_, am = QtT[d_][par], attm[d_][par]
                po = psb(3 * d_ + 2, 256)
                for h in range(4):
                    oh = po[:, h * 64:(h + 1) * 64]
                    P("pe", lambda e, o=oh, a=qt_[:, 0, :], b=sb_cur[:, h, :]: e.matmul(o, lhsT=a, rhs=b, start=True, stop=False),
                      outs=[oh], ins=[qt_, sb_cur[:, h, :]])
                    P("pe", lambda e, o=oh, a=am[:, h, :], b=glaTM[:, t, h * 64:(h + 1) * 64]: e.matmul(o, lhsT=a, rhs=b, start=False, stop=True),
                      outs=[oh], ins=[am[:, h, :], glaTM[:, t, h * 64:(h + 1) * 64]])
                if t not in visited:
                    visited.add(t)
                    P("act", lambda e, o=ost[:, t, :], i=po: e.copy(out=o, in_=i), outs=[ost[:, t, :]], ins=[po])
                else:
                    P("dve", lambda e, o=ost[:, t, :], i=po: e.tensor_tensor(out=o, in0=i, in1=o, op=ALU.add),
                      outs=[ost[:, t, :]], ins=[po, ost[:, t, :]])
            P("dve", lambda e, o=Sf[d_], sc=decs[d_][par], b=Um[d_][par]: e.scalar_tensor_tensor(
                out=o, in0=o, scalar=sc, in1=b, op0=ALU.mult, op1=ALU.add), outs=[Sf[d_]], ins=[Sf[d_], decs[d_][par], Um[d_][par]])
            P("act", lambda e, o=sb_nxt, i=Sf[d_]: e.copy(out=o, in_=i), outs=[sb_nxt], ins=[Sf[d_]])

        mod_ems = []
        if l + 1 < L and dbg_stage == "full":
            wb2 = [A.alloc([128, 8, 128], BF16) for _ in range(2)]
            mod_ems = mod_emitters(l + 1, wb2, 6)
        for st in range(NT + 1):
            for d_ in range(2):
                if st < NT:
                    gla_prep(st, d_)
            for d_ in range(2):
                if st >= 1:
                    gla_chain(st - 1, d_)
            for _ in range(4):
                if len(mod_ems) > 1:
                    mod_ems.pop(0)()
        while mod_ems:
            mod_ems.pop(0)()
        A.top = gla_top
        no = len(out_tiles)
        ssg = A.alloc([128, NT, 4, 1], F32)
        sqg = [A.alloc([128, 4, 64], F32) for _ in range(2)]
        sgl = [A.alloc([128, 256], F32) for _ in range(2)]
        for t in out_tiles:
            o4 = ost[:, t, :].rearrange("p (a b) -> p a b", a=4)
            q_ = sqg[t % 2]
            P("pool", lambda e, o=q_, a=o4: e.tensor_tensor(out=o, in0=a, in1=a, op=ALU.mult), outs=[q_], ins=[o4])
            P("dve", lambda e, o=ssg[:, t, :, 0], i=q_: e.reduce_sum(out=o, in_=i, axis=AX.X), outs=[ssg[:, t, :, :]], ins=[q_])
        sv = ssg[:, 0:no, :, :]
        P("act", lambda e, o=sv: e.activation(out=o, in_=o, func=AF.Sqrt, bias=EPS, scale=1.0 / 64.0), outs=[sv], ins=[sv])
        P("dve", lambda e, o=sv: e.reciprocal(out=o, in_=o), outs=[sv], ins=[sv])
        ggla = small[:, 0:256].rearrange("p (a b) -> p a b", a=4)
        for t in out_tiles:
            o4 = ost[:, t, :].rearrange("p (a b) -> p a b", a=4)
            q_ = sqg[t % 2]
            sg_ = sgl[t % 2]
            P("act", lambda e, o=sg_, i=glaTM[:, t, 256:512]: e.activation(out=o, in_=i, func=AF.Silu), outs=[sg_], ins=[glaTM[:, t, 256:512]])
            P("dve", lambda e, o=q_, a=o4, r=ssg[:, t, :, :]: e.tensor_tensor(out=o, in0=a, in1=r.broadcast_to([128, 4, 64]), op=ALU.mult),
              outs=[q_], ins=[o4, ssg[:, t, :, :]])
            P("pool", lambda e, o=q_: e.tensor_tensor(out=o, in0=o, in1=ggla, op=ALU.mult), outs=[q_], ins=[q_, small])
            go = gla_out[:, t, :]
            P("dve", lambda e, o=go, a=q_, b=sg_: e.tensor_tensor(out=o, in0=a.rearrange("p a b -> p (a b)"), in1=b, op=ALU.mult),
              outs=[go], ins=[q_, sg_])
        A.top = gla_top

        dump("gla_out", gla_out)
        dump("ost", ost)
        if _cut == "gla":
            P("sp", lambda e: e.dma_start(out=xT, in_=xsp_d), outs=[xT], ins=[xsp_d, gla_out], dma=True)
            A.top = top0
            return
        AH2 = Arena(arena_t, HT_BYTES, HT_OFF)
        wo_sb = AH2.alloc([128, 8, D], BF16)
        P("pool", lambda e: e.dma_start(out=wo_sb, in_=wout_d[l].rearrange("(kc p) f -> p kc f", p=128)), outs=[wo_sb], dma=True)
        pT = [AH2.alloc([128, 8, 128], BF16) for _ in range(3)]
        otok = [A.alloc([128, 768], BF16) for _ in range(2)]
        dstore = AH2.alloc([128, 4, 8, 64], F32)
        rmask = A.alloc([128, 4], F32)
        qmb = [A.alloc([128, 512], BF16) for _ in range(2)]
        P("sp", lambda e: e.dma_start(out=rmask, in_=rmask_d), outs=[rmask], dma=True)
        oTblk = A.alloc([128, 8, 512], BF16)
        xcs = [A.alloc([128, 512], F32) for _ in range(2)]
        den = [AH2.alloc([128, 4, 1], F32) for _ in range(2)]
        dctx = AH2.alloc([128, 8, 64], F32)
        od = AH2.alloc([128, 4, 64], F32)
        od2 = AH2.alloc([128, 4, 64], F32)
        ssd = AH2.alloc([128, 4, 1], F32)
        state = {"bank": 0, "pt": 0, "xc": 0, "yb": 0}
        dump("wo_sb", wo_sb[:, :, 0:2048] if False else wo_sb)

        def run_blocks(blocks, scale):
            batches = []
            for blk in blocks:
                rg = blk[0].base_partition()
                w = blk[1].shape[-1]
                if (batches and batches[-1][0][0].base_partition() == rg and batches[-1][0][1].shape[-1] == w
                        and (len(batches[-1]) + 1) * w <= 1024):
                    batches[-1].append(blk)
                else:
                    batches.append([blk])
            pend = None
            for bl in batches + [None]:
                cur = None
                if bl is not None:
                    w = bl[0][1].shape[-1]
                    ncol = len(bl) * w
                    b2 = (state["bank"] % 2) * 2
                    state["bank"] += 1
                    bank = ps_t[:, b2:b2 + 2, :].rearrange("p a c -> p (a c)")
                    p_ = pT[state["pt"] % 3].rearrange("p a c -> p (a c)")
                    state["pt"] += 1
                    for i, (kT, qT, m_, v, accs, first, last) in enumerate(bl):
                        P("pe", lambda e, o=bank[:, i * w:(i + 1) * w], a=kT, b=qT: e.matmul(o, lhsT=a, rhs=b, start=True, stop=True),
                          outs=[bank[:, i * w:(i + 1) * w]], ins=[kT, qT])
                    P("act", lambda e, o=p_[:, 0:ncol], i=bank[:, 0:ncol]: e.activation(out=o, in_=i, func=AF.Exp, scale=scale),
                      outs=[p_[:, 0:ncol]], ins=[bank[:, 0:ncol]])
                    for i, (kT, qT, m_, v, accs, first, last) in enumerate(bl):
                        if m_ is not None:
                            P("dve", lambda e, o=p_[:, i * w:(i + 1) * w], m_=m_: e.tensor_tensor(out=o, in0=o, in1=m_[:, 0, :], op=ALU.mult),
                              outs=[p_[:, i * w:(i + 1) * w]], ins=[p_[:, i * w:(i + 1) * w], m_])
                    cur = (bl, p_, w)
                if pend is not None:
                    pbl, pp_, pw = pend
                    for i, (kT, qT, m_, v, accs, first, last) in enumerate(pbl):
                        for su, acc in enumerate(accs):
                            lh = pp_[:, i * pw + su * 128:i * pw + (su + 1) * 128]
                            st_ = first and su == 0
                            P("pe", lambda e, o=acc, a=lh, b=v, st_=st_, last=last: e.matmul(
                                o, lhsT=a, rhs=b, start=st_, stop=last, skip_group_check=(len(accs) > 1)),
                              outs=[acc], ins=[lh, v])
                pend = cur

        def diff_finish(src4, ot):
            t4 = src4.rearrange("p (h w) d -> p h w d", w=2)
            P("dve", lambda e, a=t4[:, :, 1, :], b=t4[:, :, 0, :]: e.scalar_tensor_tensor(
                out=od, in0=a, scalar=nlam[:, 0:1], in1=b, op0=ALU.mult, op1=ALU.add), outs=[od], ins=[src4, nlam])
            P("pool", lambda e: e.tensor_tensor(out=od2, in0=od, in1=od, op=ALU.mult), outs=[od2], ins=[od])
            P("dve", lambda e: e.reduce_sum(out=ssd[:, :, 0], in_=od2, axis=AX.X), outs=[ssd], ins=[od2])
            P("act", lambda e: e.activation(out=ssd, in_=ssd, func=AF.Sqrt, bias=EPS, scale=1.0 / 64.0), outs=[ssd], ins=[ssd])
            P("dve", lambda e: e.reciprocal(out=ssd, in_=ssd), outs=[ssd], ins=[ssd])
            P("dve", lambda e: e.tensor_tensor(out=od2, in0=od, in1=ssd.broadcast_to([128, 4, 64]), op=ALU.mult), outs=[od2], ins=[od, ssd])
            o_ = ot[:, 512:768].rearrange("p (a b) -> p a b", a=4)
            P("pool", lambda e, o=o_: e.tensor_tensor(out=o, in0=od2, in1=gdiff, op=ALU.mult), outs=[o_], ins=[od2, gdiff])

        def diff_qblock(qb):
            q0 = qb * 512
            for g in range(8):
                h = g // 2
                ch, base = g // 3, (g % 3) * 32
                accb = psb(6 + g % 2)[:, 0:260].rearrange("p (a b) -> p a b", a=4)
                qm = qmb[g % 2]
                P("dve", lambda e, o=qm, a=diffQT[:, ch, q0:q0 + 512], m_=rmask[:, (g % 3):(g % 3) + 1]: e.tensor_scalar(
                    out=o, in0=a, scalar1=m_, scalar2=None, op0=ALU.mult), outs=[qm], ins=[diffQT[:, ch, q0:q0 + 512], rmask])
                blocks = []
                for kt in range(NT):
                    blocks.append((diffKT[:, ch, kt * 128:(kt + 1) * 128], qm, None,
                                   diffV[:, kt, h, :], [accb[:, su, :] for su in range(4)], kt == 0, kt == NT - 1))
                run_blocks(blocks, float(32 ** -0.5))
                dn = den[g % 2]
                P("dve", lambda e, o=dn, a=accb[:, :, 64:65]: e.reciprocal(out=o, in_=a), outs=[dn], ins=[accb[:, :, 64:65]])
                P("dve", lambda e, o=dstore[:, :, g, :], a=accb[:, :, 0:64], r=dn: e.tensor_tensor(
                    out=o, in0=a, in1=r.broadcast_to([128, 4, 64]), op=ALU.mult), outs=[dstore[:, :, g, :]], ins=[accb[:, :, 0:64], dn])

        for qi, qt in enumerate(out_tiles):
            which = 0 if qt < 16 else 1
            ot = otok[qi % 2]
            qs = slice(qt * 128, (qt + 1) * 128)
            if qt < 16 and qt % 4 == 0:
                diff_qblock(qt // 4)
            blocks = []
            for h in range(8):
                kg, kq, base = h // 4, h // 2, (h % 2) * 64
                if qt < 16:
                    kts = [(kt, (None if kt == qt else (msk[1] if kt < qt else msk[0])))
                           for kt in (qt - 1, qt, qt + 1) if 0 <= kt < 16] + [(16, None), (17, None)]
                else:
                    kts = [(16, None), (17, None)]
                acc = psb(4 + h // 4)[:, (h % 4) * 65:(h % 4) * 65 + 65]
                for j, (kt, m_) in enumerate(kts):
                    blocks.append((swaKT[base:base + 64, kg, kt * 128:(kt + 1) * 128], swaQT[base:base + 64, kq, qs], m_,
                                   swaV[:, kt, kg, :], [acc], j == 0, j == len(kts) - 1))
            run_blocks(blocks, 0.125)
            for b_ in range(2):
                av = psb(4 + b_)[:, 0:260].rearrange("p (a b) -> p a b", a=4)
                dn = den[b_]
                es_ = esink[:, 4 * b_:4 * b_ + 4].rearrange("p (a b) -> p a b", b=1)
                P("dve", lambda e, o=dn, a=av[:, :, 64:65], b=es_: e.tensor_tensor(out=o, in0=a, in1=b, op=ALU.add),
                  outs=[dn], ins=[av[:, :, 64:65], esink])
                P("dve", lambda e, o=dn: e.reciprocal(out=o, in_=o), outs=[dn], ins=[dn])
                o_ = ot[:, b_ * 256:(b_ + 1) * 256].rearrange("p (a b) -> p a b", a=4)
                P("dve", lambda e, o=o_, a=av[:, :, 0:64], r=dn: e.tensor_tensor(out=o, in0=a, in1=r.broadcast_to([128, 4, 64]), op=ALU.mult),
                  outs=[o_], ins=[av[:, :, 0:64], dn])
            if qt < 16:
                diff_finish(dstore[:, qt % 4, :, :], ot)
            else:
                blocks = []
                for g in range(8):
                    h = g // 2
                    ch, base = g // 3, (g % 3) * 32
                    kts = [16, 17]
                    acc = psb(6 + g // 4)[:, (g % 4) * 65:(g % 4) * 65 + 65]
                    for j, kt in enumerate(kts):
                        blocks.append((diffKT[base:base + 32, ch, kt * 128:(kt + 1) * 128], diffQT[base:base + 32, ch, qs], None,
                                       diffV[:, kt, h, :], [acc], j == 0, j == len(kts) - 1))
                run_blocks(blocks, float(32 ** -0.5))
                for b_ in range(2):
                    av = psb(6 + b_)[:, 0:260].rearrange("p (a b) -> p a b", a=4)
                    dn = den[b_]
                    P("dve", lambda e, o=dn, a=av[:, :, 64:65]: e.reciprocal(out=o, in_=a), outs=[dn], ins=[av[:, :, 64:65]])
                    tm_ = dctx[:, 4 * b_:4 * b_ + 4, :]
                    P("dve", lambda e, o=tm_, a=av[:, :, 0:64], r=dn: e.tensor_tensor(out=o, in0=a, in1=r.broadcast_to([128, 4, 64]), op=ALU.mult),
                      outs=[tm_], ins=[av[:, :, 0:64], dn])
                diff_finish(dctx, ot)
            _sel = os.environ.get("MK_SEL", "gsd")
            if "g" not in _sel:
                P("pool", lambda e, o=gla_out[:, qt, :]: e.memset(o, 0.0), outs=[gla_out[:, qt, :]])
            if "s" not in _sel:
                P("pool", lambda e, o=ot[:, 0:512]: e.memset(o, 0.0), outs=[ot[:, 0:512]])
            if "d" not in _sel:
                P("pool", lambda e, o=ot[:, 512:768]: e.memset(o, 0.0), outs=[ot[:, 512:768]])
            if qt == int(os.environ.get("MK_DUMP_QT", "3")):
                dump("otok", ot)
            tb_ = psb((state["bank"] % 2) * 2, 1024, BF16).rearrange("p (a b) -> p a b", a=8)
            for c in range(8):
                src = gla_out[:, qt, c * 128:(c + 1) * 128] if c < 2 else ot[:, (c - 2) * 128:(c - 1) * 128]
                P("pe", lambda e, o=tb_[:, c, :], i=src: e.transpose(out=o, in_=i, identity=ident_bf),
                  outs=[tb_[:, c, :]], ins=[src, ident_bf])
            sblk = qt % 4 if qt < 16 else qt - 16
            ob_ = oTblk[:, :, sblk * 128:(sblk + 1) * 128]
            P("act", lambda e, o=ob_, i=tb_: e.copy(out=o, in_=i), outs=[ob_], ins=[tb_])
            last_in_blk = (qt % 4 == 3) if qt < 16 else (qt == 17)
            if not last_in_blk:
                continue
            q0 = (qt // 4) * 512 if qt < 16 else S
            ntok = 512 if qt < 16 else C
            for c in range(8):
                xc = xcs[state["xc"] % 2][:, 0:ntok]
                state["xc"] += 1
                src = xsp_d[:, c, q0:q0 + ntok]
                P("sp", lambda e, o=xc, i=src: e.dma_start(out=o, in_=i), outs=[xc], ins=[src], dma=True)
                yb = psb(state["yb"] % 4, ntok)
                state["yb"] += 1
                for k in range(8):
                    P("pe", lambda e, o=yb, w_=wo_sb[:, k, c * 128:(c + 1) * 128], r=oTblk[:, k, 0:ntok], k=k:
                      e.matmul(o, lhsT=w_, rhs=r, start=(k == 0), stop=(k == 7)),
                      outs=[yb], ins=[wo_sb[:, k, c * 128:(c + 1) * 128], oTblk[:, k, 0:ntok]])
                gh_ = ghTs[l % 2][:, 1, c, which:which + 1]
                P("dve", lambda e, o=xc, y=yb, gh_=gh_: e.scalar_tensor_tensor(
                    out=o, in0=y, scalar=gh_, in1=o, op0=ALU.mult, op1=ALU.add),
                  outs=[xc], ins=[yb, xc, gh_])
                P("sp", lambda e, o=src, i=xc: e.dma_start(out=o, in_=i), outs=[src], ins=[xc], dma=True)
        for g0_ in range(0, T, 512):
            n_ = min(512, T - g0_)
            P("sp", lambda e, o=xT[:, :, g0_:g0_ + n_], i=xsp_d[:, :, g0_:g0_ + n_]: e.dma_start(out=o, in_=i),
              outs=[xT[:, :, g0_:g0_ + n_]], ins=[xsp_d[:, :, g0_:g0_ + n_]], dma=True)
        A.top = top0

    n_layers = L
    if dbg_stage == "ident":
        n_layers = 0
    for l in range(n_layers):
        LP["p"] = l % 2
        if l == 0:
            compute_mod(l)
        ffn(l, 0, w1i_d, w1o_d, [(0, 1152, [(0, 1152, 0)]), (1152, 1152, [(1152, 896, 0), (2048, 256, 1)])])
        if dbg_stage == "ffn1":
            break
        mixer(l)
        if dbg_stage == "mix":
            break
        last = (l == L - 1)
        if last:
            ffn(l, 2, w2i_d, w2o_d, [(0, 1024, [(0, 1024, 0)]), (1024, 1024, [(1024, 1024, 0)])])
        else:
            ffn(l, 2, w2i_d, w2o_d, [(0, 1152, [(0, 1152, 0)]), (1152, 1152, [(1152, 896, 0), (2048, 256, 1)])])

    tmp_top = A.top
    yT = A.alloc([128, 8, 512], F32, "yT")
    ob = [A.alloc([128, D], F32, "ob%d" % i) for i in range(2)]
    bi = 0
    for g in range(S // 512):
        rms_norm_to_hT(g * 512, 512, 0, 0, final=True, dst=yT)
        for tt in range(4):
            t = g * 4 + tt
            o_sb = ob[t % 2]
            for h in range(2):
                bank = bi % 6
                bi += 1
                pv = psb(bank).rearrange("p (a b) -> p a b", a=4)
                for c4 in range(4):
                    c = h * 4 + c4
                    src = yT[:, c, tt * 128:(tt + 1) * 128]
                    P("pe", lambda e, o=pv[:, c4, :], i=src: e.transpose(out=o, in_=i, identity=ident),
                      outs=[pv[:, c4, :]], ins=[src, ident])
                dst = o_sb[:, h * 512:(h + 1) * 512]
                pvf = psb(bank)
                if h == 0:
                    P("dve", lambda e, o=dst, i=pvf: e.tensor_copy(out=o, in_=i), outs=[dst], ins=[pvf])
                else:
                    P("act", lambda e, o=dst, i=pvf: e.copy(out=o, in_=i), outs=[dst], ins=[pvf])
            dd = out_d[t * 128:(t + 1) * 128, :]
            P("sp", lambda e, o=dd, i=o_sb: e.dma_start(out=o, in_=i), outs=[dd], ins=[o_sb], dma=True)
    A.top = tmp_top
    for e_ in ENGS:
        P(e_, None, ins=[out_d])

    with ExitStack() as st:
        sems = {e: st.enter_context(nc.semaphore("s_" + e)) for e in ENGS}
        dsems = {e: [st.enter_context(nc.semaphore("d_%s%d" % (e, i))) for i in range(NSLOT)] for e in ("sp", "pool", "act")}
        block = st.enter_context(nc.Block())
        sch.emit(nc, block, sems, dsems)
    es.close()
    return nc, sch


def _mix_cols():
    GQ, GK, GV, GF, GB, OG = 0, 128, 256, 512, 528, 544
    SQ, SK, SV = 800, 1312, 1440
    DQ, DK_, DV = 1568, 1824, 2080

    def rng(a, n):
        return list(range(a, a + n))
    perm64 = rng(16, 16) + rng(0, 16) + rng(48, 16) + rng(32, 16)
    perm32 = rng(8, 8) + rng(0, 8) + rng(24, 8) + rng(16, 8)

    def partner64(cl):
        return [cl[h * 64 + perm64[i]] for h in range(2) for i in range(64)]
    fm = [rng(GQ, 128), rng(GK, 128), rng(GF, 16) + rng(GB, 16) + [-1] * 96]
    swaq = [rng(SQ + c * 128, 128) for c in range(4)]
    fm += swaq
    fm += [partner64(c) for c in swaq]
    swak = [rng(SK, 64) * 2, rng(SK + 64, 64) * 2]
    fm += swak
    fm += [partner64(c) for c in swak]

    def dgroups(base):
        return [rng(base + h * 64 + w * 32, 32) for h in range(4) for w in range(2)]

    def dchunks(gr):
        out = []
        for gs in ((0, 1, 2), (3, 4, 5), (6, 7)):
            cc = []
            for g in gs:
                cc += gr[g]
            cc += [-1] * (128 - len(cc))
            out.append(cc)
        return out

    def partner32(gr):
        return [[g[perm32[i]] for i in range(32)] for g in gr]
    dq, dk = dgroups(DQ), dgroups(DK_)
    fm += dchunks(dq) + dchunks(partner32(dq)) + dchunks(dk) + dchunks(partner32(dk))
    assert len(fm) == NFM
    cols = []
    for c in fm:
        assert len(c) == 128
        cols += c
    cols += rng(GV, 256) + rng(OG, 256) + rng(GK, 128) + rng(SV, 128) + rng(DV, 256)
    return np.asarray(cols, np.int64)


def _rope_tables():
    rows = S // 64

    def tables(hd):
        half = hd // 2
        row = np.repeat(np.arange(rows, dtype=np.float32), 64)
        col = np.tile(np.arange(64, dtype=np.float32), rows)
        inv = (1.0 / (np.float32(10000.0) ** (np.arange(0, half, 2, dtype=np.float32) / np.float32(half)))).astype(np.float32)

        def ang(pos):
            a = (pos[:, None] * inv[None, :]).astype(np.float32)
            return np.concatenate([a, a], -1)
        an = np.concatenate([ang(row), ang(col)], -1)
        q = hd // 4
        sign = np.concatenate([-np.ones(q), np.ones(q), -np.ones(q), np.ones(q)]).astype(np.float32)
        return np.cos(an).astype(np.float32).T, (np.sin(an).astype(np.float32) * sign[None, :]).T
    c64, s64 = tables(64)
    c32, s32 = tables(32)
    out = np.zeros((128, 4, S), np.float32)
    out[:, 0] = np.tile(c64, (2, 1))
    out[:, 1] = np.tile(s64, (2, 1))
    out[:, 2] = np.tile(c32, (4, 1))
    out[:, 3] = np.tile(s32, (4, 1))
    return out


def _shared_inputs(inputs):
    f = np.float32
    d = {}
    wm = np.asarray(inputs["w_mod"], dtype=f).reshape(L, 8, 128, 72, 128)
    d["w_mod"] = np.ascontiguousarray(wm.transpose(0, 3, 2, 1, 4))
    bm = np.asarray(inputs["b_mod"], f)
    d["b_modT"] = np.ascontiguousarray(bm.reshape(L, 72, 128).transpose(0, 2, 1))
    gs = [inputs["g_ffn1"][0], inputs["g_mix"][0], inputs["g_ffn2"][0],
          inputs["g_ffn1"][1], inputs["g_mix"][1], inputs["g_ffn2"][1], inputs["g_final"]]
    g = np.stack([np.asarray(v, f) for v in gs], 0)
    d["gT"] = np.ascontiguousarray(g.reshape(7, 8, 128).transpose(2, 0, 1))
    d["w_out"] = np.ascontiguousarray(inputs["w_out"], dtype=f)
    for k in ("w_ffn1_in", "w_ffn2_in"):
        wi = np.asarray(inputs[k], f).reshape(L, 8, 128, 2, NJ, 128)
        d[k] = np.ascontiguousarray(wi.transpose(0, 4, 2, 1, 3, 5)).reshape(L, NJ, 128, 8, 256)
    for k in ("w_ffn1_out", "w_ffn2_out"):
        wo = np.asarray(inputs[k], f).reshape(L, NJ, 128, 8, 128)
        d[k] = np.ascontiguousarray(wo.transpose(0, 3, 2, 1, 4))
    cols = _mix_cols()
    w_in = np.asarray(inputs["w_in"], f)
    wz = np.concatenate([w_in, np.zeros((L, D, 1), f)], axis=-1)
    d["w_mix"] = np.ascontiguousarray(wz[:, :, cols])
    d["rope"] = _rope_tables()
    pm = np.zeros((128, 2, 128), f)
    perm64 = list(range(16, 32)) + list(range(0, 16)) + list(range(48, 64)) + list(range(32, 48))
    perm32 = list(range(8, 16)) + list(range(0, 8)) + list(range(24, 32)) + list(range(16, 24))
    for m in range(128):
        pm[(m // 64) * 64 + perm64[m % 64], 0, m] = 1.0
        pm[(m // 32) * 32 + perm32[m % 32], 1, m] = 1.0
    d["permT"] = pm
    rm = np.zeros((128, 4), f)
    for p_ in range(128):
        rm[p_, p_ // 32] = 1.0
    d["rmask"] = rm
    sm = np.concatenate([np.asarray(inputs["g_gla_norm"], f), np.asarray(inputs["g_diff_norm"], f),
                         np.asarray(inputs["swa_sink"], f), np.asarray(inputs["diff_lambda"], f).reshape(L, 128)], axis=-1)
    d["small_bc"] = np.ascontiguousarray(np.broadcast_to(sm[:, None, :], (L, 128, 648)))
    wg = np.asarray(inputs["w_gla_gate"], f)
    wup = np.zeros((L, 32, 2, 128), f)
    wup[:, 0:16, 0, :] = wg[:, 0]
    wup[:, 16:32, 1, :] = wg[:, 1]
    d["w_up"] = wup
    d["b_gate"] = np.ascontiguousarray(np.asarray(inputs["b_gla_gate"], f).reshape(L, 1, 256))
    return d


def _prep_inputs(inputs, core, shared):
    f = np.float32
    d = dict(shared)
    d["x"] = np.ascontiguousarray(inputs["x"][core], dtype=f)
    d["ctx"] = np.ascontiguousarray(inputs["ctx"][core], dtype=f)
    cc = np.stack([np.asarray(inputs["c"][core], f), np.asarray(inputs["c_ctx"], f)], axis=-1)
    d["cT"] = np.ascontiguousarray(cc.reshape(8, 128, 2).transpose(1, 0, 2))
    return d


_CACHE = {}


def kernel(**inputs):
    stage = os.environ.get("MK_STAGE", "full")
    if stage not in _CACHE:
        _CACHE[stage] = build(stage)[0]
    nc = _CACHE[stage]
    shared = _shared_inputs(inputs)
    in_maps = [_prep_inputs(inputs, core, shared) for core in range(8)]
    res = run_bass_kernel_spmd(nc, in_maps, core_ids=list(range(8)))
    out = np.stack([np.asarray(r["out"], dtype=np.float32) for r in res.results], axis=0)
    return out
```

```python
import math
import os
import numpy as np
import concourse.bass as bass
import concourse.mybir as mybir
from concourse.bass_utils import run_bass_kernel_spmd

F32 = mybir.dt.float32
BF16 = mybir.dt.bfloat16
ALU = mybir.AluOpType
AF = mybir.ActivationFunctionType
AX = mybir.AxisListType

D = 1024
S = 2048
C = 256
T = S + C
NT = T // 128
DFF = 2816
NJ = DFF // 128
L = 2
EPS = 1e-6

ENGS = ("pe", "act", "dve", "pool", "sp")
NSLOT = 8
NFM = 27
NMIX = NFM * 128 + 1024


def _esz(dt):
    return mybir.dt.size(dt)


class _Op:
    __slots__ = ("eng", "fn", "waits", "signal", "ev", "dma")


class Sched:
    def __init__(self):
        self.ops = {e: [] for e in ENGS}
        self.clock = {e: {} for e in ENGS}
        self.snap = {}
        self.opof = {}
        self.recs = {}
        self.slot_cnt = {e: [0] * NSLOT for e in ENGS}
        self.slot_rr = {e: 0 for e in ENGS}
        self.untracked = set()
        self.GR = 2048

    def boxes(self, ap):
        name = ap.tensor.name
        if name in self.untracked:
            return []
        es = _esz(ap.dtype)
        aps = list(ap.ap)
        off = ap.offset
        if str(ap.space) not in ("SB", "PSUM"):
            ext = sum((c - 1) * abs(s) for s, c in aps)
            return [(name, 0, 1, off * es, (off + ext + 1) * es)]
        pstep, pcnt = aps[0]
        p0 = off // pstep
        f0 = off % pstep
        free = aps[1:]
        out = []

        def rec(base, dims):
            if not dims:
                out.append((name, p0, p0 + pcnt, base * es, (base + 1) * es))
                return
            inner_ext = sum((c - 1) * abs(s) for s, c in dims[1:]) + 1
            s0, c0 = dims[0]
            if len(dims) > 1 and abs(s0) >= inner_ext and 1 < c0 <= 64 and s0 > 0:
                for i in range(c0):
                    rec(base + i * s0, dims[1:])
            else:
                ext = sum((c - 1) * abs(s) for s, c in dims) + 1
                out.append((name, p0, p0 + pcnt, base * es, (base + ext) * es))

        rec(f0, free)
        return out

    def _conf(self, b, kind, deps, eng):
        name, p0, p1, f0, f1 = b
        for g in range(f0 // self.GR, (f1 - 1) // self.GR + 1):
            for r in self.recs.get((name, g), ()):
                rb = r[0]
                if rb[1] < p1 and p0 < rb[2] and rb[3] < f1 and f0 < rb[4]:
                    rk = r[1]
                    if kind == "R" and rk == "R":
                        continue
                    if kind == "X" and rk == "X" and r[3] == eng:
                        continue
                    deps.add(r[2])

    def _reg(self, b, kind, ev, eng, dma):
        name, p0, p1, f0, f1 = b
        for g in range(f0 // self.GR, (f1 - 1) // self.GR + 1):
            lst = self.recs.setdefault((name, g), [])
            g0 = max(f0, g * self.GR)
            g1 = min(f1, (g + 1) * self.GR)
            keep = []
            for r in lst:
                rb = r[0]
                r0 = max(rb[3], g * self.GR)
                r1 = min(rb[4], (g + 1) * self.GR)
                contained = rb[1] >= p0 and rb[2] <= p1 and r0 >= g0 and r1 <= g1
                if contained:
                    if kind == "W":
                        continue
                    if kind == r[1] and r[3] == eng and not dma and not r[4]:
                        continue
                keep.append(r)
            keep.append((b, kind, ev, eng, dma))
            self.recs[(name, g)] = keep

    def add(self, eng, fn, outs=(), ins=(), dma=False):
        op = _Op()
        op.eng, op.fn, op.dma, op.signal = eng, fn, dma, dma
        idx = len(self.ops[eng])
        deps = set()
        acc = []
        for ap in ins:
            for b in self.boxes(ap):
                if b[0] == "ps":
                    acc.append(((b[0], 0, 128, (b[3] // 2048) * 2048, ((b[4] - 1) // 2048 + 1) * 2048), "X"))
                else:
                    acc.append((b, "R"))
        for ap in outs:
            for b in self.boxes(ap):
                if b[0] == "ps":
                    acc.append(((b[0], 0, 128, (b[3] // 2048) * 2048, ((b[4] - 1) // 2048 + 1) * 2048), "W"))
                else:
                    acc.append((b, "W"))
        for b, kind in acc:
            self._conf(b, kind, deps, eng)
        if dma:
            s = self.slot_rr[eng]
            self.slot_rr[eng] = (s + 1) % NSLOT
            k = self.slot_cnt[eng][s]
            if k > 0:
                deps.add(((eng, s), k))
            self.slot_cnt[eng][s] = k + 1
            ev = ((eng, s), k + 1)
        else:
            ev = (eng, idx + 1)
        op.ev = ev
        clk = self.clock[eng]
        waits = []
        for key, val in sorted(deps, key=lambda d: -d[1]):
            if eng == "pe" and key == "pe":
                continue
            if clk.get(key, 0) >= val:
                continue
            waits.append((key, val))
            self.opof[(key, val)].signal = True
            for k2, v2 in self.snap[(key, val)].items():
                if clk.get(k2, 0) < v2:
                    clk[k2] = v2
            clk[key] = max(clk.get(key, 0), val)
        op.waits = waits
        self.snap[ev] = dict(clk)
        self.opof[ev] = op
        for b, kind in acc:
            self._reg(b, kind, ev, eng, dma)
        self.ops[eng].append(op)
        return op

    def emit(self, nc, block, sems, dsems):
        rank = {}
        for e in ENGS:
            n = 0
            for i, op in enumerate(self.ops[e]):
                if op.signal and not op.dma:
                    n += 1
                    rank[(e, i + 1)] = n

        def run(eng_name, eng):
            for op in self.ops[eng_name]:
                for key, val in op.waits:
                    if isinstance(key, tuple):
                        eng.wait_ge(dsems[key[0]][key[1]], 16 * val)
                    else:
                        eng.wait_ge(sems[key], rank[(key, val)])
                if op.fn is None:
                    continue
                inst = op.fn(eng)
                if op.dma:
                    inst.then_inc(dsems[op.ev[0][0]][op.ev[0][1]], 16)
                elif op.signal:
                    inst.then_inc(sems[eng_name], 1)

        @block.tensor
        def _(e):
            run("pe", e)

        @block.scalar
        def _(e):
            run("act", e)

        @block.vector
        def _(e):
            run("dve", e)

        @block.gpsimd
        def _(e):
            run("pool", e)

        @block.sync
        def _(e):
            run("sp", e)


class Arena:
    def __init__(self, t, nbytes, base=0):
        self.t = t
        self.nbytes = base + nbytes
        self.base = base
        self.top = base

    def alloc(self, shape, dt, name=None):
        es = _esz(dt)
        n = 1
        for s in shape[1:]:
            n *= s
        nb = (n * es + 63) // 64 * 64
        off = self.top
        self.top += nb
        assert self.top <= self.nbytes, ("arena overflow", name, self.top)
        v = self.t[0:shape[0], off // 4:(off + nb) // 4]
        if dt != F32:
            v = v.bitcast(dt)
        v = v[:, 0:n]
        if len(shape) == 3:
            v = v.rearrange("p (a b) -> p a b", a=shape[1])
        elif len(shape) == 4:
            v = v.rearrange("p (a b c) -> p a b c", a=shape[1], b=shape[2])
        return v


def build(dbg_stage="full"):
    nc = bass.Bass("TRN2", target_bir_lowering=False)
    dram = {}

    def din(name, shape, dt=F32):
        dram[name] = nc.dram_tensor(name, list(shape), dt, kind="ExternalInput").ap()
        return dram[name]

    x_d = din("x", [S, D])
    ctx_d = din("ctx", [C, D])
    cT_d = din("cT", [128, 8, 2])
    wmod_d = din("w_mod", [L, 72, 128, 8, 128])
    bmodT_d = din("b_modT", [L, 128, 72])
    gT_d = din("gT", [128, 7, 8])
    w1i_d = din("w_ffn1_in", [L, NJ, 128, 8, 256])
    w1o_d = din("w_ffn1_out", [L, 8, 128, NJ, 128])
    w2i_d = din("w_ffn2_in", [L, NJ, 128, 8, 256])
    w2o_d = din("w_ffn2_out", [L, 8, 128, NJ, 128])
    wmix_d = din("w_mix", [L, D, NMIX])
    wout_d = din("w_out", [L, D, D])
    rope_d = din("rope", [128, 4, S])
    small_d = din("small_bc", [L, 128, 648])
    wup_d = din("w_up", [L, 32, 2, 128])
    bg_d = din("b_gate", [L, 1, 256])
    perm_d = din("permT", [128, 2, 128])
    rmask_d = din("rmask", [128, 4])
    xsp_d = nc.dram_tensor("xspill", [128, 8, T], F32, kind="Internal").ap()
    out_d = nc.dram_tensor("out", [S, D], F32, kind="ExternalOutput").ap()
    DUMP = os.environ.get("MK_DUMP", "")
    if DUMP:
        dbg_d = nc.dram_tensor("dbg", [128, 16384], F32, kind="ExternalOutput").ap()

    def dump(name, ap):
        if not DUMP or name != DUMP:
            return
        shp = list(ap.shape)
        n = 1
        for v_ in shp[1:]:
            n *= v_
        dv = dbg_d[0:shp[0], 0:n]
        if len(shp) == 3:
            dv = dv.rearrange("p (a b) -> p a b", a=shp[1])
        elif len(shp) == 4:
            dv = dv.rearrange("p (a b c) -> p a b c", a=shp[1], b=shp[2])
        sch.add("pool", lambda e, o=dv, i=ap: e.dma_start(out=o, in_=i), outs=[dv], ins=[ap], dma=True)

    sch = Sched()
    for n in ("x", "ctx", "cT", "w_mod", "b_modT", "gT", "w_ffn1_in", "w_ffn1_out", "w_ffn2_in", "w_ffn2_out",
              "w_mix", "w_out", "rope", "small_bc", "w_up", "b_gate", "permT", "rmask"):
        sch.untracked.add(n)

    ARENA_BYTES = 206 * 1024
    from contextlib import ExitStack
    es = ExitStack()
    arena_t = es.enter_context(nc.sbuf_tensor("arena", [128, ARENA_BYTES // 4], F32))
    ps_t = es.enter_context(nc.psum_tensor("ps", [128, 8, 512], F32))
    A = Arena(arena_t, ARENA_BYTES)

    def psb(b, n=512, dt=F32):
        v = ps_t[:, b, :]
        if dt != F32:
            v = v.bitcast(dt)
        return v[:, 0:n]

    xT = A.alloc([128, 8, T], F32, "xT")
    hT = A.alloc([128, 8, T], BF16, "hT")
    ident = A.alloc([128, 128], F32, "ident")
    ones_bf = A.alloc([128, 128], BF16, "ones_bf")
    ones_f = A.alloc([128, 128], F32, "ones_f")
    gT = A.alloc([128, 7, 8], F32, "gT")
    cT = A.alloc([128, 8, 2], F32, "cT")
    scT = A.alloc([128, 8, 2], BF16, "scT")
    modTs = [A.alloc([128, 72, 2], F32, "modT%d" % i) for i in range(2)]
    gsTs = [A.alloc([128, 3, 8, 2], F32, "gsT%d" % i) for i in range(2)]
    ghTs = [A.alloc([128, 3, 8, 2], F32, "ghT%d" % i) for i in range(2)]
    bmTs = [A.alloc([128, 72], F32, "bmT%d" % i) for i in range(2)]
    LP = {"p": 0}
    ident_bf = A.alloc([128, 128], BF16, "ident_bf")
    triA = [A.alloc([128, 128], F32, "triA%d" % i) for i in range(2)]
    triB = [A.alloc([128, 128], F32, "triB%d" % i) for i in range(2)]
    msk = [A.alloc([128, 1, 128], BF16, "msk%d" % i) for i in range(2)]
    bmask = A.alloc([128, 4, 64], F32, "bmask")
    bmask4 = A.alloc([128, 4, 128], BF16, "bmask4")
    PERSIST_TOP = A.top
    m16 = A.alloc([128, 128], F32, "m16")
    ones4 = A.alloc([128, 4, 128], F32, "ones4")
    XT_BYTES = 8 * T * 4
    HT_OFF = XT_BYTES
    HT_BYTES = 8 * T * 2

    P = sch.add

    P("pool", lambda e: e.memset(ident, 0.0), outs=[ident])
    P("pool", lambda e: e.memset(ones_f, 1.0), outs=[ones_f])
    P("pool", lambda e: e.affine_select(ident, ones_f, [[-1, 128]], ALU.is_equal, 0.0, base=0, channel_multiplier=1),
      outs=[ident], ins=[ones_f])
    P("pool", lambda e: e.memset(ones_bf, 1.0), outs=[ones_bf])
    P("dve", lambda e: e.tensor_copy(out=ident_bf, in_=ident), outs=[ident_bf], ins=[ident])
    P("pool", lambda e: e.memset(m16, -1.0 / 16.0), outs=[m16])
    P("pool", lambda e: e.memset(ones4, 1.0), outs=[ones4])
    P("pool", lambda e: e.affine_select(triA[0], m16, [[1, 128]], ALU.is_ge, 0.0, base=0, channel_multiplier=-1),
      outs=[triA[0]], ins=[m16])
    P("pool", lambda e: e.affine_select(triA[1], m16, [[-1, 128]], ALU.is_ge, 0.0, base=0, channel_multiplier=1),
      outs=[triA[1]], ins=[m16])
    P("pool", lambda e: e.affine_select(triB[0], m16, [[-1, 128]], ALU.is_gt, 0.0, base=0, channel_multiplier=1),
      outs=[triB[0]], ins=[m16])
    P("pool", lambda e: e.affine_select(triB[1], m16, [[1, 128]], ALU.is_gt, 0.0, base=0, channel_multiplier=-1),
      outs=[triB[1]], ins=[m16])
    P("pool", lambda e: e.affine_select(msk[0][:, 0, :], ones_f, [[1, 128]], ALU.is_ge, 0.0, base=0, channel_multiplier=-1),
      outs=[msk[0]], ins=[ones_f])
    P("pool", lambda e: e.affine_select(msk[1][:, 0, :], ones_f, [[-1, 128]], ALU.is_ge, 0.0, base=0, channel_multiplier=1),
      outs=[msk[1]], ins=[ones_f])
    tmpm = A.alloc([128, 4, 128], F32, "tmpm")
    P("pool", lambda e: e.affine_select(tmpm, ones4, [[-32, 4], [0, 128]], ALU.is_ge, 0.0, base=0, channel_multiplier=1),
      outs=[tmpm], ins=[ones4])
    P("pool", lambda e: e.affine_select(bmask4, tmpm, [[32, 4], [0, 128]], ALU.is_ge, 0.0, base=31, channel_multiplier=-1),
      outs=[bmask4], ins=[tmpm])
    P("pool", lambda e: e.affine_select(bmask, tmpm[:, :, 0:64], [[32, 4], [0, 64]], ALU.is_ge, 0.0, base=31, channel_multiplier=-1),
      outs=[bmask], ins=[tmpm])
    P("sp", lambda e: e.dma_start(out=gT, in_=gT_d), outs=[gT], dma=True)
    P("sp", lambda e: e.dma_start(out=cT, in_=cT_d), outs=[cT], dma=True)
    sig = A.alloc([128, 8, 2], F32, "sig")
    P("act", lambda e: e.activation(out=sig, in_=cT, func=AF.Silu), outs=[sig], ins=[cT])
    P("dve", lambda e: e.tensor_copy(out=scT, in_=sig), outs=[scT], ins=[sig])

    ldb = [A.alloc([128, D], F32, "ld%d" % i) for i in range(2)]
    for t in range(NT):
        src = x_d[t * 128:(t + 1) * 128, :] if t < 16 else ctx_d[(t - 16) * 128:(t - 15) * 128, :]
        lb = ldb[t % 2]
        P("sp", lambda e, lb=lb, src=src: e.dma_start(out=lb, in_=src), outs=[lb], dma=True)
        for h in range(2):
            bank = (t * 2 + h) % 8
            pv = psb(bank).rearrange("p (a b) -> p a b", a=4)
            for c4 in range(4):
                c = h * 4 + c4
                P("pe", lambda e, o=pv[:, c4, :], i=lb[:, c * 128:(c + 1) * 128]: e.transpose(out=o, in_=i, identity=ident),
                  outs=[pv[:, c4, :]], ins=[lb[:, c * 128:(c + 1) * 128], ident])
            dst = xT[:, h * 4:(h + 1) * 4, t * 128:(t + 1) * 128]
            if (t + h) % 2 == 0:
                P("dve", lambda e, o=dst, i=pv: e.tensor_copy(out=o, in_=i), outs=[dst], ins=[pv])
            else:
                P("act", lambda e, o=dst, i=pv: e.copy(out=o, in_=i), outs=[dst], ins=[pv])

    A.top = PERSIST_TOP
    def norm_bufs():
        return dict(sq=[A.alloc([128, 512], BF16) for _ in range(2)], rs=A.alloc([128, 512], F32),
                    t1=[A.alloc([128, 512], F32) for _ in range(2)])

    def norm_groups(tok0, ntok, gs_idx, which, nb, final=False, dst=None):
        sq, rs, t1b = nb["sq"], nb["rs"], nb["t1"]
        lp = LP["p"]

        def emit_group(g0, n, gi):
            bank = 7 - (gi % 2)
            acc = psb(bank, n)
            for c in range(8):
                s_ = sq[c % 2][:, 0:n]
                src = xT[:, c, g0:g0 + n]
                if c % 2 == 0:
                    P("act", lambda e, o=s_, i=src: e.activation(out=o, in_=i, func=AF.Square), outs=[s_], ins=[src])
                else:
                    P("pool", lambda e, o=s_, i=src: e.tensor_tensor(out=o, in0=i, in1=i, op=ALU.mult), outs=[s_], ins=[src])
                P("pe", lambda e, o=acc, r=s_, c=c: e.matmul(o, lhsT=ones_bf, rhs=r, start=(c == 0), stop=(c == 7)),
                  outs=[acc], ins=[ones_bf, s_])
            r = rs[:, 0:n]
            P("act", lambda e, o=r, i=acc: e.activation(out=o, in_=i, func=AF.Sqrt, bias=EPS, scale=1.0 / D), outs=[r], ins=[acc])
            P("dve", lambda e, o=r: e.reciprocal(out=o, in_=o), outs=[r], ins=[r])
            for c in range(8):
                src = xT[:, c, g0:g0 + n]
                if final:
                    o = dst[:, c, g0 - tok0:g0 - tok0 + n]
                    P("dve", lambda e, o=o, i=src, r=r, c=c: e.scalar_tensor_tensor(
                        out=o, in0=i, scalar=gT[:, 6, c:c + 1], in1=r, op0=ALU.mult, op1=ALU.mult), outs=[o], ins=[src, r, gT])
                else:
                    o = hT[:, c, g0:g0 + n]
                    t1 = t1b[c % 2][:, 0:n]
                    gs_ = gsTs[lp][:, gs_idx, c, which:which + 1]
                    P("dve", lambda e, o=t1, i=src, r=r, gs_=gs_: e.scalar_tensor_tensor(
                        out=o, in0=i, scalar=gs_, in1=r, op0=ALU.mult, op1=ALU.mult),
                      outs=[t1], ins=[src, r, gs_])
                    sh = modTs[lp][:, (3 * gs_idx) * 8 + c, which:which + 1]
                    P("act", lambda e, o=o, i=t1, sh=sh: e.activation(out=o, in_=i, func=AF.Identity, bias=sh, scale=1.0),
                      outs=[o], ins=[t1, sh])

        out = []
        g0 = tok0
        gi = 0
        while g0 < tok0 + ntok:
            n = min(512, tok0 + ntok - g0)
            out.append(lambda g0=g0, n=n, gi=gi: emit_group(g0, n, gi))
            g0 += n
            gi += 1
        return out

    def rms_norm_to_hT(tok0, ntok, gs_idx, which, final=False, dst=None):
        tmp_top = A.top
        nb = norm_bufs()
        for g in norm_groups(tok0, ntok, gs_idx, which, nb, final, dst):
            g()
        A.top = tmp_top

    def mod_emitters(l, wbufs, bank):
        mT, gS, gH, bm = modTs[l % 2], gsTs[l % 2], ghTs[l % 2], bmTs[l % 2]
        acc = psb(bank, 144).rearrange("p (a b) -> p a b", a=72)

        def chunk(fc):
            w = wbufs[fc % len(wbufs)]
            src = wmod_d[l, fc]
            P("pool", lambda e, o=w, i=src: e.dma_start(out=o, in_=i), outs=[w], dma=True)
            for k in range(8):
                P("pe", lambda e, o=acc[:, fc, :], w_=w[:, k, :], r=scT[:, k, :], k=k:
                  e.matmul(o, lhsT=w_, rhs=r, start=(k == 0), stop=(k == 7)),
                  outs=[acc[:, fc, :]], ins=[w[:, k, :], scT[:, k, :]])

        def fin():
            P("sp", lambda e: e.dma_start(out=bm, in_=bmodT_d[l]), outs=[bm], dma=True)
            for w_ in range(2):
                P("dve", lambda e, w_=w_: e.tensor_tensor(out=mT[:, :, w_], in0=acc[:, :, w_], in1=bm, op=ALU.add),
                  outs=[mT[:, :, w_]], ins=[acc[:, :, w_], bm])
            for n in range(3):
                for w_ in range(2):
                    sc = mT[:, (3 * n + 1) * 8:(3 * n + 2) * 8, w_]
                    o = gS[:, n, :, w_]
                    P("dve", lambda e, o=o, sc=sc, n=n: e.scalar_tensor_tensor(
                        out=o, in0=sc, scalar=1.0, in1=gT[:, 3 * l + n, :], op0=ALU.add, op1=ALU.mult),
                      outs=[o], ins=[sc, gT])
                    ga = mT[:, (3 * n + 2) * 8:(3 * n + 3) * 8, w_]
                    o2 = gH[:, n, :, w_]
                    P("dve", lambda e, o=o2, ga=ga, n=n: e.tensor_scalar(
                        out=o, in0=ga, scalar1=(1.0 if n == 1 else 0.5), scalar2=None, op0=ALU.mult),
                      outs=[o2], ins=[ga])

        return [(lambda fc=fc: chunk(fc)) for fc in range(72)] + [fin]

    def compute_mod(l):
        tmp_top = A.top
        wb = [A.alloc([128, 8, 128], BF16) for _ in range(4)]
        for em in mod_emitters(l, wb, 6):
            em()
        A.top = tmp_top

    def ffn(l, n_idx, w_in_d, w_out_d, groups):
        tmp_top = A.top
        aT = A.alloc([128, NJ, 1152], BF16, "aT")
        wi = [A.alloc([128, 8, 256], BF16, "wi%d" % i) for i in range(3)]
        wo = [A.alloc([128, NJ, 128], BF16, "wo%d" % i) for i in range(3)]
        sg = [A.alloc([128, 384], F32, "sg%d" % i) for i in range(2)]
        nbuf = norm_bufs()
        for pi, (p0, pn, segs) in enumerate(groups):
            if pi == 0:
                for (t0, tn, which) in segs:
                    for g in norm_groups(t0, tn, n_idx, which, nbuf):
                        g()
            nxt_norm = []
            if pi + 1 < len(groups):
                for (t0, tn, which) in groups[pi + 1][2]:
                    nxt_norm += norm_groups(t0, tn, n_idx, which, nbuf)
            regs = []
            r0 = p0
            while r0 < p0 + pn:
                rn = min(384, p0 + pn - r0)
                regs.append((r0, rn))
                r0 += rn
            assert len(regs) <= 3
            for j in range(NJ):
                w = wi[j % 3]
                P("pool", lambda e, o=w, i=w_in_d[l, j]: e.dma_start(out=o, in_=i), outs=[w], dma=True)
                for ri, (r0, rn) in enumerate(regs):
                    pg = psb(ri, rn)
                    pu = psb(3 + ri, rn)
                    for k in range(8):
                        P("pe", lambda e, o=pg, w_=w[:, k, 0:128], r=hT[:, k, r0:r0 + rn], k=k:
                          e.matmul(o, lhsT=w_, rhs=r, start=(k == 0), stop=(k == 7)),
                          outs=[pg], ins=[w[:, k, 0:128], hT[:, k, r0:r0 + rn]])
                    for k in range(8):
                        P("pe", lambda e, o=pu, w_=w[:, k, 128:256], r=hT[:, k, r0:r0 + rn], k=k:
                          e.matmul(o, lhsT=w_, rhs=r, start=(k == 0), stop=(k == 7)),
                          outs=[pu], ins=[w[:, k, 128:256], hT[:, k, r0:r0 + rn]])
                    s = sg[(j * 3 + ri) % 2][:, 0:rn]
                    P("act", lambda e, o=s, i=pg: e.activation(out=o, in_=i, func=AF.Silu), outs=[s], ins=[pg])
                    o = aT[:, j, r0 - p0:r0 - p0 + rn]
                    P("dve", lambda e, o=o, a=pu, b=s: e.tensor_tensor(out=o, in0=a, in1=b, op=ALU.mult), outs=[o], ins=[pu, s])
            bi = 0
            for c in range(8):
                w = wo[c % 3]
                src = w_out_d[l, c]
                P("pool", lambda e, o=w, i=src: e.dma_start(out=o, in_=i), outs=[w], dma=True)
                for (r0, rn) in regs:
                    py = psb(bi % 6, rn)
                    bi += 1
                    for j in range(NJ):
                        P("pe", lambda e, o=py, w_=w[:, j, :], r=aT[:, j, r0 - p0:r0 - p0 + rn], j=j:
                          e.matmul(o, lhsT=w_, rhs=r, start=(j == 0), stop=(j == NJ - 1)),
                          outs=[py], ins=[w[:, j, :], aT[:, j, r0 - p0:r0 - p0 + rn]])
                    for (t0, tn, which) in segs:
                        a0 = max(t0, r0)
                        a1 = min(t0 + tn, r0 + rn)
                        if a1 <= a0:
                            continue
                        xs = xT[:, c, a0:a1]
                        ys = py[:, a0 - r0:a1 - r0]
                        gh_ = ghTs[LP["p"]][:, n_idx, c, which:which + 1]
                        P("dve", lambda e, xs=xs, ys=ys, gh_=gh_: e.scalar_tensor_tensor(
                            out=xs, in0=ys, scalar=gh_, in1=xs, op0=ALU.mult, op1=ALU.add),
                          outs=[xs], ins=[ys, xs, gh_])
                if nxt_norm and c % 2 == 1:
                    nxt_norm.pop(0)()
            while nxt_norm:
                nxt_norm.pop(0)()
        A.top = tmp_top

    def mixer(l):
        ctx_out = (l < L - 1)
        lam_init = 0.8 - 0.6 * math.exp(-0.3 * l)
        top0 = A.top
        out_tiles = list(range(NT)) if ctx_out else list(range(16))
        tmp_top_n = A.top
        nb_ = norm_bufs()
        for (t0_, tn_, wh_) in ((0, S, 0), (S, C, 1)):
            g0_ = t0_
            for gfn in norm_groups(t0_, tn_, 1, wh_, nb_):
                gfn()
                n_ = min(512, t0_ + tn_ - g0_)
                P("sp", lambda e, o=xsp_d[:, :, g0_:g0_ + n_], i=xT[:, :, g0_:g0_ + n_]: e.dma_start(out=o, in_=i),
                  outs=[xsp_d[:, :, g0_:g0_ + n_]], ins=[xT[:, :, g0_:g0_ + n_]], dma=True)
                g0_ += n_
        A.top = tmp_top_n
        AXa = Arena(arena_t, XT_BYTES, 0)
        AH = Arena(arena_t, HT_BYTES, HT_OFF)
        glaQT = AXa.alloc([128, 1, T], BF16)
        glaKT = AXa.alloc([128, 1, T], BF16)
        gfbT = AXa.alloc([128, 1, T], BF16)
        swaQT = AXa.alloc([128, 4, T], BF16)
        swaKT = AXa.alloc([128, 2, T], BF16)
        diffQT = AXa.alloc([128, 3, T], BF16)
        diffKT = AXa.alloc([128, 3, T], BF16)
        glaTM = A.alloc([128, NT, 640], BF16)
        swaV = A.alloc([128, NT, 2, 65], BF16)
        diffV = A.alloc([128, NT, 4, 65], BF16)
        small = A.alloc([128, 648], F32)
        wup = A.alloc([32, 2, 128], BF16)
        bg = A.alloc([1, 256], BF16)
        esink = A.alloc([128, 8], F32)
        nlam = A.alloc([128, 1], F32)
        gdiff = A.alloc([128, 4, 64], F32)
        lamt = A.alloc([128, 2, 32], F32)
        lams = A.alloc([128, 2], F32)
        mix_top = A.top
        ropeT = A.alloc([128, 4, S], BF16)
        P("pool", lambda e: e.dma_start(out=ropeT, in_=rope_d), outs=[ropeT], dma=True)
        P("sp", lambda e: e.dma_start(out=small, in_=small_d[l]), outs=[small], dma=True)
        P("pool", lambda e: e.dma_start(out=wup, in_=wup_d[l]), outs=[wup], dma=True)
        P("pool", lambda e: e.dma_start(out=bg, in_=bg_d[l]), outs=[bg], dma=True)
        P("pool", lambda e: e.memset(swaV[:, :, :, 64:65], 1.0), outs=[swaV[:, :, :, 64:65]])
        P("pool", lambda e: e.memset(diffV[:, :, :, 64:65], 1.0), outs=[diffV[:, :, :, 64:65]])
        P("act", lambda e: e.activation(out=esink, in_=small[:, 512:520], func=AF.Exp), outs=[esink], ins=[small])
        dl = small[:, 520:648].rearrange("p (a b) -> p a b", a=4)
        P("dve", lambda e: e.tensor_tensor(out=lamt[:, 0, :], in0=dl[:, 0, :], in1=dl[:, 1, :], op=ALU.mult),
          outs=[lamt[:, 0, :]], ins=[small])
        P("dve", lambda e: e.tensor_tensor(out=lamt[:, 1, :], in0=dl[:, 2, :], in1=dl[:, 3, :], op=ALU.mult),
          outs=[lamt[:, 1, :]], ins=[small])
        P("dve", lambda e: e.reduce_sum(out=lams, in_=lamt, axis=AX.X), outs=[lams], ins=[lamt])
        P("act", lambda e: e.activation(out=lams, in_=lams, func=AF.Exp), outs=[lams], ins=[lams])
        P("dve", lambda e: e.scalar_tensor_tensor(out=nlam, in0=lams[:, 1:2], scalar=-lam_init, in1=lams[:, 0:1],
                                                  op0=ALU.add, op1=ALU.subtract), outs=[nlam], ins=[lams])
        P("dve", lambda e: e.tensor_scalar(out=gdiff, in0=small[:, 256:512].rearrange("p (a b) -> p a b", a=4),
                                           scalar1=1.0 - lam_init, scalar2=None, op0=ALU.mult), outs=[gdiff], ins=[small])

        wfm = [A.alloc([128, 2, 8, 128], BF16) for _ in range(2)]
        permT = A.alloc([128, 2, 128], BF16)
        xb = [A.alloc([128, 512], BF16) for _ in range(2)]
        P("pool", lambda e: e.dma_start(out=permT, in_=perm_d), outs=[permT], dma=True)
        wtm = [A.alloc([128, 8, 512], BF16) for _ in range(2)]
        rt = [A.alloc([128, 512], F32) for _ in range(2)]
        fm = [(0, None, glaQT[:, 0, :], None), (1, None, glaKT[:, 0, :], None), (2, None, gfbT[:, 0, :], None)]
        for c in range(4):
            fm.append((3 + c, 7 + c, swaQT[:, c, :], 0))
        for c in range(2):
            fm.append((11 + c, 13 + c, swaKT[:, c, :], 0))
        for c in range(3):
            fm.append((15 + c, 18 + c, diffQT[:, c, :], 2))
        for c in range(3):
            fm.append((21 + c, 24 + c, diffKT[:, c, :], 2))
        groups = [(0, 512), (512, 512), (1024, 512), (1536, 512), (2048, 256)]
        gi = 0
        def load_fm(fi):
            cm, cp, dest, tb = fm[fi]
            w = wfm[fi % 2]
            src = wmix_d[l, :, cm * 128:(cm + 1) * 128].rearrange("(kc p) f -> p kc f", p=128)
            P("pool", lambda e, o=w[:, 0, :, :], i=src: e.dma_start(out=o, in_=i), outs=[w[:, 0, :, :]], dma=True)

        load_fm(0)
        for g in range(2):
            src = wmix_d[l, :, NFM * 128 + g * 512:NFM * 128 + (g + 1) * 512].rearrange("(kc p) f -> p kc f", p=128)
            P("pool", lambda e, o=wtm[g], i=src: e.dma_start(out=o, in_=i), outs=[wtm[g]], dma=True)
        for fi, (cm, cp, dest, tb) in enumerate(fm):
            w = wfm[fi % 2]
            if fi + 1 < len(fm):
                load_fm(fi + 1)
            for (g0, n) in groups:
                b0 = (2 * gi) % 4
                gi += 1
                pm = psb(b0, n)
                pp = psb(b0 + 1, n)
                rope = (cp is not None) and g0 < S
                for k in range(8):
                    P("pe", lambda e, o=pm, w_=w[:, 0, k, :], r=hT[:, k, g0:g0 + n], k=k:
                      e.matmul(o, lhsT=w_, rhs=r, start=(k == 0), stop=(k == 7)),
                      outs=[pm], ins=[w[:, 0, k, :], hT[:, k, g0:g0 + n]])
                if rope:
                    xb_ = xb[gi % 2][:, 0:n]
                    P("act", lambda e, o=xb_, i=pm: e.copy(out=o, in_=i), outs=[xb_], ins=[pm])
                    pmx = permT[:, (0 if tb == 0 else 1), :]
                    P("pe", lambda e, o=pp, w_=pmx, r=xb_: e.matmul(o, lhsT=w_, rhs=r, start=True, stop=True),
                      outs=[pp], ins=[pmx, xb_])
                    t1 = rt[0][:, 0:n]
                    t2 = rt[1][:, 0:n]
                    cs = ropeT[:, tb, g0:g0 + n]
                    sn = ropeT[:, tb + 1, g0:g0 + n]
                    P("dve", lambda e, o=t1, a=pm, b=cs: e.tensor_tensor(out=o, in0=a, in1=b, op=ALU.mult), outs=[t1], ins=[pm, cs])
                    P("dve", lambda e, o=t2, a=pp, b=sn: e.tensor_tensor(out=o, in0=a, in1=b, op=ALU.mult), outs=[t2], ins=[pp, sn])
                    d_ = dest[:, g0:g0 + n]
                    P("dve", lambda e, o=d_, a=t1, b=t2: e.tensor_tensor(out=o, in0=a, in1=b, op=ALU.add), outs=[d_], ins=[t1, t2])
                else:
                    d_ = dest[:, g0:g0 + n]
                    P("act", lambda e, o=d_, i=pm: e.copy(out=o, in_=i), outs=[d_], ins=[pm])
        for t in range(NT):
            for g in range(2):
                pb = psb(4 + (t * 2 + g) % 4)
                for k in range(8):
                    P("pe", lambda e, o=pb, a=hT[:, k, t * 128:(t + 1) * 128], w_=wtm[g][:, k, :], k=k:
                      e.matmul(o, lhsT=a, rhs=w_, start=(k == 0), stop=(k == 7)),
                      outs=[pb], ins=[hT[:, k, t * 128:(t + 1) * 128], wtm[g][:, k, :]])
                if g == 0:
                    d_ = glaTM[:, t, 0:512]
                    P("act", lambda e, o=d_, i=pb: e.copy(out=o, in_=i), outs=[d_], ins=[pb])
                else:
                    d_ = glaTM[:, t, 512:640]
                    P("dve", lambda e, o=d_, i=pb[:, 0:128]: e.tensor_copy(out=o, in_=i), outs=[d_], ins=[pb[:, 0:128]])
                    d2 = swaV[:, t, :, 0:64]
                    s2 = pb[:, 128:256].rearrange("p (a b) -> p a b", a=2)
                    P("act", lambda e, o=d2, i=s2: e.copy(out=o, in_=i), outs=[d2], ins=[s2])
                    d3 = diffV[:, t, :, 0:64]
                    s3 = pb[:, 256:512].rearrange("p (a b) -> p a b", a=4)
                    P("dve", lambda e, o=d3, i=s3: e.tensor_copy(out=o, in_=i), outs=[d3], ins=[s3])
        A.top = mix_top
        for nm_, ap_ in (("glaQT", glaQT), ("glaKT", glaKT), ("gfbT", gfbT), ("swaQT", swaQT), ("swaKT", swaKT),
                         ("diffQT", diffQT), ("diffKT", diffKT), ("glaTM", glaTM), ("swaV", swaV), ("diffV", diffV), ("hT", hT)):
            dump(nm_, ap_)

        _cut = os.environ.get("MK_MIXCUT", "")
        if _cut == "inproj":
            P("sp", lambda e: e.dma_start(out=xT, in_=xsp_d), outs=[xT], ins=[xsp_d], dma=True)
            A.top = top0
            return
        esp = AH.alloc([128, 2, NT, 128], F32)
        ost = AH.alloc([128, NT, 256], F32)
        gla_out = A.alloc([128, NT, 256], BF16)
        gla_top = A.top
        bi = 0
        for d_ in range(2):
            for t0 in range(0, NT, 4):
                nb = min(4, NT - t0)
                pb = psb(bi % 4).rearrange("p (a b) -> p a b", a=4)
                bi += 1
                for s_ in range(nb):
                    t = t0 + s_
                    P("pe", lambda e, o=pb[:, s_, :], a=gfbT[0:32, 0, t * 128:(t + 1) * 128], w_=wup[:, d_, :]:
                      e.matmul(o, lhsT=a, rhs=w_, start=True, stop=False),
                      outs=[pb[:, s_, :]], ins=[gfbT[0:32, 0, t * 128:(t + 1) * 128], wup[:, d_, :]])
                    P("pe", lambda e, o=pb[:, s_, :], a=ones_bf[0:1, :], w_=bg[0:1, d_ * 128:(d_ + 1) * 128]:
                      e.matmul(o, lhsT=a, rhs=w_, start=False, stop=True),
                      outs=[pb[:, s_, :]], ins=[ones_bf[0:1, :], bg[0:1, d_ * 128:(d_ + 1) * 128]])
                o_ = esp[:, d_, t0:t0 + nb, :]
                P("act", lambda e, o=o_, i=pb[:, 0:nb, :]: e.activation(out=o, in_=i, func=AF.Exp, scale=-1.0),
                  outs=[o_], ins=[pb[:, 0:nb, :]])
        for d_ in range(2):
            P("act", lambda e, o=esp[:, d_, :, :]: e.activation(out=o, in_=o, func=AF.Ln, bias=1.0, scale=1.0),
              outs=[esp[:, d_, :, :]], ins=[esp[:, d_, :, :]])
        E1 = [A.alloc([128, 128], F32) for _ in range(2)]
        E2 = [A.alloc([128, 128], F32) for _ in range(2)]
        E3 = [A.alloc([128, 128], F32) for _ in range(2)]
        KtT = [A.alloc([128, 128], BF16) for _ in range(2)]
        Kh = [A.alloc([128, 128], BF16) for _ in range(2)]
        Qexp = [A.alloc([128, 4, 128], BF16) for _ in range(2)]
        QtT = [[A.alloc([128, 1, 128], BF16) for _ in range(2)] for _ in range(2)]
        attm = [[A.alloc([128, 4, 128], BF16) for _ in range(2)] for _ in range(2)]
        Um = [[A.alloc([128, 4, 64], F32) for _ in range(2)] for _ in range(2)]
        decs = [[A.alloc([128, 1], F32) for _ in range(2)] for _ in range(2)]
        Sf = [A.alloc([128, 4, 64], F32) for _ in range(2)]
        Sb = [[A.alloc([128, 4, 64], BF16) for _ in range(2)] for _ in range(2)]
        orders = [[16, 17] + list(range(16)), [17, 16] + list(range(15, -1, -1))]
        visited = set()
        for d_ in range(2):
            P("pool", lambda e, d_=d_: e.memset(Sf[d_], 0.0), outs=[Sf[d_]])
            P("pool", lambda e, d_=d_: e.memset(Sb[d_][0], 0.0), outs=[Sb[d_][0]])

        def gla_prep(st, d_):
            t = orders[d_][st]
            par = st % 2
            sp_t = esp[:, d_, t, :]
            bA = psb(3 * d_)
            pc, pr, pu = bA[:, 0:128], bA[:, 128:256], bA[:, 256:512]
            P("pe", lambda e, o=pc, a=sp_t, b=triA[d_]: e.matmul(o, lhsT=a, rhs=b, start=True, stop=True),
              outs=[pc], ins=[sp_t, triA[d_]])
            P("pe", lambda e, o=pr, a=triB[d_], b=sp_t: e.matmul(o, lhsT=a, rhs=b, start=True, stop=True),
              outs=[pr], ins=[triB[d_], sp_t])
            e1, e2, e3 = E1[d_], E2[d_], E3[d_]
            P("act", lambda e, o=e3, i=pr: e.activation(out=o, in_=i, func=AF.Exp), outs=[e3], ins=[pr])
            P("act", lambda e, o=e1, i=pc: e.activation(out=o, in_=i, func=AF.Exp), outs=[e1], ins=[pc])
            if t in out_tiles:
                P("act", lambda e, o=e2, i=pc: e.activation(out=o, in_=i, func=AF.Exp, scale=-1.0), outs=[e2], ins=[pc])
            kh = Kh[d_]
            P("dve", lambda e, o=kh, a=glaTM[:, t, 512:640], b=e3: e.tensor_tensor(out=o, in0=a, in1=b, op=ALU.mult),
              outs=[kh], ins=[glaTM[:, t, 512:640], e3])
            P("pe", lambda e, o=pu, a=kh, b=glaTM[:, t, 0:256]: e.matmul(o, lhsT=a, rhs=b, start=True, stop=True),
              outs=[pu], ins=[kh, glaTM[:, t, 0:256]])
            dsrc = e1[:, 127:128] if d_ == 0 else e1[:, 0:1]
            dc = decs[d_][par]
            P("dve", lambda e, o=dc, i=dsrc: e.tensor_copy(out=o, in_=i), outs=[dc], ins=[dsrc])
            um = Um[d_][par]
            P("dve", lambda e, o=um, a=pu.rearrange("p (a b) -> p a b", a=4): e.tensor_tensor(out=o, in0=a, in1=bmask, op=ALU.mult),
              outs=[um], ins=[pu, bmask])
            if t in out_tiles:
                qt_, kt_ = QtT[d_][par], KtT[d_]
                tq = glaQT[:, :, t * 128:(t + 1) * 128]
                P("dve", lambda e, o=qt_, a=tq, b=e1: e.scalar_tensor_tensor(
                    out=o[:, 0, :], in0=a[:, 0, :], scalar=float(32 ** -0.5), in1=b, op0=ALU.mult, op1=ALU.mult),
                  outs=[qt_], ins=[tq, e1])
                tk = glaKT[:, 0, t * 128:(t + 1) * 128]
                P("dve", lambda e, o=kt_, a=tk, b=e2: e.tensor_tensor(out=o, in0=a, in1=b, op=ALU.mult), outs=[kt_], ins=[tk, e2])
                qe = Qexp[d_]
                P("dve", lambda e, o=qe, a=qt_: e.tensor_tensor(out=o, in0=bmask4, in1=a.broadcast_to([128, 4, 128]), op=ALU.mult),
                  outs=[qe], ins=[bmask4, qt_])
                pa = psb(3 * d_ + 1)
                P("pe", lambda e, o=pa, a=kt_, b=qe: e.matmul(o, lhsT=a, rhs=b.rearrange("p a b -> p (a b)"), start=True, stop=True),
                  outs=[pa], ins=[kt_, qe])
                am = attm[d_][par]
                P("dve", lambda e, o=am, a=pa.rearrange("p (a b) -> p a b", a=4), m_=msk[d_]:
                  e.tensor_tensor(out=o, in0=a, in1=m_.broadcast_to([128, 4, 128]), op=ALU.mult), outs=[am], ins=[pa, msk[d_]])

        def gla_chain(st, d_):
            t = orders[d_][st]
            par = st % 2
            cur, nxt = st % 2, (st + 1) % 2
            sb_cur, sb_nxt = Sb[d_][cur], Sb[d_][nxt]
            if t in out_tiles:
                qt_, am = QtT[d_][par], attm[d_][par]
                po = psb(3 * d_ + 2, 256)
                for h in range(4):
                    oh = po[:, h * 64:(h + 1) * 64]
                    P("pe", lambda e, o=oh, a=qt_[:, 0, :], b=sb_cur[:, h, :]: e.matmul(o, lhsT=a, rhs=b, start=True, stop=False),
                      outs=[oh], ins=[qt_, sb_cur[:, h, :]])
                    P("pe", lambda e, o=oh, a=am[:, h, :], b=glaTM[:, t, h * 64:(h + 1) * 64]: e.matmul(o, lhsT=a, rhs=b, start=False, stop=True),
                      outs=[oh], ins=[am[:, h, :], glaTM[:, t, h * 64:(h + 1) * 64]])
                if t not in visited:
                    visited.add(t)
                    P("act", lambda e, o=ost[:, t, :], i=po: e.copy(out=o, in_=i), outs=[ost[:, t, :]], ins=[po])
                else:
                    P("dve", lambda e, o=ost[:, t, :], i=po: e.tensor_tensor(out=o, in0=i, in1=o, op=ALU.add),
                      outs=[ost[:, t, :]], ins=[po, ost[:, t, :]])
            P("dve", lambda e, o=Sf[d_], sc=decs[d_][par], b=Um[d_][par]: e.scalar_tensor_tensor(
                out=o, in0=o, scalar=sc, in1=b, op0=ALU.mult, op1=ALU.add), outs=[Sf[d_]], ins=[Sf[d_], decs[d_][par], Um[d_][par]])
            P("act", lambda e, o=sb_nxt, i=Sf[d_]: e.copy(out=o, in_=i), outs=[sb_nxt], ins=[Sf[d_]])

        mod_ems = []
        if l + 1 < L and dbg_stage == "full":
            wb2 = [A.alloc([128, 8, 128], BF16) for _ in range(2)]
            mod_ems = mod_emitters(l + 1, wb2, 6)
        for st in range(NT + 1):
            for d_ in range(2):
                if st < NT:
                    gla_prep(st, d_)
            for d_ in range(2):
                if st >= 1:
                    gla_chain(st - 1, d_)
            for _ in range(4):
                if len(mod_ems) > 1:
                    mod_ems.pop(0)()
        while mod_ems:
            mod_ems.pop(0)()
        A.top = gla_top
        no = len(out_tiles)
        ssg = A.alloc([128, NT, 4, 1], F32)
        sqg = [A.alloc([128, 4, 64], F32) for _ in range(2)]
        sgl = [A.alloc([128, 256], F32) for _ in range(2)]
        for t in out_tiles:
            o4 = ost[:, t, :].rearrange("p (a b) -> p a b", a=4)
            q_ = sqg[t % 2]
            P("pool", lambda e, o=q_, a=o4: e.tensor_tensor(out=o, in0=a, in1=a, op=ALU.mult), outs=[q_], ins=[o4])
            P("dve", lambda e, o=ssg[:, t, :, 0], i=q_: e.reduce_sum(out=o, in_=i, axis=AX.X), outs=[ssg[:, t, :, :]], ins=[q_])
        sv = ssg[:, 0:no, :, :]
        P("act", lambda e, o=sv: e.activation(out=o, in_=o, func=AF.Sqrt, bias=EPS, scale=1.0 / 64.0), outs=[sv], ins=[sv])
        P("dve", lambda e, o=sv: e.reciprocal(out=o, in_=o), outs=[sv], ins=[sv])
        ggla = small[:, 0:256].rearrange("p (a b) -> p a b", a=4)
        for t in out_tiles:
            o4 = ost[:, t, :].rearrange("p (a b) -> p a b", a=4)
            q_ = sqg[t % 2]
            sg_ = sgl[t % 2]
            P("act", lambda e, o=sg_, i=glaTM[:, t, 256:512]: e.activation(out=o, in_=i, func=AF.Silu), outs=[sg_], ins=[glaTM[:, t, 256:512]])
            P("dve", lambda e, o=q_, a=o4, r=ssg[:, t, :, :]: e.tensor_tensor(out=o, in0=a, in1=r.broadcast_to([128, 4, 64]), op=ALU.mult),
              outs=[q_], ins=[o4, ssg[:, t, :, :]])
            P("pool", lambda e, o=q_: e.tensor_tensor(out=o, in0=o, in1=ggla, op=ALU.mult), outs=[q_], ins=[q_, small])
            go = gla_out[:, t, :]
            P("dve", lambda e, o=go, a=q_, b=sg_: e.tensor_tensor(out=o, in0=a.rearrange("p a b -> p (a b)"), in1=b, op=ALU.mult),
              outs=[go], ins=[q_, sg_])
        A.top = gla_top

        dump("gla_out", gla_out)
        dump("ost", ost)
        if _cut == "gla":
            P("sp", lambda e: e.dma_start(out=xT, in_=xsp_d), outs=[xT], ins=[xsp_d, gla_out], dma=True)
            A.top = top0
            return
        AH2 = Arena(arena_t, HT_BYTES, HT_OFF)
        wo_sb = AH2.alloc([128, 8, D], BF16)
        P("pool", lambda e: e.dma_start(out=wo_sb, in_=wout_d[l].rearrange("(kc p) f -> p kc f", p=128)), outs=[wo_sb], dma=True)
        pT = [AH2.alloc([128, 8, 128], BF16) for _ in range(3)]
        otok = [A.alloc([128, 768], BF16) for _ in range(2)]
        dstore = AH2.alloc([128, 4, 8, 64], F32)
        rmask = A.alloc([128, 4], F32)
        qmb = [A.alloc([128, 512], BF16) for _ in range(2)]
        P("sp", lambda e: e.dma_start(out=rmask, in_=rmask_d), outs=[rmask], dma=True)
        oTblk = A.alloc([128, 8, 512], BF16)
        xcs = [A.alloc([128, 512], F32) for _ in range(2)]
        den = [AH2.alloc([128, 4, 1], F32) for _ in range(2)]
        dctx = AH2.alloc([128, 8, 64], F32)
        od = AH2.alloc([128, 4, 64], F32)
        od2 = AH2.alloc([128, 4, 64], F32)
        ssd = AH2.alloc([128, 4, 1], F32)
        state = {"bank": 0, "pt": 0, "xc": 0, "yb": 0}
        dump("wo_sb", wo_sb[:, :, 0:2048] if False else wo_sb)

        def run_blocks(blocks, scale):
            batches = []
            for blk in blocks:
                rg = blk[0].base_partition()
                w = blk[1].shape[-1]
                if (batches and batches[-1][0][0].base_partition() == rg and batches[-1][0][1].shape[-1] == w
                        and (len(batches[-1]) + 1) * w <= 1024):
                    batches[-1].append(blk)
                else:
                    batches.append([blk])
            pend = None
            for bl in batches + [None]:
                cur = None
                if bl is not None:
                    w = bl[0][1].shape[-1]
                    ncol = len(bl) * w
                    b2 = (state["bank"] % 2) * 2
                    state["bank"] += 1
                    bank = ps_t[:, b2:b2 + 2, :].rearrange("p a c -> p (a c)")
                    p_ = pT[state["pt"] % 3].rearrange("p a c -> p (a c)")
                    state["pt"] += 1
                    for i, (kT, qT, m_, v, accs, first, last) in enumerate(bl):
                        P("pe", lambda e, o=bank[:, i * w:(i + 1) * w], a=kT, b=qT: e.matmul(o, lhsT=a, rhs=b, start=True, stop=True),
                          outs=[bank[:, i * w:(i + 1) * w]], ins=[kT, qT])
                    P("act", lambda e, o=p_[:, 0:ncol], i=bank[:, 0:ncol]: e.activation(out=o, in_=i, func=AF.Exp, scale=scale),
                      outs=[p_[:, 0:ncol]], ins=[bank[:, 0:ncol]])
                    for i, (kT, qT, m_, v, accs, first, last) in enumerate(bl):
                        if m_ is not None:
                            P("dve", lambda e, o=p_[:, i * w:(i + 1) * w], m_=m_: e.tensor_tensor(out=o, in0=o, in1=m_[:, 0, :], op=ALU.mult),
                              outs=[p_[:, i * w:(i + 1) * w]], ins=[p_[:, i * w:(i + 1) * w], m_])
                    cur = (bl, p_, w)
                if pend is not None:
                    pbl, pp_, pw = pend
                    for i, (kT, qT, m_, v, accs, first, last) in enumerate(pbl):
                        for su, acc in enumerate(accs):
                            lh = pp_[:, i * pw + su * 128:i * pw + (su + 1) * 128]
                            st_ = first and su == 0
                            P("pe", lambda e, o=acc, a=lh, b=v, st_=st_, last=last: e.matmul(
                                o, lhsT=a, rhs=b, start=st_, stop=last, skip_group_check=(len(accs) > 1)),
                              outs=[acc], ins=[lh, v])
                pend = cur

        def diff_finish(src4, ot):
            t4 = src4.rearrange("p (h w) d -> p h w d", w=2)
            P("dve", lambda e, a=t4[:, :, 1, :], b=t4[:, :, 0, :]: e.scalar_tensor_tensor(
                out=od, in0=a, scalar=nlam[:, 0:1], in1=b, op0=ALU.mult, op1=ALU.add), outs=[od], ins=[src4, nlam])
            P("pool", lambda e: e.tensor_tensor(out=od2, in0=od, in1=od, op=ALU.mult), outs=[od2], ins=[od])
            P("dve", lambda e: e.reduce_sum(out=ssd[:, :, 0], in_=od2, axis=AX.X), outs=[ssd], ins=[od2])
            P("act", lambda e: e.activation(out=ssd, in_=ssd, func=AF.Sqrt, bias=EPS, scale=1.0 / 64.0), outs=[ssd], ins=[ssd])
            P("dve", lambda e: e.reciprocal(out=ssd, in_=ssd), outs=[ssd], ins=[ssd])
            P("dve", lambda e: e.tensor_tensor(out=od2, in0=od, in1=ssd.broadcast_to([128, 4, 64]), op=ALU.mult), outs=[od2], ins=[od, ssd])
            o_ = ot[:, 512:768].rearrange("p (a b) -> p a b", a=4)
            P("pool", lambda e, o=o_: e.tensor_tensor(out=o, in0=od2, in1=gdiff, op=ALU.mult), outs=[o_], ins=[od2, gdiff])

        def diff_qblock(qb):
            q0 = qb * 512
            for g in range(8):
                h = g // 2
                ch, base = g // 3, (g % 3) * 32
                accb = psb(6 + g % 2)[:, 0:260].rearrange("p (a b) -> p a b", a=4)
                qm = qmb[g % 2]
                P("dve", lambda e, o=qm, a=diffQT[:, ch, q0:q0 + 512], m_=rmask[:, (g % 3):(g % 3) + 1]: e.tensor_scalar(
                    out=o, in0=a, scalar1=m_, scalar2=None, op0=ALU.mult), outs=[qm], ins=[diffQT[:, ch, q0:q0 + 512], rmask])
                blocks = []
                for kt in range(NT):
                    blocks.append((diffKT[:, ch, kt * 128:(kt + 1) * 128], qm, None,
                                   diffV[:, kt, h, :], [accb[:, su, :] for su in range(4)], kt == 0, kt == NT - 1))
                run_blocks(blocks, float(32 ** -0.5))
                dn = den[g % 2]
                P("dve", lambda e, o=dn, a=accb[:, :, 64:65]: e.reciprocal(out=o, in_=a), outs=[dn], ins=[accb[:, :, 64:65]])
                P("dve", lambda e, o=dstore[:, :, g, :], a=accb[:, :, 0:64], r=dn: e.tensor_tensor(
                    out=o, in0=a, in1=r.broadcast_to([128, 4, 64]), op=ALU.mult), outs=[dstore[:, :, g, :]], ins=[accb[:, :, 0:64], dn])

        for qi, qt in enumerate(out_tiles):
            which = 0 if qt < 16 else 1
            ot = otok[qi % 2]
            qs = slice(qt * 128, (qt + 1) * 128)
            if qt < 16 and qt % 4 == 0:
                diff_qblock(qt // 4)
            blocks = []
            for h in range(8):
                kg, kq, base = h // 4, h // 2, (h % 2) * 64
                if qt < 16:
                    kts = [(kt, (None if kt == qt else (msk[1] if kt < qt else msk[0])))
                           for kt in (qt - 1, qt, qt + 1) if 0 <= kt < 16] + [(16, None), (17, None)]
                else:
                    kts = [(16, None), (17, None)]
                acc = psb(4 + h // 4)[:, (h % 4) * 65:(h % 4) * 65 + 65]
                for j, (kt, m_) in enumerate(kts):
                    blocks.append((swaKT[base:base + 64, kg, kt * 128:(kt + 1) * 128], swaQT[base:base + 64, kq, qs], m_,
                                   swaV[:, kt, kg, :], [acc], j == 0, j == len(kts) - 1))
            run_blocks(blocks, 0.125)
            for b_ in range(2):
                av = psb(4 + b_)[:, 0:260].rearrange("p (a b) -> p a b", a=4)
                dn = den[b_]
                es_ = esink[:, 4 * b_:4 * b_ + 4].rearrange("p (a b) -> p a b", b=1)
                P("dve", lambda e, o=dn, a=av[:, :, 64:65], b=es_: e.tensor_tensor(out=o, in0=a, in1=b, op=ALU.add),
                  outs=[dn], ins=[av[:, :, 64:65], esink])
                P("dve", lambda e, o=dn: e.reciprocal(out=o, in_=o), outs=[dn], ins=[dn])
                o_ = ot[:, b_ * 256:(b_ + 1) * 256].rearrange("p (a b) -> p a b", a=4)
                P("dve", lambda e, o=o_, a=av[:, :, 0:64], r=dn: e.tensor_tensor(out=o, in0=a, in1=r.broadcast_to([128, 4, 64]), op=ALU.mult),
                  outs=[o_], ins=[av[:, :, 0:64], dn])
            if qt < 16:
                diff_finish(dstore[:, qt % 4, :, :], ot)
            else:
                blocks = []
                for g in range(8):
                    h = g // 2
                    ch, base = g // 3, (g % 3) * 32
                    kts = [16, 17]
                    acc = psb(6 + g // 4)[:, (g % 4) * 65:(g % 4) * 65 + 65]
                    for j, kt in enumerate(kts):
                        blocks.append((diffKT[base:base + 32, ch, kt * 128:(kt + 1) * 128], diffQT[base:base + 32, ch, qs], None,
                                       diffV[:, kt, h, :], [acc], j == 0, j == len(kts) - 1))
                run_blocks(blocks, float(32 ** -0.5))
                for b_ in range(2):
                    av = psb(6 + b_)[:, 0:260].rearrange("p (a b) -> p a b", a=4)
                    dn = den[b_]
                    P("dve", lambda e, o=dn, a=av[:, :, 64:65]: e.reciprocal(out=o, in_=a), outs=[dn], ins=[av[:, :, 64:65]])
                    tm_ = dctx[:, 4 * b_:4 * b_ + 4, :]
                    P("dve", lambda e, o=tm_, a=av[:, :, 0:64], r=dn: e.tensor_tensor(out=o, in0=a, in1=r.broadcast_to([128, 4, 64]), op=ALU.mult),
                      outs=[tm_], ins=[av[:, :, 0:64], dn])
                diff_finish(dctx, ot)
            _sel = os.environ.get("MK_SEL", "gsd")
            if "g" not in _sel:
                P("pool", lambda e, o=gla_out[:, qt, :]: e.memset(o, 0.0), outs=[gla_out[:, qt, :]])
            if "s" not in _sel:
                P("pool", lambda e, o=ot[:, 0:512]: e.memset(o, 0.0), outs=[ot[:, 0:512]])
            if "d" not in _sel:
                P("pool", lambda e, o=ot[:, 512:768]: e.memset(o, 0.0), outs=[ot[:, 512:768]])
            if qt == int(os.environ.get("MK_DUMP_QT", "3")):
                dump("otok", ot)
            tb_ = psb((state["bank"] % 2) * 2, 1024, BF16).rearrange("p (a b) -> p a b", a=8)
            for c in range(8):
                src = gla_out[:, qt, c * 128:(c + 1) * 128] if c < 2 else ot[:, (c - 2) * 128:(c - 1) * 128]
                P("pe", lambda e, o=tb_[:, c, :], i=src: e.transpose(out=o, in_=i, identity=ident_bf),
                  outs=[tb_[:, c, :]], ins=[src, ident_bf])
            sblk = qt % 4 if qt < 16 else qt - 16
            ob_ = oTblk[:, :, sblk * 128:(sblk + 1) * 128]
            P("act", lambda e, o=ob_, i=tb_: e.copy(out=o, in_=i), outs=[ob_], ins=[tb_])
            last_in_blk = (qt % 4 == 3) if qt < 16 else (qt == 17)
            if not last_in_blk:
                continue
            q0 = (qt // 4) * 512 if qt < 16 else S
            ntok = 512 if qt < 16 else C
            for c in range(8):
                xc = xcs[state["xc"] % 2][:, 0:ntok]
                state["xc"] += 1
                src = xsp_d[:, c, q0:q0 + ntok]
                P("sp", lambda e, o=xc, i=src: e.dma_start(out=o, in_=i), outs=[xc], ins=[src], dma=True)
                yb = psb(state["yb"] % 4, ntok)
                state["yb"] += 1
                for k in range(8):
                    P("pe", lambda e, o=yb, w_=wo_sb[:, k, c * 128:(c + 1) * 128], r=oTblk[:, k, 0:ntok], k=k:
                      e.matmul(o, lhsT=w_, rhs=r, start=(k == 0), stop=(k == 7)),
                      outs=[yb], ins=[wo_sb[:, k, c * 128:(c + 1) * 128], oTblk[:, k, 0:ntok]])
                gh_ = ghTs[l % 2][:, 1, c, which:which + 1]
                P("dve", lambda e, o=xc, y=yb, gh_=gh_: e.scalar_tensor_tensor(
                    out=o, in0=y, scalar=gh_, in1=o, op0=ALU.mult, op1=ALU.add),
                  outs=[xc], ins=[yb, xc, gh_])
                P("sp", lambda e, o=src, i=xc: e.dma_start(out=o, in_=i), outs=[src], ins=[xc], dma=True)
        for g0_ in range(0, T, 512):
            n_ = min(512, T - g0_)
            P("sp", lambda e, o=xT[:, :, g0_:g0_ + n_], i=xsp_d[:, :, g0_:g0_ + n_]: e.dma_start(out=o, in_=i),
              outs=[xT[:, :, g0_:g0_ + n_]], ins=[xsp_d[:, :, g0_:g0_ + n_]], dma=True)
        A.top = top0

    n_layers = L
    if dbg_stage == "ident":
        n_layers = 0
    for l in range(n_layers):
        LP["p"] = l % 2
        if l == 0:
            compute_mod(l)
        ffn(l, 0, w1i_d, w1o_d, [(0, 1152, [(0, 1152, 0)]), (1152, 1152, [(1152, 896, 0), (2048, 256, 1)])])
        if dbg_stage == "ffn1":
            break
        mixer(l)
        if dbg_stage == "mix":
            break
        last = (l == L - 1)
        if last:
            ffn(l, 2, w2i_d, w2o_d, [(0, 1024, [(0, 1024, 0)]), (1024, 1024, [(1024, 1024, 0)])])
        else:
            ffn(l, 2, w2i_d, w2o_d, [(0, 1152, [(0, 1152, 0)]), (1152, 1152, [(1152, 896, 0), (2048, 256, 1)])])

    tmp_top = A.top
    yT = A.alloc([128, 8, 512], F32, "yT")
    ob = [A.alloc([128, D], F32, "ob%d" % i) for i in range(2)]
    bi = 0
    for g in range(S // 512):
        rms_norm_to_hT(g * 512, 512, 0, 0, final=True, dst=yT)
        for tt in range(4):
            t = g * 4 + tt
            o_sb = ob[t % 2]
            for h in range(2):
                bank = bi % 6
                bi += 1
                pv = psb(bank).rearrange("p (a b) -> p a b", a=4)
                for c4 in range(4):
                    c = h * 4 + c4
                    src = yT[:, c, tt * 128:(tt + 1) * 128]
                    P("pe", lambda e, o=pv[:, c4, :], i=src: e.transpose(out=o, in_=i, identity=ident),
                      outs=[pv[:, c4, :]], ins=[src, ident])
                dst = o_sb[:, h * 512:(h + 1) * 512]
                pvf = psb(bank)
                if h == 0:
                    P("dve", lambda e, o=dst, i=pvf: e.tensor_copy(out=o, in_=i), outs=[dst], ins=[pvf])
                else:
                    P("act", lambda e, o=dst, i=pvf: e.copy(out=o, in_=i), outs=[dst], ins=[pvf])
            dd = out_d[t * 128:(t + 1) * 128, :]
            P("sp", lambda e, o=dd, i=o_sb: e.dma_start(out=o, in_=i), outs=[dd], ins=[o_sb], dma=True)
    A.top = tmp_top
    for e_ in ENGS:
        P(e_, None, ins=[out_d])

    with ExitStack() as st:
        sems = {e: st.enter_context(nc.semaphore("s_" + e)) for e in ENGS}
        dsems = {e: [st.enter_context(nc.semaphore("d_%s%d" % (e, i))) for i in range(NSLOT)] for e in ("sp", "pool", "act")}
        block = st.enter_context(nc.Block())
        sch.emit(nc, block, sems, dsems)
    es.close()
    return nc, sch


def _mix_cols():
    GQ, GK, GV, GF, GB, OG = 0, 128, 256, 512, 528, 544
    SQ, SK, SV = 800, 1312, 1440
    DQ, DK_, DV = 1568, 1824, 2080

    def rng(a, n):
        return list(range(a, a + n))
    perm64 = rng(16, 16) + rng(0, 16) + rng(48, 16) + rng(32, 16)
    perm32 = rng(8, 8) + rng(0, 8) + rng(24, 8) + rng(16, 8)

    def partner64(cl):
        return [cl[h * 64 + perm64[i]] for h in range(2) for i in range(64)]
    fm = [rng(GQ, 128), rng(GK, 128), rng(GF, 16) + rng(GB, 16) + [-1] * 96]
    swaq = [rng(SQ + c * 128, 128) for c in range(4)]
    fm += swaq
    fm += [partner64(c) for c in swaq]
    swak = [rng(SK, 64) * 2, rng(SK + 64, 64) * 2]
    fm += swak
    fm += [partner64(c) for c in swak]

    def dgroups(base):
        return [rng(base + h * 64 + w * 32, 32) for h in range(4) for w in range(2)]

    def dchunks(gr):
        out = []
        for gs in ((0, 1, 2), (3, 4, 5), (6, 7)):
            cc = []
            for g in gs:
                cc += gr[g]
            cc += [-1] * (128 - len(cc))
            out.append(cc)
        return out

    def partner32(gr):
        return [[g[perm32[i]] for i in range(32)] for g in gr]
    dq, dk = dgroups(DQ), dgroups(DK_)
    fm += dchunks(dq) + dchunks(partner32(dq)) + dchunks(dk) + dchunks(partner32(dk))
    assert len(fm) == NFM
    cols = []
    for c in fm:
        assert len(c) == 128
        cols += c
    cols += rng(GV, 256) + rng(OG, 256) + rng(GK, 128) + rng(SV, 128) + rng(DV, 256)
    return np.asarray(cols, np.int64)


def _rope_tables():
    rows = S // 64

    def tables(hd):
        half = hd // 2
        row = np.repeat(np.arange(rows, dtype=np.float32), 64)
        col = np.tile(np.arange(64, dtype=np.float32), rows)
        inv = (1.0 / (np.float32(10000.0) ** (np.arange(0, half, 2, dtype=np.float32) / np.float32(half)))).astype(np.float32)

        def ang(pos):
            a = (pos[:, None] * inv[None, :]).astype(np.float32)
            return np.concatenate([a, a], -1)
        an = np.concatenate([ang(row), ang(col)], -1)
        q = hd // 4
        sign = np.concatenate([-np.ones(q), np.ones(q), -np.ones(q), np.ones(q)]).astype(np.float32)
        return np.cos(an).astype(np.float32).T, (np.sin(an).astype(np.float32) * sign[None, :]).T
    c64, s64 = tables(64)
    c32, s32 = tables(32)
    out = np.zeros((128, 4, S), np.float32)
    out[:, 0] = np.tile(c64, (2, 1))
    out[:, 1] = np.tile(s64, (2, 1))
    out[:, 2] = np.tile(c32, (4, 1))
    out[:, 3] = np.tile(s32, (4, 1))
    return out


def _shared_inputs(inputs):
    f = np.float32
    d = {}
    wm = np.asarray(inputs["w_mod"], dtype=f).reshape(L, 8, 128, 72, 128)
    d["w_mod"] = np.ascontiguousarray(wm.transpose(0, 3, 2, 1, 4))
    bm = np.asarray(inputs["b_mod"], f)
    d["b_modT"] = np.ascontiguousarray(bm.reshape(L, 72, 128).transpose(0, 2, 1))
    gs = [inputs["g_ffn1"][0], inputs["g_mix"][0], inputs["g_ffn2"][0],
          inputs["g_ffn1"][1], inputs["g_mix"][1], inputs["g_ffn2"][1], inputs["g_final"]]
    g = np.stack([np.asarray(v, f) for v in gs], 0)
    d["gT"] = np.ascontiguousarray(g.reshape(7, 8, 128).transpose(2, 0, 1))
    d["w_out"] = np.ascontiguousarray(inputs["w_out"], dtype=f)
    for k in ("w_ffn1_in", "w_ffn2_in"):
        wi = np.asarray(inputs[k], f).reshape(L, 8, 128, 2, NJ, 128)
        d[k] = np.ascontiguousarray(wi.transpose(0, 4, 2, 1, 3, 5)).reshape(L, NJ, 128, 8, 256)
    for k in ("w_ffn1_out", "w_ffn2_out"):
        wo = np.asarray(inputs[k], f).reshape(L, NJ, 128, 8, 128)
        d[k] = np.ascontiguousarray(wo.transpose(0, 3, 2, 1, 4))
    cols = _mix_cols()
    w_in = np.asarray(inputs["w_in"], f)
    wz = np.concatenate([w_in, np.zeros((L, D, 1), f)], axis=-1)
    d["w_mix"] = np.ascontiguousarray(wz[:, :, cols])
    d["rope"] = _rope_tables()
    pm = np.zeros((128, 2, 128), f)
    perm64 = list(range(16, 32)) + list(range(0, 16)) + list(range(48, 64)) + list(range(32, 48))
    perm32 = list(range(8, 16)) + list(range(0, 8)) + list(range(24, 32)) + list(range(16, 24))
    for m in range(128):
        pm[(m // 64) * 64 + perm64[m % 64], 0, m] = 1.0
        pm[(m // 32) * 32 + perm32[m % 32], 1, m] = 1.0
    d["permT"] = pm
    rm = np.zeros((128, 4), f)
    for p_ in range(128):
        rm[p_, p_ // 32] = 1.0
    d["rmask"] = rm
    sm = np.concatenate([np.asarray(inputs["g_gla_norm"], f), np.asarray(inputs["g_diff_norm"], f),
                         np.asarray(inputs["swa_sink"], f), np.asarray(inputs["diff_lambda"], f).reshape(L, 128)], axis=-1)
    d["small_bc"] = np.ascontiguousarray(np.broadcast_to(sm[:, None, :], (L, 128, 648)))
    wg = np.asarray(inputs["w_gla_gate"], f)
    wup = np.zeros((L, 32, 2, 128), f)
    wup[:, 0:16, 0, :] = wg[:, 0]
    wup[:, 16:32, 1, :] = wg[:, 1]
    d["w_up"] = wup
    d["b_gate"] = np.ascontiguousarray(np.asarray(inputs["b_gla_gate"], f).reshape(L, 1, 256))
    return d


def _prep_inputs(inputs, core, shared):
    f = np.float32
    d = dict(shared)
    d["x"] = np.ascontiguousarray(inputs["x"][core], dtype=f)
    d["ctx"] = np.ascontiguousarray(inputs["ctx"][core], dtype=f)
    cc = np.stack([np.asarray(inputs["c"][core], f), np.asarray(inputs["c_ctx"], f)], axis=-1)
    d["cT"] = np.ascontiguousarray(cc.reshape(8, 128, 2).transpose(1, 0, 2))
    return d


_CACHE = {}


def kernel(**inputs):
    stage = os.environ.get("MK_STAGE", "full")
    if stage not in _CACHE:
        _CACHE[stage] = build(stage)[0]
    nc = _CACHE[stage]
    shared = _shared_inputs(inputs)
    in_maps = [_prep_inputs(inputs, core, shared) for core in range(8)]
    res = run_bass_kernel_spmd(nc, in_maps, core_ids=list(range(8)))
    out = np.stack([np.asarray(r["out"], dtype=np.float32) for r in res.results], axis=0)
    return out
```
